# Optimizing a Trainium2 kernel written in Bass

```python
import jax, jax.numpy as jnp
from jax import lax
import numpy as np

D_MODEL = 1024
BATCH = 16
SEQ = 4096
DEPTH = 4

MEM_LEN = 256
HEAD_DIM = 64
CONV_WIDTH = D_MODEL // 2
CONV_K = 31
DSA_HEADS = 8
DSA_WIDTH = DSA_HEADS * HEAD_DIM
IDX_HEADS = 4
IDX_DIM = 64
TOPK_MAX = 256
Q_BLOCK = 128
SG_CHUNK = 128
SG_GROUPS = 8
SG_WIDTH = D_MODEL
MEM_HEADS = 4
MEM_HEAD_DIM = 128
MEM_WIDTH = MEM_HEADS * MEM_HEAD_DIM

ROPE_THETA = 10000.0
LN_EPS = 1e-5
DN_ALPHA = (2 * DEPTH) ** 0.25
DN_BETA = (8 * DEPTH) ** -0.25

EVEN_SPLITS = (CONV_WIDTH, CONV_WIDTH, CONV_WIDTH,
               DSA_WIDTH, HEAD_DIM, HEAD_DIM,
               IDX_HEADS * IDX_DIM, IDX_DIM, IDX_HEADS, DSA_WIDTH,
               MEM_WIDTH, MEM_WIDTH)
ODD_SPLITS = (SG_WIDTH, SG_WIDTH, SG_WIDTH,
              MEM_WIDTH, MEM_WIDTH)
EVEN_IN = sum(EVEN_SPLITS)
ODD_IN = sum(ODD_SPLITS)
EVEN_OUT = CONV_WIDTH + DSA_WIDTH + MEM_WIDTH
ODD_OUT = SG_WIDTH + MEM_WIDTH
N_EVEN = (DEPTH + 1) // 2
N_ODD = DEPTH // 2

kernel_name = "hybrid_conv_dsa_sgmlp_memory_deepnorm"


def _split(h, sizes):
    return jnp.split(h, tuple(int(c) for c in np.cumsum(sizes)[:-1]), axis=-1)


def layer_norm(x, g, b):
    xf = x.astype(jnp.float32)
    mu = jnp.mean(xf, -1, keepdims=True)
    var = jnp.mean(jnp.square(xf - mu), -1, keepdims=True)
    y = (xf - mu) * lax.rsqrt(var + LN_EPS) * g.astype(jnp.float32) + b.astype(jnp.float32)
    return y.astype(x.dtype)


def rope_tables(positions, dim):
    inv = ROPE_THETA ** (-jnp.arange(0, dim, 2, dtype=jnp.float32) / dim)
    ang = positions.astype(jnp.float32)[..., None] * inv
    return jnp.cos(ang), jnp.sin(ang)


def apply_rope(t, cos, sin):
    c = cos[:, :, None, :].astype(t.dtype)
    s = sin[:, :, None, :].astype(t.dtype)
    t1, t2 = jnp.split(t, 2, axis=-1)
    return jnp.concatenate([t1 * c - t2 * s, t2 * c + t1 * s], axis=-1)


def conformer_conv(a_val, a_glu, conv_w, conv_b, ln_g, ln_b, pw2_w, pw2_b):
    h = a_val * jax.nn.sigmoid(a_glu)
    h = lax.conv_general_dilated(h, conv_w[:, None, :].astype(h.dtype), window_strides=(1,),
                                 padding=[(CONV_K - 1, 0)],
                                 dimension_numbers=("NWC", "WIO", "NWC"),
                                 feature_group_count=CONV_WIDTH) + conv_b
    h = jax.nn.silu(layer_norm(h, ln_g, ln_b))
    return h @ pw2_w + pw2_b


def dsa_attention(q, kv, q_idx, k_idx, w_idx):
    B, S = q.shape[0], q.shape[1]
    topk = min(TOPK_MAX, S // 4)
    key_pos = jnp.arange(S)

    def block(i):
        start = i * Q_BLOCK
        qb = lax.dynamic_slice_in_dim(q, start, Q_BLOCK, 1)
        qib = lax.dynamic_slice_in_dim(q_idx, start, Q_BLOCK, 1)
        wb = lax.dynamic_slice_in_dim(w_idx, start, Q_BLOCK, 1)
        qpos = start + jnp.arange(Q_BLOCK)
        logits = jnp.einsum("bthd,bsd->bths", qib, k_idx, preferred_element_type=jnp.float32)
        score = jnp.einsum("bths,bth->bts", jax.nn.relu(logits), wb.astype(jnp.float32))
        causal = key_pos[None, :] <= qpos[:, None]
        score = jnp.where(causal[None], score, -jnp.inf)
        _, idx = lax.top_k(score, topk)
        kvg = jax.vmap(lambda kk, ii: kk[ii])(kv, idx)
        kg, vg = jnp.split(kvg, 2, axis=-1)
        valid = idx <= qpos[None, :, None]
        att = jnp.einsum("bthd,btkd->bthk", qb, kg, preferred_element_type=jnp.float32) * (HEAD_DIM ** -0.5)
        att = jnp.where(valid[:, :, None, :], att, -jnp.inf)
        p = jax.nn.softmax(att, axis=-1)
        return jnp.einsum("bthk,btkd->bthd", p.astype(vg.dtype), vg)

    out = lax.map(block, jnp.arange(S // Q_BLOCK))
    return jnp.transpose(out, (1, 0, 2, 3, 4)).reshape(B, S, DSA_WIDTH)


def memory_attention(mq, mem, wk, wv):
    B, S = mq.shape[0], mq.shape[1]
    M = mem.shape[1]
    q = mq.reshape(B, S, MEM_HEADS, MEM_HEAD_DIM)
    k = (mem @ wk).reshape(B, M, MEM_HEADS, MEM_HEAD_DIM)
    v = (mem @ wv).reshape(B, M, MEM_HEADS, MEM_HEAD_DIM)
    s = jnp.einsum("bthd,bmhd->bhtm", q, k, preferred_element_type=jnp.float32) * (MEM_HEAD_DIM ** -0.5)
    p = jax.nn.softmax(s, axis=-1)
    o = jnp.einsum("bhtm,bmhd->bthd", p.astype(v.dtype), v)
    return o.reshape(B, S, MEM_WIDTH)


def spatial_gating(u, v, ln_g, ln_b, ws, bs):
    B, S = u.shape[0], u.shape[1]
    u = jax.nn.gelu(u)
    v = layer_norm(jax.nn.gelu(v), ln_g, ln_b)
    vc = v.reshape(B, S // SG_CHUNK, SG_CHUNK, SG_GROUPS, SG_WIDTH // SG_GROUPS)
    tril = jnp.tril(jnp.ones((SG_CHUNK, SG_CHUNK), dtype=bool))
    w = jnp.where(tril[None], ws, jnp.zeros_like(ws))
    mixed = jnp.einsum("gts,bcsgd->bctgd", w, vc) + jnp.transpose(bs)[None, None, :, :, None]
    return u * mixed.reshape(B, S, SG_WIDTH)


def even_layer(x, mem, cos, sin, w_in, conv_w, conv_b, cln_g, cln_b, pw2_w, pw2_b, w_out, mem_wk, mem_wv):
    B, S = x.shape[0], x.shape[1]
    (a_val, a_glu, a_gate, q, k, v, qi, ki, wi, b_gate, mq, m_gate) = _split(x @ w_in, EVEN_SPLITS)
    ya = conformer_conv(a_val, a_glu, conv_w, conv_b, cln_g, cln_b, pw2_w, pw2_b) * jax.nn.silu(a_gate)
    q = apply_rope(q.reshape(B, S, DSA_HEADS, HEAD_DIM), cos, sin)
    k = apply_rope(k[:, :, None, :], cos, sin)[:, :, 0]
    qi = apply_rope(qi.reshape(B, S, IDX_HEADS, IDX_DIM), cos, sin)
    ki = apply_rope(ki[:, :, None, :], cos, sin)[:, :, 0]
    wi = wi * ((IDX_HEADS ** -0.5) * (IDX_DIM ** -0.5))
    yb = dsa_attention(q, jnp.concatenate([k, v], axis=-1), qi, ki, wi) * jax.nn.silu(b_gate)
    ym = memory_attention(mq, mem, mem_wk, mem_wv) * jax.nn.silu(m_gate)
    return jnp.concatenate([ya, yb, ym], axis=-1) @ w_out


def odd_layer(x, mem, vln_g, vln_b, ws, bs, w_in, w_out, mem_wk, mem_wv):
    (u, v, c_gate, mq, m_gate) = _split(x @ w_in, ODD_SPLITS)
    yc = spatial_gating(u, v, vln_g, vln_b, ws, bs) * jax.nn.silu(c_gate)
    ym = memory_attention(mq, mem, mem_wk, mem_wv) * jax.nn.silu(m_gate)
    return jnp.concatenate([yc, ym], axis=-1) @ w_out


def setup_inputs(seed: int = 0) -> dict:
    key = jax.random.key(seed)
    ks = jax.random.split(key, 24)
    nrm = lambda k, shape, scale: jax.random.normal(k, shape, jnp.float32) * scale
    D = D_MODEL
    x = nrm(ks[0], (BATCH, SEQ, D), 1.0)
    mem = nrm(ks[1], (BATCH, MEM_LEN, D), 1.0)
    offset = jax.random.randint(ks[2], (BATCH, 1), 0, 1024, dtype=jnp.int32)
    positions = (offset + jnp.arange(SEQ, dtype=jnp.int32)[None, :]).astype(jnp.int32)
    return {
        "x": x,
        "mem": mem,
        "positions": positions,
        "e_w_in": nrm(ks[3], (N_EVEN, D, EVEN_IN), D ** -0.5),
        "e_conv_w": nrm(ks[4], (N_EVEN, CONV_K, CONV_WIDTH), CONV_K ** -0.5),
        "e_conv_b": nrm(ks[5], (N_EVEN, CONV_WIDTH), 0.02),
        "e_cln_g": 1.0 + nrm(ks[6], (N_EVEN, CONV_WIDTH), 0.02),
        "e_cln_b": nrm(ks[7], (N_EVEN, CONV_WIDTH), 0.02),
        "e_pw2_w": nrm(ks[8], (N_EVEN, CONV_WIDTH, CONV_WIDTH), CONV_WIDTH ** -0.5),
        "e_pw2_b": nrm(ks[9], (N_EVEN, CONV_WIDTH), 0.02),
        "e_w_out": nrm(ks[10], (N_EVEN, EVEN_OUT, D), DN_BETA * EVEN_OUT ** -0.5),
        "o_w_in": nrm(ks[11], (N_ODD, D, ODD_IN), D ** -0.5),
        "o_vln_g": 1.0 + nrm(ks[12], (N_ODD, SG_WIDTH), 0.02),
        "o_vln_b": nrm(ks[13], (N_ODD, SG_WIDTH), 0.02),
        "o_ws": nrm(ks[14], (N_ODD, SG_GROUPS, SG_CHUNK, SG_CHUNK), SG_CHUNK ** -0.5),
        "o_bs": 1.0 + nrm(ks[15], (N_ODD, SG_GROUPS, SG_CHUNK), 0.02),
        "o_w_out": nrm(ks[16], (N_ODD, ODD_OUT, D), DN_BETA * ODD_OUT ** -0.5),
        "mem_wk": nrm(ks[17], (DEPTH, D, MEM_WIDTH), D ** -0.5),
        "mem_wv": nrm(ks[18], (DEPTH, D, MEM_WIDTH), D ** -0.5),
        "ln_g": 1.0 + nrm(ks[19], (DEPTH, D), 0.02),
        "ln_b": nrm(ks[20], (DEPTH, D), 0.02),
    }


def reference(x, mem, positions, e_w_in, e_conv_w, e_conv_b, e_cln_g, e_cln_b, e_pw2_w, e_pw2_b, e_w_out,
              o_w_in, o_vln_g, o_vln_b, o_ws, o_bs, o_w_out, mem_wk, mem_wv, ln_g, ln_b):
    cos, sin = rope_tables(positions, HEAD_DIM)
    for layer in range(DEPTH):
        j = layer // 2
        if layer % 2 == 0:
            y = even_layer(x, mem, cos, sin, e_w_in[j], e_conv_w[j], e_conv_b[j], e_cln_g[j], e_cln_b[j],
                           e_pw2_w[j], e_pw2_b[j], e_w_out[j], mem_wk[layer], mem_wv[layer])
        else:
            y = odd_layer(x, mem, o_vln_g[j], o_vln_b[j], o_ws[j], o_bs[j], o_w_in[j], o_w_out[j],
                          mem_wk[layer], mem_wv[layer])
        x = layer_norm(DN_ALPHA * x + y, ln_g[layer], ln_b[layer])
    return x
```

```python
import os
import numpy as np
import concourse.bass as bass
import concourse.mybir as mybir
from concourse.bass_utils import run_bass_kernel_spmd

F32 = mybir.dt.float32
BF16 = mybir.dt.bfloat16
I32 = mybir.dt.int32
AF = mybir.ActivationFunctionType
ALU = mybir.AluOpType
AX = mybir.AxisListType

NCORES = 8
D = 1024
SEQ = 4096
MEM = 256
DEPTH = 4
NT = SEQ // 128
LN_EPS = 1e-5
DN_ALPHA = (2 * DEPTH) ** 0.25
NIT = 22
NEG = -30000.0
SEM_EPOCH = 30000
DMA_K = 16
NCH_E = 38
NCH_O = 24


class Buf:
    __slots__ = ("name", "lw", "rd", "excl")

    def __init__(self, name="", excl=False):
        self.name = name
        self.lw = {}
        self.rd = {}
        self.excl = excl


class Stream:
    def __init__(self, S, name, eng, inc, k):
        self.S, self.name, self.eng, self.inc, self.k = S, name, eng, inc, k
        self.i = 0
        self.vals = [0] * k
        self.epochs = [0] * k
        self.sems = {}
        for j in range(k):
            self.sems[(name, j, 0)] = S.nc.alloc_semaphore(name=f"s_{name}_{j}_0")

    def next(self):
        j = self.i % self.k
        self.i += 1
        if self.vals[j] + self.inc > SEM_EPOCH:
            self.epochs[j] += 1
            self.vals[j] = 0
            self.sems[(self.name, j, self.epochs[j])] = self.S.nc.alloc_semaphore(
                name=f"s_{self.name}_{j}_{self.epochs[j]}")
        self.vals[j] += self.inc
        return ((self.name, j, self.epochs[j]), self.vals[j])

    def current(self):
        return [((self.name, j, self.epochs[j]), self.vals[j]) for j in range(self.k) if self.vals[j] > 0]


class Sched:
    def __init__(self, nc):
        self.nc = nc
        self.lists = {e: [] for e in ("pe", "dve", "act", "pool", "sp")}
        self.seen = {e: {} for e in self.lists}
        self.pending = {e: {} for e in self.lists}
        self.streams = {}
        for nm, eng, inc, k in (("pe", "pe", 1, 1), ("dve", "dve", 1, 1), ("act", "act", 1, 1),
                                ("pool", "pool", 1, 1), ("spd", "sp", 16, DMA_K), ("actd", "act", 16, DMA_K)):
            self.streams[nm] = Stream(self, nm, eng, inc, k)
        self.n = 0

    def semh(self, key):
        return self.streams[key[0]].sems[key]

    def op(self, stream, fn, reads=(), writes=()):
        st = self.streams[stream]
        eng = st.eng
        if any(b.excl for b in reads):
            writes = list(writes) + [b for b in reads if b.excl and b not in writes]
            reads = [b for b in reads if not b.excl]
        deps = dict(self.pending[eng])
        self.pending[eng] = {}
        if st.k > 1:
            jj = st.i % st.k
            if st.vals[jj] > 0:
                deps[(st.name, jj, st.epochs[jj])] = st.vals[jj]

        def add(d):
            for k, v in d.items():
                if deps.get(k, 0) < v:
                    deps[k] = v
        for b in reads:
            add(b.lw)
        for b in writes:
            add(b.lw)
            add(b.rd)
        seen = self.seen[eng]
        waits = []
        for k, v in deps.items():
            if stream == "pe" and k[0] == "pe":
                continue
            if seen.get(k, 0) >= v:
                continue
            seen[k] = v
            waits.append((k, v))
        key, val = st.next()
        self.lists[eng].append((waits, fn, key, st.inc))
        for b in reads:
            if b.rd.get(key, 0) < val:
                b.rd[key] = val
        for b in writes:
            b.lw = {key: val}
            b.rd = {}
        self.n += 1

    def barrier(self):
        cur = {}
        for s in self.streams.values():
            for k, v in s.current():
                cur[k] = v
        for e in self.lists:
            p = self.pending[e]
            for k, v in cur.items():
                if p.get(k, 0) < v:
                    p[k] = v

    def emit(self):
        nc = self.nc
        engmap = {"pe": "tensor", "dve": "vector", "act": "scalar", "pool": "gpsimd", "sp": "sync"}
        fin = []
        for s in self.streams.values():
            fin += s.current()
        with nc.Block() as block:
            for e, lst in self.lists.items():
                def body(engine, lst=lst, e=e):
                    for waits, fn, key, inc in lst:
                        for (wk, wv) in waits:
                            engine.wait_ge(self.semh(wk), wv)
                        fn(engine).then_inc(self.semh(key), inc)
                    if e == "sp":
                        for (wk, wv) in fin:
                            engine.wait_ge(self.semh(wk), wv)
                getattr(block, engmap[e])(body)


class T:
    __slots__ = ("a", "b")

    def __init__(self, a, b):
        self.a, self.b = a, b


class Prog:
    def __init__(self, nseq, layers, debug=False):
        self.nseq, self.layers, self.debug = nseq, layers, debug
        self.nc = nc = bass.Bass("TRN2", target_bir_lowering=False)
        self.S = Sched(nc)
        self.dbg_names = []
        ntok = nseq * SEQ
        ein = lambda n, sh, dt=F32: nc.dram_tensor(n, list(sh), dt, kind="ExternalInput").ap()
        self.x_in = ein("x", [ntok, D])
        self.mem_in = ein("mem", [nseq * MEM, D])
        self.pos_in = ein("pos", [nseq, SEQ], I32)
        self.c_ident = ein("c_ident", [128, 128])
        self.c_invf = ein("c_invf", [128, 2])
        self.c_negtri = ein("c_negtri", [128, 128])
        self.c_tril = ein("c_tril", [128, 128])
        self.c_pow2 = ein("c_pow2", [128, NIT + 2])
        nE, nO = (DEPTH + 1) // 2, DEPTH // 2
        self.e_wfm = ein("e_wfm", [nE, NCH_E, 128, 8 * 128])
        self.e_wtm = ein("e_wtm", [nE, 128, 8 * 68])
        self.e_convw = ein("e_convw", [nE, 128, 4 * 31])
        self.e_vec = ein("e_vec", [nE, 128, 16])
        self.e_pw2 = ein("e_pw2", [nE, 128, 4 * 512])
        self.e_woA = ein("e_woA", [nE, 128, 4 * 1024])
        self.e_woB = ein("e_woB", [nE, 64, 8 * 1024])
        self.e_woM = ein("e_woM", [nE, 128, 4 * 1024])
        self.o_wfm = ein("o_wfm", [nO, NCH_O, 128, 8 * 128])
        self.o_wtm = ein("o_wtm", [nO, 128, 8 * 1024])
        self.o_vln = ein("o_vln", [nO, 2, 1024])
        self.o_wsT = ein("o_wsT", [nO, 128, 8 * 128])
        self.o_bs = ein("o_bs", [nO, 1, 8 * 128])
        self.o_wo = ein("o_wo", [nO, 128, 12 * 1024])
        self.m_wk = ein("m_wk", [DEPTH, 128, 8 * 512])
        self.m_wv = ein("m_wv", [DEPTH, 128, 8 * 512])
        self.ln_gb = ein("ln_gb", [DEPTH, 2, 1024])
        self.y_out = nc.dram_tensor("y", [ntok, D], F32, kind="ExternalOutput").ap()
        self.X = self.scr("X", [ntok, D], F32)
        self.Xb = [[Buf() for _ in range(NT)] for _ in range(nseq)]
        mk = lambda n: [[Buf() for _ in range(n)] for _ in range(nseq)]
        self.CT = self.scr("CT", [nseq, 128, SEQ], F32); self.CTb = mk(1)
        self.ST = self.scr("ST", [nseq, 128, SEQ], F32); self.STb = mk(1)
        self.H = self.scr("H", [nseq, 4, 128, SEQ], BF16); self.Hb = mk(4)
        self.GA = self.scr("GA", [nseq, 4, 128, SEQ], BF16); self.GAb = mk(4)
        self.QR = self.scr("QR", [nseq, 4, 128, SEQ], BF16); self.QRb = mk(4)
        self.KR = self.scr("KR", [nseq, 2, 64, SEQ], BF16); self.KRb = mk(1)
        self.QI = self.scr("QI", [nseq, 2, 128, SEQ], BF16); self.QIb = mk(2)
        self.V = self.scr("V", [nseq, 128, NT, 64], BF16); self.Vb = mk(1)
        self.WI = self.scr("WI", [nseq, 128, NT, 4], F32); self.WIb = mk(1)
        self.GB = self.scr("GB", [nseq, 8, 64, SEQ], BF16); self.GBb = mk(4)
        self.MQ = self.scr("MQ", [nseq, 4, 128, SEQ], BF16); self.MQb = mk(4)
        self.GM = self.scr("GM", [nseq, 4, 128, SEQ], BF16); self.GMb = mk(4)
        self.U = self.scr("U", [nseq, 8, 128, SEQ], BF16); self.Ub = mk(8)
        self.GC = self.scr("GC", [nseq, 8, 128, SEQ], BF16); self.GCb = mk(8)
        self.VLN = self.scr("VLN", [nseq, SEQ, D], BF16); self.VLNb = mk(NT)
        self.CATA = self.scr("CATA", [nseq, 8, 128, SEQ], BF16); self.CATAb = mk(8)
        self.CATB = self.scr("CATB", [nseq, 8, 64, SEQ], BF16); self.CATBb = mk(NT)
        self.CATM = self.scr("CATM", [nseq, 4, 128, SEQ], BF16); self.CATMb = mk(4)
        self.PS = []
        for i in range(4):
            t = nc.alloc_psum_tensor(f"ps{i}", [128, 1024], F32)
            self.PS.append((t, [Buf(f"ps{i}a", True), Buf(f"ps{i}b", True)]))
        self.psi = 0
        self.consts()
        self.arena_t = nc.alloc_sbuf_tensor("arena", [128, self.ARENA], BF16)
        self.aoff = 0

    ARENA = 90 * 1024

    def scr(self, name, shape, dt):
        kind = "ExternalOutput" if self.debug else "Internal"
        if self.debug:
            self.dbg_names.append(name)
        return self.nc.dram_tensor("scr_" + name, list(shape), dt, kind=kind).ap()

    def phase(self):
        self.S.barrier()
        self.aoff = 0

    def alloc(self, shape, dt):
        n = int(np.prod(shape[1:]))
        ne = n * (2 if dt == F32 or dt == I32 else 1)
        ne = (ne + 15) // 16 * 16
        assert self.aoff + ne <= self.ARENA, ("arena overflow", self.aoff, ne)
        v = self.arena_t[0:shape[0], self.aoff:self.aoff + ne]
        self.aoff += ne
        if dt != BF16:
            v = v.bitcast(dt)
        v = v[:, 0:n]
        if len(shape) == 3:
            v = v.rearrange("p (a b) -> p a b", a=shape[1])
        return T(v, Buf())

    def ring(self, n, shape, dt):
        return [self.alloc(shape, dt) for _ in range(n)]

    def psum(self):
        i = self.psi % 8
        self.psi += 1
        t, bs = self.PS[i // 2]
        h = i % 2
        return T(t[:, h * 512:(h + 1) * 512], bs[h])

    def dma(self, out, in_, reads=(), writes=(), q=None):
        if q is None:
            q = "actd" if str(out.space) == "DRAM" else "spd"
        self.S.op(q, lambda e: e.dma_start(out=out, in_=in_), reads, writes)

    def mm(self, out, lhsT, rhs, start, stop, reads, writes):
        self.S.op("pe", lambda e: e.matmul(out, lhsT=lhsT, rhs=rhs, start=start, stop=stop), reads, writes)

    def act(self, out, in_, func, reads, writes, bias=0.0, scale=1.0):
        self.S.op("act", lambda e: e.activation(out=out, in_=in_, func=func, bias=bias, scale=scale), reads, writes)

    def ts(self, eng, out, in0, s1, s2, op0, op1, reads, writes, accum=None):
        if accum is None:
            if s2 is None:
                self.S.op(eng, lambda e: e.tensor_scalar(out=out, in0=in0, scalar1=s1, scalar2=None, op0=op0), reads, writes)
            else:
                self.S.op(eng, lambda e: e.tensor_scalar(out=out, in0=in0, scalar1=s1, scalar2=s2, op0=op0, op1=op1), reads, writes)
        else:
            self.S.op(eng, lambda e: e.tensor_scalar(out=out, in0=in0, scalar1=s1, scalar2=s2, op0=op0, op1=op1, accum_out=accum), reads, writes)

    def tt(self, eng, out, in0, in1, op, reads, writes):
        self.S.op(eng, lambda e: e.tensor_tensor(out=out, in0=in0, in1=in1, op=op), reads, writes)

    def stt(self, out, in0, scalar, in1, op0, op1, reads, writes):
        self.S.op("dve", lambda e: e.scalar_tensor_tensor(out=out, in0=in0, scalar=scalar, in1=in1, op0=op0, op1=op1), reads, writes)

    def copy(self, eng, out, in_, reads, writes):
        if eng == "act":
            self.S.op("act", lambda e: e.copy(out=out, in_=in_), reads, writes)
        else:
            self.S.op(eng, lambda e: e.tensor_copy(out=out, in_=in_), reads, writes)

    def consts(self):
        nc = self.nc
        al = lambda n, sh, dt: T(nc.alloc_sbuf_tensor(n, sh, dt), Buf(n))
        self.ident32 = al("ident32", [128, 128], F32)
        self.identb = al("identb", [128, 128], BF16)
        self.ident4 = al("ident4", [128, 512], BF16)
        self.ones = al("ones", [128, 128], BF16)
        self.negtri = al("negtri", [128, 128], F32)
        self.tril = al("tril", [128, 128], F32)
        self.pow2 = al("pow2", [128, NIT + 2], F32)
        self.invf = al("invf", [128, 2], F32)
        self.epsc = al("epsc", [128, 1], F32)
        self.dma(self.ident32.a[:], self.c_ident, writes=[self.ident32.b])
        self.dma(self.negtri.a[:], self.c_negtri, writes=[self.negtri.b])
        self.dma(self.tril.a[:], self.c_tril, writes=[self.tril.b])
        self.dma(self.pow2.a[:], self.c_pow2, writes=[self.pow2.b])
        self.dma(self.invf.a[:], self.c_invf, writes=[self.invf.b])
        self.copy("dve", self.identb.a[:], self.ident32.a[:], [self.ident32.b], [self.identb.b])
        for i in range(4):
            self.copy("dve", self.ident4.a[:, i * 128:(i + 1) * 128], self.ident32.a[:], [self.ident32.b], [self.ident4.b])
        self.S.op("dve", lambda e: e.memset(self.ones.a[:], 1.0), (), [self.ones.b])
        self.S.op("dve", lambda e: e.memset(self.epsc.a[:], LN_EPS), (), [self.epsc.b])

    def rope_tables(self, s):
        self.phase()
        posi = self.alloc([128, SEQ], I32)
        ang = self.alloc([128, SEQ], F32)
        t1 = self.alloc([128, SEQ], F32)
        t2 = self.alloc([128, SEQ], F32)
        self.dma(posi.a, self.pos_in[s:s + 1, :].partition_broadcast(128), writes=[posi.b])
        self.copy("dve", ang.a, posi.a, [posi.b], [ang.b])
        self.ts("dve", ang.a, ang.a, self.invf.a[:, 0:1], None, ALU.mult, None, [ang.b, self.invf.b], [ang.b])
        twopi = float(np.float32(2 * np.pi))
        magic = 12582912.0
        for (shift, dst, dstb, signed) in ((np.pi / 2, self.CT, self.CTb, False), (0.0, self.ST, self.STb, True)):
            self.ts("dve", t1.a, ang.a, float(shift), 1.0 / twopi, ALU.add, ALU.mult, [ang.b], [t1.b])
            self.ts("dve", t1.a, t1.a, magic, None, ALU.add, None, [t1.b], [t1.b])
            self.ts("dve", t1.a, t1.a, magic, None, ALU.subtract, None, [t1.b], [t1.b])
            self.stt(t2.a, t1.a, -twopi, ang.a, ALU.mult, ALU.add, [t1.b, ang.b], [t2.b])
            self.ts("dve", t2.a, t2.a, float(shift), 3.1415925, ALU.add, ALU.min, [t2.b], [t2.b])
            self.ts("dve", t2.a, t2.a, -3.1415925, None, ALU.max, None, [t2.b], [t2.b])
            self.act(t1.a, t2.a, AF.Sin, [t2.b], [t1.b])
            if signed:
                self.ts("dve", t1.a, t1.a, self.invf.a[:, 1:2], None, ALU.mult, None, [t1.b, self.invf.b], [t1.b])
            self.dma(dst[s], t1.a, [t1.b], [dstb[s][0]])

    def load_xT(self, layer, s, half, xT):
        xsrc = self.x_in if layer == 0 else self.X
        xl = self.ring(2, [128, D], F32)
        xb = self.ring(2, [128, D], BF16)
        for i in range(16):
            ti = half * 16 + i
            row = s * SEQ + ti * 128
            a, b = xl[i % 2], xb[i % 2]
            self.dma(a.a, xsrc[row:row + 128, :], [self.Xb[s][ti]], [a.b])
            self.copy("act", b.a, a.a, [a.b], [b.b])
            ps = self.psum()
            pv = ps.a.bitcast(BF16)
            for k in range(8):
                self.S.op("pe", lambda e, k=k, pv=pv, b=b: e.transpose(pv[:, k * 128:(k + 1) * 128], b.a[:, k * 128:(k + 1) * 128], self.identb.a[:]),
                          [b.b, self.identb.b], [ps.b])
            self.copy("dve", xT.a[:, :, i * 128:(i + 1) * 128], pv.rearrange("p (k t) -> p k t", k=8), [ps.b], [xT.b])

    def fm_chunk(self, wsrc, xT, wl, wb, ci):
        a, b = wl[ci % 2], wb[ci % 2]
        self.dma(a.a, wsrc, (), [a.b])
        self.copy("pool", b.a, a.a, [a.b], [b.b])
        outs = []
        for tc in range(4):
            ps = self.psum()
            for k in range(8):
                self.mm(ps.a, b.a[:, k * 128:(k + 1) * 128], xT.a[:, k, tc * 512:(tc + 1) * 512], k == 0, k == 7, [b.b, xT.b], [ps.b])
            outs.append(ps)
        return outs

    def inproj_even(self, layer, s):
        j = layer // 2
        for half in range(2):
            self.phase()
            t0 = half * 2048
            xT = self.alloc([128, 8, 2048], BF16)
            self.load_xT(layer, s, half, xT)
            cts = self.alloc([128, 2048], F32)
            sts = self.alloc([128, 2048], F32)
            self.dma(cts.a, self.CT[s, :, t0:t0 + 2048], [self.CTb[s][0]], [cts.b])
            self.dma(sts.a, self.ST[s, :, t0:t0 + 2048], [self.STb[s][0]], [sts.b])
            wl = self.ring(2, [128, 1024], F32)
            wb = self.ring(2, [128, 1024], BF16)
            stage = self.ring(3, [128, 2048], BF16)
            tmpa = self.ring(2, [128, 2048], F32)
            tmpb = self.ring(2, [128, 512], F32)
            st_i = [0]

            def next_stage():
                st_i[0] += 1
                return stage[st_i[0] % 3]
            ci = [0]

            def chunk(idx):
                r = self.fm_chunk(self.e_wfm[j, idx], xT, wl, wb, ci[0])
                ci[0] += 1
                return r
            sl = lambda tc: slice(tc * 512, (tc + 1) * 512)
            SEC = os.environ.get('KSEC', 'Asbpt')
            for i in range(4 if 'A' in SEC else 0):
                ta = tmpa[i % 2]
                for tc, ps in enumerate(chunk(4 + i)):
                    self.act(ta.a[:, sl(tc)], ps.a, AF.Sigmoid, [ps.b], [ta.b])
                sg = next_stage()
                for tc, ps in enumerate(chunk(0 + i)):
                    self.tt("dve", sg.a[:, sl(tc)], ps.a, ta.a[:, sl(tc)], ALU.mult, [ps.b, ta.b], [sg.b])
                self.dma(self.H[s, i, :, t0:t0 + 2048], sg.a, [sg.b], [self.Hb[s][i]])
            simple = [(8 + i, AF.Silu, self.GA, self.GAb, i) for i in range(4)]
            simple += [(26 + i, AF.Copy, self.MQ, self.MQb, i) for i in range(4)]
            simple += [(30 + i, AF.Silu, self.GM, self.GMb, i) for i in range(4)]
            for (idx, fn, dst, dstb, di) in (simple if 's' in SEC else []):
                sg = next_stage()
                for tc, ps in enumerate(chunk(idx)):
                    self.act(sg.a[:, sl(tc)], ps.a, fn, [ps.b], [sg.b])
                self.dma(dst[s, di, :, t0:t0 + 2048], sg.a, [sg.b], [dstb[s][di]])
            for i in range(4 if 'b' in SEC else 0):
                sg = next_stage()
                for tc, ps in enumerate(chunk(34 + i)):
                    self.act(sg.a[:, sl(tc)], ps.a, AF.Silu, [ps.b], [sg.b])
                self.dma(self.GB[s, 2 * i:2 * i + 2, :, t0:t0 + 2048].rearrange("h d t -> (h d) t"), sg.a, [sg.b], [self.GBb[s][i]])
            pairs = [(12 + i, 16 + i, self.QR[s, i, :, t0:t0 + 2048], self.QRb[s][i]) for i in range(4)]
            pairs += [(20, 21, self.KR[s, :, :, t0:t0 + 2048].rearrange("a d t -> (a d) t"), self.KRb[s][0])]
            pairs += [(22 + i, 24 + i, self.QI[s, i, :, t0:t0 + 2048], self.QIb[s][i]) for i in range(2)]
            KP = os.environ.get('KP', 'qki')
            pairs = [p for p, tag in zip(pairs, 'qqqqkii') if tag in KP]
            for pi, (i0, i1, dst, dstb) in enumerate(pairs if 'p' in SEC else []):
                ta = tmpa[pi % 2]
                for tc, ps in enumerate(chunk(i0)):
                    self.tt("dve", ta.a[:, sl(tc)], ps.a, cts.a[:, sl(tc)], ALU.mult, [ps.b, cts.b], [ta.b])
                sg = next_stage()
                for tc, ps in enumerate(chunk(i1)):
                    tb = tmpb[tc % 2]
                    self.tt("dve", tb.a, ps.a, sts.a[:, sl(tc)], ALU.mult, [ps.b, sts.b], [tb.b])
                    self.tt(os.environ.get("KPE", "pool"), sg.a[:, sl(tc)], tb.a, ta.a[:, sl(tc)], ALU.add, [tb.b, ta.b], [sg.b])
                self.dma(dst, sg.a, [sg.b], [dstb])
            if 't' not in SEC:
                continue
            wt32 = self.alloc([128, 8 * 68], F32)
            wtb = self.alloc([128, 8 * 68], BF16)
            self.dma(wt32.a, self.e_wtm[j], (), [wt32.b])
            self.copy("pool", wtb.a, wt32.a, [wt32.b], [wtb.b])
            vst = self.alloc([128, 16, 64], BF16)
            wst = self.alloc([128, 16, 4], F32)
            for i in range(16):
                ps = self.psum()
                for k in range(8):
                    self.mm(ps.a[:, 0:68], xT.a[:, k, i * 128:(i + 1) * 128], wtb.a[:, k * 68:(k + 1) * 68], k == 0, k == 7, [xT.b, wtb.b], [ps.b])
                KT = os.environ.get('KT', 'vwVW')
                if 'v' in KT:
                    self.copy("act", vst.a[:, i, :], ps.a[:, 0:64], [ps.b], [vst.b])
                if 'w' in KT:
                    self.ts("dve", wst.a[:, i, :], ps.a[:, 64:68], 1.0 / 16.0, None, ALU.mult, None, [ps.b], [wst.b])
            if 'V' in KT:
                self.dma(self.V[s, :, half * 16:half * 16 + 16, :], vst.a, [vst.b], [self.Vb[s][0]])
            if 'W' in KT:
                self.dma(self.WI[s, :, half * 16:half * 16 + 16, :], wst.a, [wst.b], [self.WIb[s][0]])

    def conv_branch(self, layer, s):
        j = layer // 2
        self.phase()
        vec = self.alloc([128, 16], F32)
        self.dma(vec.a, self.e_vec[j], (), [vec.b])
        cw = self.alloc([128, 4 * 31], F32)
        self.dma(cw.a, self.e_convw[j], (), [cw.b])
        dg = self.alloc([128, 4 * 31 * 128], BF16)
        for i in range(4 * 31):
            self.act(dg.a[:, i * 128:(i + 1) * 128], self.ident32.a[:], AF.Copy, [self.ident32.b, cw.b], [dg.b], scale=cw.a[:, i:i + 1])
        p32 = self.alloc([128, 4 * 512], F32)
        pw = self.alloc([128, 4 * 512], BF16)
        self.dma(p32.a, self.e_pw2[j], (), [p32.b])
        self.copy("pool", pw.a, p32.a, [p32.b], [pw.b])
        hs = [self.alloc([128, 32 + SEQ], BF16) for _ in range(4)]
        for c in range(4):
            self.S.op("pool", lambda e, c=c: e.memset(hs[c].a[:, 0:32], 0.0), (), [hs[c].b])
            self.dma(hs[c].a[:, 32:32 + SEQ], self.H[s, c], [self.Hb[s][c]], [hs[c].b])
        y32 = self.ring(2, [128, 4, 512], F32)
        ybf = self.ring(2, [128, 4, 512], BF16)
        ysq = self.ring(2, [128, 4, 512], BF16)
        st = self.ring(2, [128, 4, 512], F32)
        sb = self.ring(2, [128, 4, 512], BF16)
        ga = self.ring(2, [128, 4, 512], BF16)
        yo = self.ring(2, [128, 4, 512], BF16)
        for tc in range(8):
            t0 = tc * 512
            Y, YB, YQ, ST, SB, G, YO = (r[tc % 2] for r in (y32, ybf, ysq, st, sb, ga, yo))
            self.dma(G.a, self.GA[s, :, :, t0:t0 + 512].rearrange("c p t -> p c t"), [self.GAb[s][c] for c in range(4)], [G.b])
            for c in range(4):
                ps = self.psum()
                for k in range(31):
                    self.mm(ps.a, dg.a[:, (c * 31 + k) * 128:(c * 31 + k + 1) * 128], hs[c].a[:, 2 + k + t0:2 + k + t0 + 512],
                            k == 0, k == 30, [dg.b, hs[c].b], [ps.b])
                self.act(Y.a[:, c, :], ps.a, AF.Identity, [ps.b, vec.b], [Y.b], bias=vec.a[:, c:c + 1])
                self.copy("pool", YB.a[:, c, :], Y.a[:, c, :], [Y.b], [YB.b])
                self.act(YQ.a[:, c, :], Y.a[:, c, :], AF.Square, [Y.b], [YQ.b])
            p1, p2 = self.psum(), self.psum()
            for c in range(4):
                self.mm(p1.a, self.ones.a[:], YB.a[:, c, :], c == 0, c == 3, [self.ones.b, YB.b], [p1.b])
            for c in range(4):
                self.mm(p2.a, self.ones.a[:], YQ.a[:, c, :], c == 0, c == 3, [self.ones.b, YQ.b], [p2.b])
            mean, var, rstd = ST.a[:, 0, :], ST.a[:, 1, :], ST.a[:, 2, :]
            self.ts("dve", mean, p1.a, 1.0 / 512, None, ALU.mult, None, [p1.b], [ST.b])
            self.tt("dve", var, mean, mean, ALU.mult, [ST.b], [ST.b])
            self.stt(var, p2.a, 1.0 / 512, var, ALU.mult, ALU.subtract, [p2.b, ST.b], [ST.b])
            self.act(rstd, var, AF.Ln, [ST.b, self.epsc.b], [ST.b], bias=self.epsc.a[:, 0:1])
            self.act(rstd, rstd, AF.Exp, [ST.b], [ST.b], scale=-0.5)
            for c in range(4):
                self.tt("dve", ST.a[:, 3, :], Y.a[:, c, :], mean, ALU.subtract, [Y.b, ST.b], [ST.b])
                self.tt("pool", Y.a[:, c, :], ST.a[:, 3, :], rstd, ALU.mult, [ST.b], [Y.b])
                self.act(SB.a[:, c, :], Y.a[:, c, :], AF.Silu, [Y.b, vec.b], [SB.b], bias=vec.a[:, 8 + c:9 + c], scale=vec.a[:, 4 + c:5 + c])
            for n in range(4):
                ps = self.psum()
                for c in range(4):
                    self.mm(ps.a, pw.a[:, c * 512 + n * 128:c * 512 + (n + 1) * 128], SB.a[:, c, :], c == 0, c == 3, [pw.b, SB.b], [ps.b])
                self.stt(YO.a[:, n, :], ps.a, vec.a[:, 12 + n:13 + n], G.a[:, n, :], ALU.add, ALU.mult, [ps.b, vec.b, G.b], [YO.b])
            self.dma(self.CATA[s, 0:4, :, t0:t0 + 512].rearrange("c p t -> p c t"), YO.a, [YO.b], [self.CATAb[s][c] for c in range(4)])

    def dsa(self, layer, s):
        self.phase()
        QRs = self.alloc([128, 4, SEQ], BF16)
        KK = self.alloc([128, SEQ], BF16)
        KI2 = self.alloc([128, SEQ], BF16)
        QIs = self.alloc([128, 2, SEQ], BF16)
        Vs = self.alloc([128, NT, 64], BF16)
        WIs = self.alloc([128, NT, 4], F32)
        for c in range(4):
            self.dma(QRs.a[:, c, :], self.QR[s, c], self.QRb[s], [QRs.b])
        for hh in range(2):
            self.dma(KK.a[hh * 64:(hh + 1) * 64, :], self.KR[s, 0], self.KRb[s], [KK.b])
            self.dma(KI2.a[hh * 64:(hh + 1) * 64, :], self.KR[s, 1], self.KRb[s], [KI2.b])
        for c in range(2):
            self.dma(QIs.a[:, c, :], self.QI[s, c], self.QIb[s], [QIs.b])
        self.dma(Vs.a, self.V[s], self.Vb[s], [Vs.b])
        self.dma(WIs.a, self.WI[s], self.WIb[s], [WIs.b])
        score = self.ring(2, [128, SEQ], F32)
        mb = self.ring(4, [128, SEQ], BF16)
        junkD = self.alloc([128, SEQ], BF16)
        junkA = self.alloc([128, SEQ], BF16)
        rl = self.ring(3, [128, 256], F32)
        sm = self.ring(4, [128, 8 + NIT + 2], F32)
        pT = self.ring(2, [128, 1024], BF16)
        gb = self.ring(2, [64, 8, 128], BF16)
        rs = self.ring(2, [64, 1024], F32)
        yb = self.ring(2, [64, 8, 128], BF16)
        (tA, bA), (tO, bO), (tS, bS), (tI, bI) = self.PS
        nqb = int(os.environ.get('KQB', NT))

        def stageA1(qb):
            q0 = qb * 128
            L = q0 + 128
            SC, SM = score[qb % 2], sm[qb % 4]
            for c in range((L + 255) // 256):
                k0 = c * 256
                n = min(256, L - k0)
                col = lambda h: (h % 2) * 512 + (h // 2) * 256
                for h in range(4):
                    pr = slice((h % 2) * 64, (h % 2) * 64 + 64)
                    self.mm(tI[:, col(h):col(h) + n], QIs.a[pr, h // 2, q0:q0 + 128], KI2.a[pr, k0:k0 + n], True, True,
                            [QIs.b, KI2.b], [bI[h % 2]])
                dst = SC.a[:, k0:k0 + n]
                self.ts("dve", dst, tI[:, 0:n], 0.0, WIs.a[:, qb, 0:1], ALU.max, ALU.mult, [bI[0], WIs.b], [SC.b])
                for h in range(1, 4):
                    r = rl[h - 1]
                    self.act(r.a[:, 0:n], tI[:, col(h):col(h) + n], AF.Relu, [bI[h % 2]], [r.b])
                    self.stt(dst, r.a[:, 0:n], WIs.a[:, qb, h:h + 1], dst, ALU.mult, ALU.add, [r.b, WIs.b, SC.b], [SC.b])
            if qb >= 2:
                self.S.op("dve", lambda e: e.tensor_reduce(out=SM.a[:, 0:1], in_=SC.a[:, 0:L], axis=AX.X, op=ALU.max), [SC.b], [SM.b])
                self.S.op("dve", lambda e: e.tensor_reduce(out=SM.a[:, 1:2], in_=SC.a[:, 0:L], axis=AX.X, op=ALU.min), [SC.b], [SM.b])
            self.tt("dve", SC.a[:, q0:L], SC.a[:, q0:L], self.negtri.a[:], ALU.add, [SC.b, self.negtri.b], [SC.b])
            if qb >= 2:
                hi, lo, W0, mid = (SM.a[:, i:i + 1] for i in range(4))
                self.tt("dve", W0, hi, lo, ALU.subtract, [SM.b], [SM.b])
                self.ts("dve", SM.a[:, 8:8 + NIT + 2], self.pow2.a[:], W0, None, ALU.mult, None, [self.pow2.b, SM.b], [SM.b])
                self.tt("dve", mid, lo, SM.a[:, 9:10], ALU.add, [SM.b], [SM.b])
            else:
                self.S.op("dve", lambda e: e.memset(SM.a[:, 6:7], -1e29), (), [SM.b])

        def bisect_dve(qb):
            L = qb * 128 + 128
            SC, SM = score[qb % 2], sm[qb % 4]
            mid, cnt, tt_, thr = SM.a[:, 3:4], SM.a[:, 4:5], SM.a[:, 5:6], SM.a[:, 6:7]
            for k in range(1, NIT + 1):
                self.ts("dve", junkD.a[:, 0:L], SC.a[:, 0:L], mid, None, ALU.is_ge, ALU.add, [SC.b, SM.b], [junkD.b, SM.b], accum=cnt)
                if k < NIT:
                    self.ts("dve", tt_, cnt, 256.0, 0.5, ALU.is_ge, ALU.subtract, [SM.b], [SM.b])
                    self.stt(mid, tt_, SM.a[:, 8 + k:9 + k], mid, ALU.mult, ALU.add, [SM.b], [SM.b])
                else:
                    self.ts("dve", tt_, cnt, 256.0, 1.0, ALU.is_ge, ALU.subtract, [SM.b], [SM.b])
                    self.stt(thr, tt_, SM.a[:, 8 + k:9 + k], mid, ALU.mult, ALU.add, [SM.b], [SM.b])

        def bisect_act(qb):
            L = qb * 128 + 128
            SC, SM = score[qb % 2], sm[qb % 4]
            mid, sg, g, thr = SM.a[:, 3:4], SM.a[:, 4:5], SM.a[:, 5:6], SM.a[:, 6:7]
            for k in range(1, NIT + 1):
                self.S.op("act", lambda e: e.activation(out=junkA.a[:, 0:L], in_=SC.a[:, 0:L], func=AF.Sign, bias=mid, scale=-1.0, accum_out=sg),
                          [SC.b, SM.b], [junkA.b, SM.b])
                self.act(g, sg, AF.Sign, [SM.b], [SM.b], bias=float(L - 511), scale=-1.0)
                self.act(mid, g, AF.Identity, [SM.b], [SM.b], bias=mid, scale=SM.a[:, 9 + k:10 + k])
            self.act(thr, SM.a[:, 9 + NIT:10 + NIT], AF.Identity, [SM.b], [SM.b], bias=mid, scale=-1.0)

        def stageA3(qb):
            L = qb * 128 + 128
            SC, SM, MB = score[qb % 2], sm[qb % 4], mb[qb % 4]
            self.ts("dve", MB.a[:, 0:L], SC.a[:, 0:L], SM.a[:, 6:7], NEG, ALU.is_lt, ALU.mult, [SC.b, SM.b], [MB.b])

        def stageB(qb):
            q0 = qb * 128
            MB = mb[qb % 4]
            G, RS, YB = (r[qb % 2] for r in (gb, rs, yb))
            self.dma(G.a, self.GB[s, :, :, q0:q0 + 128].rearrange("h d t -> d h t"), self.GBb[s], [G.b])
            nsc = qb + 1
            for sc in range(nsc):
                P = pT[sc % 2]
                for par in range(2):
                    pr = slice(par * 64, par * 64 + 64)
                    o = tA[:, par * 512:(par + 1) * 512]
                    self.mm(o.rearrange("p (a b) -> p a b", a=4), KK.a[pr, sc * 128:(sc + 1) * 128], QRs.a[pr, :, q0:q0 + 128], True, False, [KK.b, QRs.b], [bA[par]])
                    self.mm(o, MB.a[:, sc * 128:(sc + 1) * 128], self.ident4.a[:], False, True, [MB.b, self.ident4.b], [bA[par]])
                self.act(P.a, tA[:, :], AF.Exp, [bA[0], bA[1]], [P.b], scale=0.125)
                for par in range(2):
                    cs = slice(par * 512, (par + 1) * 512)
                    self.mm(tO[0:64, cs], Vs.a[:, sc, :], P.a[:, cs], sc == 0, sc == nsc - 1, [Vs.b, P.b], [bO[par]])
                    self.mm(tS[0:64, cs], self.ones.a[:, 0:64], P.a[:, cs], sc == 0, sc == nsc - 1, [self.ones.b, P.b], [bS[par]])
            self.act(RS.a, tS[0:64, :], AF.Ln, [bS[0], bS[1]], [RS.b])
            self.act(RS.a, RS.a, AF.Exp, [RS.b], [RS.b], scale=-1.0)
            self.tt("dve", RS.a, tO[0:64, :], RS.a, ALU.mult, [bO[0], bO[1], RS.b], [RS.b])
            ybv = YB.a.rearrange("d (pair par) t -> d par pair t", par=2)
            gv = G.a.rearrange("d (pair par) t -> d par pair t", par=2)
            for par in range(2):
                self.tt("pool", ybv[:, par], RS.a[:, par * 512:(par + 1) * 512].rearrange("d (pair t) -> d pair t", pair=4), gv[:, par],
                        ALU.mult, [RS.b, G.b], [YB.b])
            self.dma(self.CATB[s, :, :, q0:q0 + 128].rearrange("h d t -> d h t"), YB.a, [YB.b], [self.CATBb[s][qb]])

        def stageA(p):
            a, b = 2 * p, 2 * p + 1
            stageA1(a)
            stageA1(b)
            if a >= 2:
                bisect_dve(a)
                bisect_act(b)
            stageA3(a)
            stageA3(b)
        npair = nqb // 2
        stageA(0)
        for p in range(npair):
            if p + 1 < npair:
                stageA(p + 1)
            stageB(2 * p)
            stageB(2 * p + 1)

    def mem_attn(self, layer, s):
        self.phase()
        ml = self.ring(2, [128, D], F32)
        mbf = self.ring(2, [128, D], BF16)
        memT = self.alloc([128, 8, MEM], BF16)
        for i in range(2):
            a, b = ml[i], mbf[i]
            self.dma(a.a, self.mem_in[s * MEM + i * 128:s * MEM + (i + 1) * 128, :], (), [a.b])
            self.copy("act", b.a, a.a, [a.b], [b.b])
            ps = self.psum()
            pv = ps.a.bitcast(BF16)
            for k in range(8):
                self.S.op("pe", lambda e, k=k, pv=pv, b=b: e.transpose(pv[:, k * 128:(k + 1) * 128], b.a[:, k * 128:(k + 1) * 128], self.identb.a[:]),
                          [b.b, self.identb.b], [ps.b])
            self.copy("dve", memT.a[:, :, i * 128:(i + 1) * 128], pv.rearrange("p (k t) -> p k t", k=8), [ps.b], [memT.b])
        w32 = self.alloc([128, 8 * 512], F32)
        wk = self.alloc([128, 8 * 512], BF16)
        wv = self.alloc([128, 8 * 512], BF16)
        self.dma(w32.a, self.m_wk[layer], (), [w32.b])
        self.copy("pool", wk.a, w32.a, [w32.b], [wk.b])
        self.dma(w32.a, self.m_wv[layer], [], [w32.b])
        self.copy("pool", wv.a, w32.a, [w32.b], [wv.b])
        kmT = self.alloc([128, 4, MEM], BF16)
        vm = self.alloc([128, 2, 512], BF16)
        for h in range(4):
            ps = self.psum()
            for k in range(8):
                self.mm(ps.a[:, 0:MEM], wk.a[:, k * 512 + h * 128:k * 512 + (h + 1) * 128], memT.a[:, k, :], k == 0, k == 7, [wk.b, memT.b], [ps.b])
            self.copy("act", kmT.a[:, h, :], ps.a[:, 0:MEM], [ps.b], [kmT.b])
        for mc in range(2):
            ps = self.psum()
            for k in range(8):
                self.mm(ps.a, memT.a[:, k, mc * 128:(mc + 1) * 128], wv.a[:, k * 512:(k + 1) * 512], k == 0, k == 7, [wv.b, memT.b], [ps.b])
            self.copy("act", vm.a[:, mc, :], ps.a, [ps.b], [vm.b])
        mq = self.ring(2, [128, 4, 512], BF16)
        gm = self.ring(2, [128, 4, 512], BF16)
        pT = self.ring(2, [128, 2, 512], BF16)
        rs = self.ring(2, [128, 512], F32)
        yo = self.ring(2, [128, 4, 512], BF16)
        scale = 128 ** -0.5
        for tc in range(8):
            t0 = tc * 512
            Q, G, YO = mq[tc % 2], gm[tc % 2], yo[tc % 2]
            self.dma(Q.a, self.MQ[s, :, :, t0:t0 + 512].rearrange("c p t -> p c t"), self.MQb[s], [Q.b])
            self.dma(G.a, self.GM[s, :, :, t0:t0 + 512].rearrange("c p t -> p c t"), self.GMb[s], [G.b])
            for h in range(4):
                P, R = pT[h % 2], rs[h % 2]
                for mc in range(2):
                    ps = self.psum()
                    self.mm(ps.a, kmT.a[:, h, mc * 128:(mc + 1) * 128], Q.a[:, h, :], True, True, [kmT.b, Q.b], [ps.b])
                    self.act(P.a[:, mc, :], ps.a, AF.Exp, [ps.b], [P.b], scale=scale)
                po, pS = self.psum(), self.psum()
                for mc in range(2):
                    self.mm(po.a, vm.a[:, mc, h * 128:(h + 1) * 128], P.a[:, mc, :], mc == 0, mc == 1, [vm.b, P.b], [po.b])
                for mc in range(2):
                    self.mm(pS.a, self.ones.a[:], P.a[:, mc, :], mc == 0, mc == 1, [self.ones.b, P.b], [pS.b])
                self.act(R.a, pS.a, AF.Ln, [pS.b], [R.b])
                self.act(R.a, R.a, AF.Exp, [R.b], [R.b], scale=-1.0)
                self.tt("dve", R.a, po.a, R.a, ALU.mult, [po.b, R.b], [R.b])
                self.tt("pool", YO.a[:, h, :], R.a, G.a[:, h, :], ALU.mult, [R.b, G.b], [YO.b])
            self.dma(self.CATM[s, :, :, t0:t0 + 512].rearrange("c p t -> p c t"), YO.a, [YO.b], self.CATMb[s])

    def inproj_odd(self, layer, s):
        j = layer // 2
        for half in range(2):
            self.phase()
            t0 = half * 2048
            xT = self.alloc([128, 8, 2048], BF16)
            self.load_xT(layer, s, half, xT)
            wl = self.ring(2, [128, 1024], F32)
            wb = self.ring(2, [128, 1024], BF16)
            stage = self.ring(3, [128, 2048], BF16)
            sl = lambda tc: slice(tc * 512, (tc + 1) * 512)
            specs = [(i, AF.Gelu_apprx_tanh, self.U, self.Ub, i) for i in range(8)]
            specs += [(8 + i, AF.Silu, self.GC, self.GCb, i) for i in range(8)]
            specs += [(16 + i, AF.Copy, self.MQ, self.MQb, i) for i in range(4)]
            specs += [(20 + i, AF.Silu, self.GM, self.GMb, i) for i in range(4)]
            for ci, (idx, fn, dst, dstb, di) in enumerate(specs):
                sg = stage[ci % 3]
                for tc, ps in enumerate(self.fm_chunk(self.o_wfm[j, idx], xT, wl, wb, ci)):
                    self.act(sg.a[:, sl(tc)], ps.a, fn, [ps.b], [sg.b])
                self.dma(dst[s, di, :, t0:t0 + 2048], sg.a, [sg.b], [dstb[s][di]])
            vg = self.alloc([128, 2048], F32)
            self.dma(vg.a[:, 0:1024], self.o_vln[j, 0:1, :].partition_broadcast(128), (), [vg.b])
            self.dma(vg.a[:, 1024:2048], self.o_vln[j, 1:2, :].partition_broadcast(128), (), [vg.b])
            v32 = self.ring(2, [128, 1024], F32)
            vo = self.ring(2, [128, 1024], BF16)
            stt_ = self.ring(2, [128, 16], F32)
            w32 = self.alloc([128, 8, 512], F32)
            wts = [self.alloc([128, 8, 512], BF16), self.alloc([128, 8, 512], BF16)]
            for hh in range(2):
                self.dma(w32.a, self.o_wtm[j].rearrange("p (k n) -> p k n", k=8)[:, :, hh * 512:(hh + 1) * 512], (), [w32.b])
                self.copy("pool", wts[hh].a, w32.a, [w32.b], [wts[hh].b])
            for i in range(16):
                ti = half * 16 + i
                Vt, VO, SS = v32[i % 2], vo[i % 2], stt_[i % 2]
                for hh in range(2):
                    ps = self.psum()
                    for k in range(8):
                        self.mm(ps.a, xT.a[:, k, i * 128:(i + 1) * 128], wts[hh].a[:, k, :], k == 0, k == 7, [xT.b, wts[hh].b], [ps.b])
                    self.act(Vt.a[:, hh * 512:(hh + 1) * 512], ps.a, AF.Gelu_apprx_tanh, [ps.b], [Vt.b])
                self.layernorm(Vt, SS, vg, 0, VO.a, VO.b)
                row = ti * 128
                self.dma(self.VLN[s, row:row + 128, :], VO.a, [VO.b], [self.VLNb[s][ti]])

    def layernorm(self, Z, SS, gb, goff, out, outb):
        for hh in range(2):
            self.S.op("dve", lambda e, hh=hh: e.bn_stats(out=SS.a[:, hh * 6:(hh + 1) * 6], in_=Z.a[:, hh * 512:(hh + 1) * 512]), [Z.b], [SS.b])
        self.S.op("dve", lambda e: e.bn_aggr(out=SS.a[:, 12:14], in_=SS.a[:, 0:12]), [SS.b], [SS.b])
        self.act(SS.a[:, 14:15], SS.a[:, 13:14], AF.Ln, [SS.b, self.epsc.b], [SS.b], bias=self.epsc.a[:, 0:1])
        self.act(SS.a[:, 14:15], SS.a[:, 14:15], AF.Exp, [SS.b], [SS.b], scale=-0.5)
        self.ts("dve", Z.a, Z.a, SS.a[:, 12:13], SS.a[:, 14:15], ALU.subtract, ALU.mult, [Z.b, SS.b], [Z.b])
        self.tt("pool", Z.a, Z.a, gb.a[:, goff:goff + 1024], ALU.mult, [Z.b, gb.b], [Z.b])
        self.tt("pool", out, Z.a, gb.a[:, goff + 1024:goff + 2048], ALU.add, [Z.b, gb.b], [outb])

    def sgu(self, layer, s):
        j = layer // 2
        self.phase()
        w32 = self.alloc([128, 8, 128], F32)
        wsb = self.alloc([128, 8, 128], BF16)
        self.dma(w32.a, self.o_wsT[j].rearrange("p (g t) -> p g t", g=8), (), [w32.b])
        for g in range(8):
            self.tt("dve", wsb.a[:, g, :], w32.a[:, g, :], self.tril.a[:], ALU.mult, [w32.b, self.tril.b], [wsb.b])
        bsb = self.alloc([128, 8, 128], F32)
        self.dma(bsb.a.rearrange("p g t -> p (g t)"), self.o_bs[j].partition_broadcast(128), (), [bsb.b])
        vt = self.ring(2, [128, 4, 1024], BF16)
        uu = self.ring(2, [128, 8, 512], BF16)
        gc = self.ring(2, [128, 8, 512], BF16)
        tm = self.ring(2, [128, 512], F32)
        yo = self.ring(2, [128, 8, 512], BF16)
        for tc in range(8):
            t0 = tc * 512
            Vt, Uu, Gc, YO = vt[tc % 2], uu[tc % 2], gc[tc % 2], yo[tc % 2]
            self.dma(Vt.a, self.VLN[s, t0:t0 + 512, :].rearrange("(c p) n -> p c n", p=128), self.VLNb[s][tc * 4:tc * 4 + 4], [Vt.b])
            self.dma(Uu.a, self.U[s, :, :, t0:t0 + 512].rearrange("c p t -> p c t"), self.Ub[s], [Uu.b])
            self.dma(Gc.a, self.GC[s, :, :, t0:t0 + 512].rearrange("c p t -> p c t"), self.GCb[s], [Gc.b])
            for g in range(8):
                ps = self.psum()
                for c in range(4):
                    self.mm(ps.a[:, c * 128:(c + 1) * 128], Vt.a[:, c, g * 128:(g + 1) * 128], wsb.a[:, g, :], True, True, [Vt.b, wsb.b], [ps.b])
                Tm = tm[g % 2]
                self.tt("dve", Tm.a.rearrange("p (c t) -> p c t", c=4), ps.a.rearrange("p (c t) -> p c t", c=4),
                        bsb.a[:, g:g + 1, :].to_broadcast([128, 4, 128]), ALU.add, [ps.b, bsb.b], [Tm.b])
                self.tt("pool", Tm.a, Tm.a, Uu.a[:, g, :], ALU.mult, [Tm.b, Uu.b], [Tm.b])
                self.tt("pool", YO.a[:, g, :], Tm.a, Gc.a[:, g, :], ALU.mult, [Tm.b, Gc.b], [YO.b])
            self.dma(self.CATA[s, :, :, t0:t0 + 512].rearrange("c p t -> p c t"), YO.a, [YO.b], self.CATAb[s])

    def outproj(self, layer, s):
        j = layer // 2
        even = layer % 2 == 0
        self.phase()
        last = layer == self.layers - 1
        xsrc = self.x_in if layer == 0 else self.X
        xdst = self.y_out if last else self.X
        gb = self.alloc([128, 2048], F32)
        self.dma(gb.a[:, 0:1024], self.ln_gb[layer, 0:1, :].partition_broadcast(128), (), [gb.b])
        self.dma(gb.a[:, 1024:2048], self.ln_gb[layer, 1:2, :].partition_broadcast(128), (), [gb.b])
        w32 = self.alloc([128, 4 * 1024], F32)
        wo = self.alloc([128, 12, 1024], BF16)
        if even:
            srcs = [(self.e_woA[j], 0, 128), (self.e_woM[j], 8, 128)]
            for (src, c0, np_) in srcs:
                self.dma(w32.a, src, (), [w32.b])
                self.copy("pool", wo.a[:, c0:c0 + 4, :], w32.a.rearrange("p (c n) -> p c n", c=4), [w32.b], [wo.b])
        woB = None
        if even:
            woB = self.alloc([64, 8, 1024], BF16)
            for hh in range(2):
                self.dma(w32.a[0:64, :], self.e_woB[j, :, hh * 4096:(hh + 1) * 4096], (), [w32.b])
                self.copy("pool", woB.a[:, hh * 4:(hh + 1) * 4, :], w32.a[0:64, :].rearrange("p (c n) -> p c n", c=4), [w32.b], [woB.b])
        else:
            for c0 in range(0, 12, 4):
                self.dma(w32.a, self.o_wo[j, :, c0 * 1024:(c0 + 4) * 1024], (), [w32.b])
                self.copy("pool", wo.a[:, c0:c0 + 4, :], w32.a.rearrange("p (c n) -> p c n", c=4), [w32.b], [wo.b])
        ca = self.ring(2, [128, 12, 512], BF16)
        cb = self.ring(2, [64, 8, 512], BF16)
        xt = self.ring(2, [128, D], F32)
        zz = self.ring(2, [128, D], F32)
        xo = self.ring(2, [128, D], F32)
        ss = self.ring(2, [128, 16], F32)
        for tc in range(8):
            t0 = tc * 512
            CA, CB = ca[tc % 2], cb[tc % 2]
            if even:
                self.dma(CA.a[:, 0:4, :], self.CATA[s, 0:4, :, t0:t0 + 512].rearrange("c p t -> p c t"), self.CATAb[s][0:4], [CA.b])
                self.dma(CB.a, self.CATB[s, :, :, t0:t0 + 512].rearrange("h d t -> d h t"), self.CATBb[s][tc * 4:tc * 4 + 4], [CB.b])
            else:
                self.dma(CA.a[:, 0:8, :], self.CATA[s, :, :, t0:t0 + 512].rearrange("c p t -> p c t"), self.CATAb[s], [CA.b])
            self.dma(CA.a[:, 8:12, :], self.CATM[s, :, :, t0:t0 + 512].rearrange("c p t -> p c t"), self.CATMb[s], [CA.b])
            for i in range(4):
                ti = tc * 4 + i
                row = s * SEQ + ti * 128
                Xt, Z, XO, SS = (r[ti % 2] for r in (xt, zz, xo, ss))
                self.dma(Xt.a, xsrc[row:row + 128, :], [self.Xb[s][ti]], [Xt.b])
                tsl = slice(i * 128, (i + 1) * 128)
                for nh in range(2):
                    ps = self.psum()
                    ns = slice(nh * 512, (nh + 1) * 512)
                    ops = []
                    for c in ([0, 1, 2, 3, 8, 9, 10, 11] if even else range(12)):
                        ops.append((CA.a[:, c, tsl], wo.a[:, c, ns], [CA.b, wo.b]))
                    if even:
                        for h in range(8):
                            ops.append((CB.a[:, h, tsl], woB.a[:, h, ns], [CB.b, woB.b]))
                    for oi, (l, r, rd) in enumerate(ops):
                        self.mm(ps.a, l, r, oi == 0, oi == len(ops) - 1, rd, [ps.b])
                    self.stt(Z.a[:, ns], Xt.a[:, ns], float(DN_ALPHA), ps.a, ALU.mult, ALU.add, [Xt.b, ps.b], [Z.b])
                self.layernorm(Z, SS, gb, 0, XO.a, XO.b)
                self.dma(xdst[row:row + 128, :], XO.a, [XO.b], [self.Xb[s][ti]])

    def build(self, only=None):
        on = lambda n: only is None or n in only
        for s in range(self.nseq):
            if on("rope"):
                self.rope_tables(s)
        for layer in range(self.layers):
            for s in range(self.nseq):
                if layer % 2 == 0:
                    if on("inproj"):
                        self.inproj_even(layer, s)
                    if on("conv"):
                        self.conv_branch(layer, s)
                    if on("dsa"):
                        self.dsa(layer, s)
                else:
                    if on("inproj"):
                        self.inproj_odd(layer, s)
                    if on("sgu"):
                        self.sgu(layer, s)
                if on("mem"):
                    self.mem_attn(layer, s)
                if on("out"):
                    self.outproj(layer, s)
        self.S.barrier()
        self.S.emit()
        return self.nc


def _fm(W, cols):
    w = W[:, cols]
    return np.ascontiguousarray(w.reshape(8, 128, 128).transpose(1, 0, 2).reshape(128, 8 * 128))


def _kmajor(W):
    K, N = W.shape
    return np.ascontiguousarray(W.reshape(K // 128, 128, N).transpose(1, 0, 2).reshape(128, (K // 128) * N))


def host_consts():
    inv = (10000.0 ** (-np.arange(0, 64, 2, dtype=np.float32) / np.float32(64))).astype(np.float32)
    invf = np.zeros((128, 2), np.float32)
    for p in range(128):
        invf[p, 0] = inv[p % 32]
        invf[p, 1] = -1.0 if (p % 64) < 32 else 1.0
    t = np.arange(128)
    negtri = np.where(t[None, :] <= t[:, None], 0.0, -1e30).astype(np.float32)
    tril = (t[None, :] >= t[:, None]).astype(np.float32)
    pow2 = np.tile((2.0 ** -np.arange(NIT + 2, dtype=np.float32))[None, :], (128, 1)).astype(np.float32)
    return {"c_ident": np.eye(128, dtype=np.float32), "c_invf": invf, "c_negtri": negtri, "c_tril": tril, "c_pow2": pow2}


def host_weights(e_w_in, e_conv_w, e_conv_b, e_cln_g, e_cln_b, e_pw2_w, e_pw2_b, e_w_out,
                 o_w_in, o_vln_g, o_vln_b, o_ws, o_bs, o_w_out, mem_wk, mem_wv, ln_g, ln_b):
    f = lambda a: np.asarray(a, dtype=np.float32)
    nE, nO = e_w_in.shape[0], o_w_in.shape[0]
    ar = np.arange
    sw = (ar(64) + 32) % 64
    out = {}
    wfm = np.zeros((nE, NCH_E, 128, 1024), np.float32)
    wtm = np.zeros((nE, 128, 8 * 68), np.float32)
    for j in range(nE):
        W = f(e_w_in[j])
        ch = []
        ch += [0 + 128 * i + ar(128) for i in range(4)]
        ch += [512 + 128 * i + ar(128) for i in range(4)]
        ch += [1024 + 128 * i + ar(128) for i in range(4)]
        ch += [1536 + 128 * i + ar(128) for i in range(4)]
        ch += [1536 + 128 * i + np.concatenate([sw, 64 + sw]) for i in range(4)]
        ch += [np.concatenate([2048 + ar(64), 2432 + ar(64)])]
        ch += [np.concatenate([2048 + sw, 2432 + sw])]
        ch += [2176 + 128 * i + ar(128) for i in range(2)]
        ch += [2176 + 128 * i + np.concatenate([sw, 64 + sw]) for i in range(2)]
        ch += [3012 + 128 * i + ar(128) for i in range(4)]
        ch += [3524 + 128 * i + ar(128) for i in range(4)]
        ch += [2500 + 128 * i + ar(128) for i in range(4)]
        assert len(ch) == NCH_E
        for c, cols in enumerate(ch):
            wfm[j, c] = _fm(W, cols)
        wtm[j] = _kmajor(W[:, np.concatenate([2112 + ar(64), 2496 + ar(4)])])
    out["e_wfm"], out["e_wtm"] = wfm, wtm
    cw = f(e_conv_w)
    out["e_convw"] = np.ascontiguousarray(cw.reshape(nE, 31, 4, 128).transpose(0, 3, 2, 1).reshape(nE, 128, 4 * 31))
    pc = lambda v: f(v).reshape(nE, 4, 128).transpose(0, 2, 1)
    out["e_vec"] = np.ascontiguousarray(np.concatenate([pc(e_conv_b), pc(e_cln_g), pc(e_cln_b), pc(e_pw2_b)], axis=2))
    out["e_pw2"] = np.stack([_kmajor(f(e_pw2_w[j])) for j in range(nE)])
    wo = f(e_w_out)
    out["e_woA"] = np.stack([_kmajor(wo[j, 0:512]) for j in range(nE)])
    out["e_woB"] = np.stack([np.ascontiguousarray(wo[j, 512:1024].reshape(8, 64, 1024).transpose(1, 0, 2).reshape(64, 8 * 1024)) for j in range(nE)])
    out["e_woM"] = np.stack([_kmajor(wo[j, 1024:1536]) for j in range(nE)])
    ofm = np.zeros((nO, NCH_O, 128, 1024), np.float32)
    otm = np.zeros((nO, 128, 8 * 1024), np.float32)
    for j in range(nO):
        W = f(o_w_in[j])
        ch = [0 + 128 * i + ar(128) for i in range(8)] + [2048 + 128 * i + ar(128) for i in range(8)]
        ch += [3072 + 128 * i + ar(128) for i in range(4)] + [3584 + 128 * i + ar(128) for i in range(4)]
        for c, cols in enumerate(ch):
            ofm[j, c] = _fm(W, cols)
        otm[j] = _kmajor(W[:, 1024:2048])
    out["o_wfm"], out["o_wtm"] = ofm, otm
    out["o_vln"] = np.ascontiguousarray(np.stack([f(o_vln_g), f(o_vln_b)], axis=1))
    out["o_wsT"] = np.ascontiguousarray(f(o_ws).transpose(0, 3, 1, 2).reshape(nO, 128, 8 * 128))
    out["o_bs"] = np.ascontiguousarray(f(o_bs).reshape(nO, 1, 8 * 128))
    out["o_wo"] = np.stack([_kmajor(f(o_w_out[j])) for j in range(nO)])
    out["m_wk"] = np.stack([_kmajor(f(mem_wk[l])) for l in range(DEPTH)])
    out["m_wv"] = np.stack([_kmajor(f(mem_wv[l])) for l in range(DEPTH)])
    out["ln_gb"] = np.ascontiguousarray(np.stack([f(ln_g), f(ln_b)], axis=1))
    return out


_PROG_CACHE = {}


def kernel(x, mem, positions, e_w_in, e_conv_w, e_conv_b, e_cln_g, e_cln_b, e_pw2_w, e_pw2_b, e_w_out,
           o_w_in, o_vln_g, o_vln_b, o_ws, o_bs, o_w_out, mem_wk, mem_wv, ln_g, ln_b):
    x = np.asarray(x, dtype=np.float32)
    mem = np.asarray(mem, dtype=np.float32)
    positions = np.asarray(positions, dtype=np.int32)
    B = x.shape[0]
    nseq = B // NCORES
    shared = host_consts()
    shared.update(host_weights(e_w_in, e_conv_w, e_conv_b, e_cln_g, e_cln_b, e_pw2_w, e_pw2_b, e_w_out,
                               o_w_in, o_vln_g, o_vln_b, o_ws, o_bs, o_w_out, mem_wk, mem_wv, ln_g, ln_b))
    if "p" not in _PROG_CACHE:
        _PROG_CACHE["p"] = Prog(nseq, DEPTH).build()
    nc = _PROG_CACHE["p"]
    in_maps = []
    for c in range(NCORES):
        d = dict(shared)
        d["x"] = np.ascontiguousarray(x[c * nseq:(c + 1) * nseq].reshape(nseq * SEQ, D))
        d["mem"] = np.ascontiguousarray(mem[c * nseq:(c + 1) * nseq].reshape(nseq * MEM, D))
        d["pos"] = np.ascontiguousarray(positions[c * nseq:(c + 1) * nseq])
        in_maps.append(d)
    res = run_bass_kernel_spmd(nc, in_maps, core_ids=list(range(NCORES)))
    out = np.concatenate([r["y"].reshape(nseq, SEQ, D) for r in res.results], axis=0)
    return out.astype(np.float32)
```

```python
import os
import numpy as np
import concourse.bass as bass
import concourse.mybir as mybir
from concourse.bass_utils import run_bass_kernel_spmd

F32 = mybir.dt.float32
BF16 = mybir.dt.bfloat16
I32 = mybir.dt.int32
AF = mybir.ActivationFunctionType
ALU = mybir.AluOpType
AX = mybir.AxisListType

NCORES = 8
D = 1024
SEQ = 4096
MEM = 256
DEPTH = 4
NT = SEQ // 128
LN_EPS = 1e-5
DN_ALPHA = (2 * DEPTH) ** 0.25
NIT = 22
NEG = -30000.0
SEM_EPOCH = 30000
DMA_K = 16
NCH_E = 38
NCH_O = 24


class Buf:
    __slots__ = ("name", "lw", "rd", "excl")

    def __init__(self, name="", excl=False):
        self.name = name
        self.lw = {}
        self.rd = {}
        self.excl = excl


class Stream:
    def __init__(self, S, name, eng, inc, k):
        self.S, self.name, self.eng, self.inc, self.k = S, name, eng, inc, k
        self.i = 0
        self.vals = [0] * k
        self.epochs = [0] * k
        self.sems = {}
        for j in range(k):
            self.sems[(name, j, 0)] = S.nc.alloc_semaphore(name=f"s_{name}_{j}_0")

    def next(self):
        j = self.i % self.k
        self.i += 1
        if self.vals[j] + self.inc > SEM_EPOCH:
            self.epochs[j] += 1
            self.vals[j] = 0
            self.sems[(self.name, j, self.epochs[j])] = self.S.nc.alloc_semaphore(
                name=f"s_{self.name}_{j}_{self.epochs[j]}")
        self.vals[j] += self.inc
        return ((self.name, j, self.epochs[j]), self.vals[j])

    def current(self):
        return [((self.name, j, self.epochs[j]), self.vals[j]) for j in range(self.k) if self.vals[j] > 0]


class Sched:
    def __init__(self, nc):
        self.nc = nc
        self.lists = {e: [] for e in ("pe", "dve", "act", "pool", "sp")}
        self.seen = {e: {} for e in self.lists}
        self.pending = {e: {} for e in self.lists}
        self.streams = {}
        for nm, eng, inc, k in (("pe", "pe", 1, 1), ("dve", "dve", 1, 1), ("act", "act", 1, 1),
                                ("pool", "pool", 1, 1), ("spd", "sp", 16, DMA_K), ("actd", "act", 16, DMA_K)):
            self.streams[nm] = Stream(self, nm, eng, inc, k)
        self.n = 0

    def semh(self, key):
        return self.streams[key[0]].sems[key]

    def op(self, stream, fn, reads=(), writes=()):
        st = self.streams[stream]
        eng = st.eng
        if any(b.excl for b in reads):
            writes = list(writes) + [b for b in reads if b.excl and b not in writes]
            reads = [b for b in reads if not b.excl]
        deps = dict(self.pending[eng])
        self.pending[eng] = {}
        if st.k > 1:
            jj = st.i % st.k
            if st.vals[jj] > 0:
                deps[(st.name, jj, st.epochs[jj])] = st.vals[jj]

        def add(d):
            for k, v in d.items():
                if deps.get(k, 0) < v:
                    deps[k] = v
        for b in reads:
            add(b.lw)
        for b in writes:
            add(b.lw)
            add(b.rd)
        seen = self.seen[eng]
        waits = []
        for k, v in deps.items():
            if stream == "pe" and k[0] == "pe":
                continue
            if seen.get(k, 0) >= v:
                continue
            seen[k] = v
            waits.append((k, v))
        key, val = st.next()
        self.lists[eng].append((waits, fn, key, st.inc))
        for b in reads:
            if b.rd.get(key, 0) < val:
                b.rd[key] = val
        for b in writes:
            b.lw = {key: val}
            b.rd = {}
        self.n += 1

    def barrier(self):
        cur = {}
        for s in self.streams.values():
            for k, v in s.current():
                cur[k] = v
        for e in self.lists:
            p = self.pending[e]
            for k, v in cur.items():
                if p.get(k, 0) < v:
                    p[k] = v

    def emit(self):
        nc = self.nc
        engmap = {"pe": "tensor", "dve": "vector", "act": "scalar", "pool": "gpsimd", "sp": "sync"}
        fin = []
        for s in self.streams.values():
            fin += s.current()
        with nc.Block() as block:
            for e, lst in self.lists.items():
                def body(engine, lst=lst, e=e):
                    for waits, fn, key, inc in lst:
                        for (wk, wv) in waits:
                            engine.wait_ge(self.semh(wk), wv)
                        fn(engine).then_inc(self.semh(key), inc)
                    if e == "sp":
                        for (wk, wv) in fin:
                            engine.wait_ge(self.semh(wk), wv)
                getattr(block, engmap[e])(body)


class T:
    __slots__ = ("a", "b")

    def __init__(self, a, b):
        self.a, self.b = a, b


class Prog:
    def __init__(self, nseq, layers, debug=False):
        self.nseq, self.layers, self.debug = nseq, layers, debug
        self.nc = nc = bass.Bass("TRN2", target_bir_lowering=False)
        self.S = Sched(nc)
        self.dbg_names = []
        ntok = nseq * SEQ
        ein = lambda n, sh, dt=F32: nc.dram_tensor(n, list(sh), dt, kind="ExternalInput").ap()
        self.x_in = ein("x", [ntok, D])
        self.mem_in = ein("mem", [nseq * MEM, D])
        self.pos_in = ein("pos", [nseq, SEQ], I32)
        self.c_ident = ein("c_ident", [128, 128])
        self.c_invf = ein("c_invf", [128, 2])
        self.c_negtri = ein("c_negtri", [128, 128])
        self.c_tril = ein("c_tril", [128, 128])
        self.c_pow2 = ein("c_pow2", [128, NIT + 2])
        nE, nO = (DEPTH + 1) // 2, DEPTH // 2
        self.e_wfm = ein("e_wfm", [nE, NCH_E, 128, 8 * 128])
        self.e_wtm = ein("e_wtm", [nE, 128, 8 * 68])
        self.e_convw = ein("e_convw", [nE, 128, 4 * 31])
        self.e_vec = ein("e_vec", [nE, 128, 16])
        self.e_pw2 = ein("e_pw2", [nE, 128, 4 * 512])
        self.e_woA = ein("e_woA", [nE, 128, 4 * 1024])
        self.e_woB = ein("e_woB", [nE, 64, 8 * 1024])
        self.e_woM = ein("e_woM", [nE, 128, 4 * 1024])
        self.o_wfm = ein("o_wfm", [nO, NCH_O, 128, 8 * 128])
        self.o_wtm = ein("o_wtm", [nO, 128, 8 * 1024])
        self.o_vln = ein("o_vln", [nO, 2, 1024])
        self.o_wsT = ein("o_wsT", [nO, 128, 8 * 128])
        self.o_bs = ein("o_bs", [nO, 1, 8 * 128])
        self.o_wo = ein("o_wo", [nO, 128, 12 * 1024])
        self.m_wk = ein("m_wk", [DEPTH, 128, 8 * 512])
        self.m_wv = ein("m_wv", [DEPTH, 128, 8 * 512])
        self.ln_gb = ein("ln_gb", [DEPTH, 2, 1024])
        self.y_out = nc.dram_tensor("y", [ntok, D], F32, kind="ExternalOutput").ap()
        self.X = self.scr("X", [ntok, D], F32)
        self.Xb = [[Buf() for _ in range(NT)] for _ in range(nseq)]
        mk = lambda n: [[Buf() for _ in range(n)] for _ in range(nseq)]
        self.CT = self.scr("CT", [nseq, 128, SEQ], F32); self.CTb = mk(1)
        self.ST = self.scr("ST", [nseq, 128, SEQ], F32); self.STb = mk(1)
        self.H = self.scr("H", [nseq, 4, 128, SEQ], BF16); self.Hb = mk(4)
        self.GA = self.scr("GA", [nseq, 4, 128, SEQ], BF16); self.GAb = mk(4)
        self.QR = self.scr("QR", [nseq, 4, 128, SEQ], BF16); self.QRb = mk(4)
        self.KR = self.scr("KR", [nseq, 2, 64, SEQ], BF16); self.KRb = mk(1)
        self.QI = self.scr("QI", [nseq, 2, 128, SEQ], BF16); self.QIb = mk(2)
        self.V = self.scr("V", [nseq, 128, NT, 64], BF16); self.Vb = mk(1)
        self.WI = self.scr("WI", [nseq, 128, NT, 4], F32); self.WIb = mk(1)
        self.GB = self.scr("GB", [nseq, 8, 64, SEQ], BF16); self.GBb = mk(4)
        self.MQ = self.scr("MQ", [nseq, 4, 128, SEQ], BF16); self.MQb = mk(4)
        self.GM = self.scr("GM", [nseq, 4, 128, SEQ], BF16); self.GMb = mk(4)
        self.U = self.scr("U", [nseq, 8, 128, SEQ], BF16); self.Ub = mk(8)
        self.GC = self.scr("GC", [nseq, 8, 128, SEQ], BF16); self.GCb = mk(8)
        self.VLN = self.scr("VLN", [nseq, SEQ, D], BF16); self.VLNb = mk(NT)
        self.CATA = self.scr("CATA", [nseq, 8, 128, SEQ], BF16); self.CATAb = mk(8)
        self.CATB = self.scr("CATB", [nseq, 8, 64, SEQ], BF16); self.CATBb = mk(NT)
        self.CATM = self.scr("CATM", [nseq, 4, 128, SEQ], BF16); self.CATMb = mk(4)
        self.PS = []
        for i in range(4):
            t = nc.alloc_psum_tensor(f"ps{i}", [128, 1024], F32)
            self.PS.append((t, [Buf(f"ps{i}a", True), Buf(f"ps{i}b", True)]))
        self.psi = 0
        self.consts()
        self.arena_t = nc.alloc_sbuf_tensor("arena", [128, self.ARENA], BF16)
        self.aoff = 0

    ARENA = 90 * 1024

    def scr(self, name, shape, dt):
        kind = "ExternalOutput" if self.debug else "Internal"
        if self.debug:
            self.dbg_names.append(name)
        return self.nc.dram_tensor("scr_" + name, list(shape), dt, kind=kind).ap()

    def phase(self):
        self.S.barrier()
        self.aoff = 0

    def alloc(self, shape, dt):
        n = int(np.prod(shape[1:]))
        ne = n * (2 if dt == F32 or dt == I32 else 1)
        ne = (ne + 15) // 16 * 16
        assert self.aoff + ne <= self.ARENA, ("arena overflow", self.aoff, ne)
        v = self.arena_t[0:shape[0], self.aoff:self.aoff + ne]
        self.aoff += ne
        if dt != BF16:
            v = v.bitcast(dt)
        v = v[:, 0:n]
        if len(shape) == 3:
            v = v.rearrange("p (a b) -> p a b", a=shape[1])
        return T(v, Buf())

    def ring(self, n, shape, dt):
        return [self.alloc(shape, dt) for _ in range(n)]

    def psum(self):
        i = self.psi % 8
        self.psi += 1
        t, bs = self.PS[i // 2]
        h = i % 2
        return T(t[:, h * 512:(h + 1) * 512], bs[h])

    def dma(self, out, in_, reads=(), writes=(), q=None):
        if q is None:
            q = "actd" if str(out.space) == "DRAM" else "spd"
        self.S.op(q, lambda e: e.dma_start(out=out, in_=in_), reads, writes)

    def mm(self, out, lhsT, rhs, start, stop, reads, writes):
        self.S.op("pe", lambda e: e.matmul(out, lhsT=lhsT, rhs=rhs, start=start, stop=stop), reads, writes)

    def act(self, out, in_, func, reads, writes, bias=0.0, scale=1.0):
        self.S.op("act", lambda e: e.activation(out=out, in_=in_, func=func, bias=bias, scale=scale), reads, writes)

    def ts(self, eng, out, in0, s1, s2, op0, op1, reads, writes, accum=None):
        if accum is None:
            if s2 is None:
                self.S.op(eng, lambda e: e.tensor_scalar(out=out, in0=in0, scalar1=s1, scalar2=None, op0=op0), reads, writes)
            else:
                self.S.op(eng, lambda e: e.tensor_scalar(out=out, in0=in0, scalar1=s1, scalar2=s2, op0=op0, op1=op1), reads, writes)
        else:
            self.S.op(eng, lambda e: e.tensor_scalar(out=out, in0=in0, scalar1=s1, scalar2=s2, op0=op0, op1=op1, accum_out=accum), reads, writes)

    def tt(self, eng, out, in0, in1, op, reads, writes):
        self.S.op(eng, lambda e: e.tensor_tensor(out=out, in0=in0, in1=in1, op=op), reads, writes)

    def stt(self, out, in0, scalar, in1, op0, op1, reads, writes):
        self.S.op("dve", lambda e: e.scalar_tensor_tensor(out=out, in0=in0, scalar=scalar, in1=in1, op0=op0, op1=op1), reads, writes)

    def copy(self, eng, out, in_, reads, writes):
        if eng == "act":
            self.S.op("act", lambda e: e.copy(out=out, in_=in_), reads, writes)
        else:
            self.S.op(eng, lambda e: e.tensor_copy(out=out, in_=in_), reads, writes)

    def consts(self):
        nc = self.nc
        al = lambda n, sh, dt: T(nc.alloc_sbuf_tensor(n, sh, dt), Buf(n))
        self.ident32 = al("ident32", [128, 128], F32)
        self.identb = al("identb", [128, 128], BF16)
        self.ident4 = al("ident4", [128, 512], BF16)
        self.ones = al("ones", [128, 128], BF16)
        self.negtri = al("negtri", [128, 128], F32)
        self.tril = al("tril", [128, 128], F32)
        self.pow2 = al("pow2", [128, NIT + 2], F32)
        self.invf = al("invf", [128, 2], F32)
        self.epsc = al("epsc", [128, 1], F32)
        self.dma(self.ident32.a[:], self.c_ident, writes=[self.ident32.b])
        self.dma(self.negtri.a[:], self.c_negtri, writes=[self.negtri.b])
        self.dma(self.tril.a[:], self.c_tril, writes=[self.tril.b])
        self.dma(self.pow2.a[:], self.c_pow2, writes=[self.pow2.b])
        self.dma(self.invf.a[:], self.c_invf, writes=[self.invf.b])
        self.copy("dve", self.identb.a[:], self.ident32.a[:], [self.ident32.b], [self.identb.b])
        for i in range(4):
            self.copy("dve", self.ident4.a[:, i * 128:(i + 1) * 128], self.ident32.a[:], [self.ident32.b], [self.ident4.b])
        self.S.op("dve", lambda e: e.memset(self.ones.a[:], 1.0), (), [self.ones.b])
        self.S.op("dve", lambda e: e.memset(self.epsc.a[:], LN_EPS), (), [self.epsc.b])

    def rope_tables(self, s):
        self.phase()
        posi = self.alloc([128, SEQ], I32)
        ang = self.alloc([128, SEQ], F32)
        t1 = self.alloc([128, SEQ], F32)
        t2 = self.alloc([128, SEQ], F32)
        self.dma(posi.a, self.pos_in[s:s + 1, :].partition_broadcast(128), writes=[posi.b])
        self.copy("dve", ang.a, posi.a, [posi.b], [ang.b])
        self.ts("dve", ang.a, ang.a, self.invf.a[:, 0:1], None, ALU.mult, None, [ang.b, self.invf.b], [ang.b])
        twopi = float(np.float32(2 * np.pi))
        magic = 12582912.0
        for (shift, dst, dstb, signed) in ((np.pi / 2, self.CT, self.CTb, False), (0.0, self.ST, self.STb, True)):
            self.ts("dve", t1.a, ang.a, float(shift), 1.0 / twopi, ALU.add, ALU.mult, [ang.b], [t1.b])
            self.ts("dve", t1.a, t1.a, magic, None, ALU.add, None, [t1.b], [t1.b])
            self.ts("dve", t1.a, t1.a, magic, None, ALU.subtract, None, [t1.b], [t1.b])
            self.stt(t2.a, t1.a, -twopi, ang.a, ALU.mult, ALU.add, [t1.b, ang.b], [t2.b])
            self.ts("dve", t2.a, t2.a, float(shift), 3.1415925, ALU.add, ALU.min, [t2.b], [t2.b])
            self.ts("dve", t2.a, t2.a, -3.1415925, None, ALU.max, None, [t2.b], [t2.b])
            self.act(t1.a, t2.a, AF.Sin, [t2.b], [t1.b])
            if signed:
                self.ts("dve", t1.a, t1.a, self.invf.a[:, 1:2], None, ALU.mult, None, [t1.b, self.invf.b], [t1.b])
            self.dma(dst[s], t1.a, [t1.b], [dstb[s][0]])

    def load_xT(self, layer, s, half, xT):
        xsrc = self.x_in if layer == 0 else self.X
        xl = self.ring(2, [128, D], F32)
        xb = self.ring(2, [128, D], BF16)
        for i in range(16):
            ti = half * 16 + i
            row = s * SEQ + ti * 128
            a, b = xl[i % 2], xb[i % 2]
            self.dma(a.a, xsrc[row:row + 128, :], [self.Xb[s][ti]], [a.b])
            self.copy("act", b.a, a.a, [a.b], [b.b])
            ps = self.psum()
            pv = ps.a.bitcast(BF16)
            for k in range(8):
                self.S.op("pe", lambda e, k=k, pv=pv, b=b: e.transpose(pv[:, k * 128:(k + 1) * 128], b.a[:, k * 128:(k + 1) * 128], self.identb.a[:]),
                          [b.b, self.identb.b], [ps.b])
            self.copy("dve", xT.a[:, :, i * 128:(i + 1) * 128], pv.rearrange("p (k t) -> p k t", k=8), [ps.b], [xT.b])

    def fm_chunk(self, wsrc, xT, wl, wb, ci):
        a, b = wl[ci % 2], wb[ci % 2]
        self.dma(a.a, wsrc, (), [a.b])
        self.copy("pool", b.a, a.a, [a.b], [b.b])
        outs = []
        for tc in range(4):
            ps = self.psum()
            for k in range(8):
                self.mm(ps.a, b.a[:, k * 128:(k + 1) * 128], xT.a[:, k, tc * 512:(tc + 1) * 512], k == 0, k == 7, [b.b, xT.b], [ps.b])
            outs.append(ps)
        return outs

    def inproj_even(self, layer, s):
        j = layer // 2
        for half in range(2):
            self.phase()
            t0 = half * 2048
            xT = self.alloc([128, 8, 2048], BF16)
            self.load_xT(layer, s, half, xT)
            cts = self.alloc([128, 2048], F32)
            sts = self.alloc([128, 2048], F32)
            self.dma(cts.a, self.CT[s, :, t0:t0 + 2048], [self.CTb[s][0]], [cts.b])
            self.dma(sts.a, self.ST[s, :, t0:t0 + 2048], [self.STb[s][0]], [sts.b])
            wl = self.ring(2, [128, 1024], F32)
            wb = self.ring(2, [128, 1024], BF16)
            stage = self.ring(3, [128, 2048], BF16)
            tmpa = self.ring(2, [128, 2048], F32)
            tmpb = self.ring(2, [128, 512], F32)
            st_i = [0]

            def next_stage():
                st_i[0] += 1
                return stage[st_i[0] % 3]
            ci = [0]

            def chunk(idx):
                r = self.fm_chunk(self.e_wfm[j, idx], xT, wl, wb, ci[0])
                ci[0] += 1
                return r
            sl = lambda tc: slice(tc * 512, (tc + 1) * 512)
            SEC = os.environ.get('KSEC', 'Asbpt')
            for i in range(4 if 'A' in SEC else 0):
                ta = tmpa[i % 2]
                for tc, ps in enumerate(chunk(4 + i)):
                    self.act(ta.a[:, sl(tc)], ps.a, AF.Sigmoid, [ps.b], [ta.b])
                sg = next_stage()
                for tc, ps in enumerate(chunk(0 + i)):
                    self.tt("dve", sg.a[:, sl(tc)], ps.a, ta.a[:, sl(tc)], ALU.mult, [ps.b, ta.b], [sg.b])
                self.dma(self.H[s, i, :, t0:t0 + 2048], sg.a, [sg.b], [self.Hb[s][i]])
            simple = [(8 + i, AF.Silu, self.GA, self.GAb, i) for i in range(4)]
            simple += [(26 + i, AF.Copy, self.MQ, self.MQb, i) for i in range(4)]
            simple += [(30 + i, AF.Silu, self.GM, self.GMb, i) for i in range(4)]
            for (idx, fn, dst, dstb, di) in (simple if 's' in SEC else []):
                sg = next_stage()
                for tc, ps in enumerate(chunk(idx)):
                    self.act(sg.a[:, sl(tc)], ps.a, fn, [ps.b], [sg.b])
                self.dma(dst[s, di, :, t0:t0 + 2048], sg.a, [sg.b], [dstb[s][di]])
            for i in range(4 if 'b' in SEC else 0):
                sg = next_stage()
                for tc, ps in enumerate(chunk(34 + i)):
                    self.act(sg.a[:, sl(tc)], ps.a, AF.Silu, [ps.b], [sg.b])
                self.dma(self.GB[s, 2 * i:2 * i + 2, :, t0:t0 + 2048].rearrange("h d t -> (h d) t"), sg.a, [sg.b], [self.GBb[s][i]])
            pairs = [(12 + i, 16 + i, self.QR[s, i, :, t0:t0 + 2048], self.QRb[s][i]) for i in range(4)]
            pairs += [(20, 21, self.KR[s, :, :, t0:t0 + 2048].rearrange("a d t -> (a d) t"), self.KRb[s][0])]
            pairs += [(22 + i, 24 + i, self.QI[s, i, :, t0:t0 + 2048], self.QIb[s][i]) for i in range(2)]
            KP = os.environ.get('KP', 'qki')
            pairs = [p for p, tag in zip(pairs, 'qqqqkii') if tag in KP]
            for pi, (i0, i1, dst, dstb) in enumerate(pairs if 'p' in SEC else []):
                ta = tmpa[pi % 2]
                for tc, ps in enumerate(chunk(i0)):
                    self.tt("dve", ta.a[:, sl(tc)], ps.a, cts.a[:, sl(tc)], ALU.mult, [ps.b, cts.b], [ta.b])
                sg = next_stage()
                for tc, ps in enumerate(chunk(i1)):
                    tb = tmpb[tc % 2]
                    self.tt("dve", tb.a, ps.a, sts.a[:, sl(tc)], ALU.mult, [ps.b, sts.b], [tb.b])
                    self.tt(os.environ.get("KPE", "pool"), sg.a[:, sl(tc)], tb.a, ta.a[:, sl(tc)], ALU.add, [tb.b, ta.b], [sg.b])
                self.dma(dst, sg.a, [sg.b], [dstb])
            if 't' not in SEC:
                continue
            wt32 = self.alloc([128, 8 * 68], F32)
            wtb = self.alloc([128, 8 * 68], BF16)
            self.dma(wt32.a, self.e_wtm[j], (), [wt32.b])
            self.copy("pool", wtb.a, wt32.a, [wt32.b], [wtb.b])
            vst = self.alloc([128, 16, 64], BF16)
            wst = self.alloc([128, 16, 4], F32)
            for i in range(16):
                ps = self.psum()
                for k in range(8):
                    self.mm(ps.a[:, 0:68], xT.a[:, k, i * 128:(i + 1) * 128], wtb.a[:, k * 68:(k + 1) * 68], k == 0, k == 7, [xT.b, wtb.b], [ps.b])
                KT = os.environ.get('KT', 'vwVW')
                if 'v' in KT:
                    self.copy("act", vst.a[:, i, :], ps.a[:, 0:64], [ps.b], [vst.b])
                if 'w' in KT:
                    self.ts("dve", wst.a[:, i, :], ps.a[:, 64:68], 1.0 / 16.0, None, ALU.mult, None, [ps.b], [wst.b])
            if 'V' in KT:
                self.dma(self.V[s, :, half * 16:half * 16 + 16, :], vst.a, [vst.b], [self.Vb[s][0]])
            if 'W' in KT:
                self.dma(self.WI[s, :, half * 16:half * 16 + 16, :], wst.a, [wst.b], [self.WIb[s][0]])

    def conv_branch(self, layer, s):
        j = layer // 2
        self.phase()
        vec = self.alloc([128, 16], F32)
        self.dma(vec.a, self.e_vec[j], (), [vec.b])
        cw = self.alloc([128, 4 * 31], F32)
        self.dma(cw.a, self.e_convw[j], (), [cw.b])
        dg = self.alloc([128, 4 * 31 * 128], BF16)
        for i in range(4 * 31):
            self.act(dg.a[:, i * 128:(i + 1) * 128], self.ident32.a[:], AF.Copy, [self.ident32.b, cw.b], [dg.b], scale=cw.a[:, i:i + 1])
        p32 = self.alloc([128, 4 * 512], F32)
        pw = self.alloc([128, 4 * 512], BF16)
        self.dma(p32.a, self.e_pw2[j], (), [p32.b])
        self.copy("pool", pw.a, p32.a, [p32.b], [pw.b])
        hs = [self.alloc([128, 32 + SEQ], BF16) for _ in range(4)]
        for c in range(4):
            self.S.op("pool", lambda e, c=c: e.memset(hs[c].a[:, 0:32], 0.0), (), [hs[c].b])
            self.dma(hs[c].a[:, 32:32 + SEQ], self.H[s, c], [self.Hb[s][c]], [hs[c].b])
        y32 = self.ring(2, [128, 4, 512], F32)
        ybf = self.ring(2, [128, 4, 512], BF16)
        ysq = self.ring(2, [128, 4, 512], BF16)
        st = self.ring(2, [128, 4, 512], F32)
        sb = self.ring(2, [128, 4, 512], BF16)
        ga = self.ring(2, [128, 4, 512], BF16)
        yo = self.ring(2, [128, 4, 512], BF16)
        for tc in range(8):
            t0 = tc * 512
            Y, YB, YQ, ST, SB, G, YO = (r[tc % 2] for r in (y32, ybf, ysq, st, sb, ga, yo))
            self.dma(G.a, self.GA[s, :, :, t0:t0 + 512].rearrange("c p t -> p c t"), [self.GAb[s][c] for c in range(4)], [G.b])
            for c in range(4):
                ps = self.psum()
                for k in range(31):
                    self.mm(ps.a, dg.a[:, (c * 31 + k) * 128:(c * 31 + k + 1) * 128], hs[c].a[:, 2 + k + t0:2 + k + t0 + 512],
                            k == 0, k == 30, [dg.b, hs[c].b], [ps.b])
                self.act(Y.a[:, c, :], ps.a, AF.Identity, [ps.b, vec.b], [Y.b], bias=vec.a[:, c:c + 1])
                self.copy("pool", YB.a[:, c, :], Y.a[:, c, :], [Y.b], [YB.b])
                self.act(YQ.a[:, c, :], Y.a[:, c, :], AF.Square, [Y.b], [YQ.b])
            p1, p2 = self.psum(), self.psum()
            for c in range(4):
                self.mm(p1.a, self.ones.a[:], YB.a[:, c, :], c == 0, c == 3, [self.ones.b, YB.b], [p1.b])
            for c in range(4):
                self.mm(p2.a, self.ones.a[:], YQ.a[:, c, :], c == 0, c == 3, [self.ones.b, YQ.b], [p2.b])
            mean, var, rstd = ST.a[:, 0, :], ST.a[:, 1, :], ST.a[:, 2, :]
            self.ts("dve", mean, p1.a, 1.0 / 512, None, ALU.mult, None, [p1.b], [ST.b])
            self.tt("dve", var, mean, mean, ALU.mult, [ST.b], [ST.b])
            self.stt(var, p2.a, 1.0 / 512, var, ALU.mult, ALU.subtract, [p2.b, ST.b], [ST.b])
            self.act(rstd, var, AF.Ln, [ST.b, self.epsc.b], [ST.b], bias=self.epsc.a[:, 0:1])
            self.act(rstd, rstd, AF.Exp, [ST.b], [ST.b], scale=-0.5)
            for c in range(4):
                self.tt("dve", ST.a[:, 3, :], Y.a[:, c, :], mean, ALU.subtract, [Y.b, ST.b], [ST.b])
                self.tt("pool", Y.a[:, c, :], ST.a[:, 3, :], rstd, ALU.mult, [ST.b], [Y.b])
                self.act(SB.a[:, c, :], Y.a[:, c, :], AF.Silu, [Y.b, vec.b], [SB.b], bias=vec.a[:, 8 + c:9 + c], scale=vec.a[:, 4 + c:5 + c])
            for n in range(4):
                ps = self.psum()
                for c in range(4):
                    self.mm(ps.a, pw.a[:, c * 512 + n * 128:c * 512 + (n + 1) * 128], SB.a[:, c, :], c == 0, c == 3, [pw.b, SB.b], [ps.b])
                self.stt(YO.a[:, n, :], ps.a, vec.a[:, 12 + n:13 + n], G.a[:, n, :], ALU.add, ALU.mult, [ps.b, vec.b, G.b], [YO.b])
            self.dma(self.CATA[s, 0:4, :, t0:t0 + 512].rearrange("c p t -> p c t"), YO.a, [YO.b], [self.CATAb[s][c] for c in range(4)])

    def dsa(self, layer, s):
        self.phase()
        QRs = self.alloc([128, 4, SEQ], BF16)
        KK = self.alloc([128, SEQ], BF16)
        KI2 = self.alloc([128, SEQ], BF16)
        QIs = self.alloc([128, 2, SEQ], BF16)
        Vs = self.alloc([128, NT, 64], BF16)
        WIs = self.alloc([128, NT, 4], F32)
        for c in range(4):
            self.dma(QRs.a[:, c, :], self.QR[s, c], self.QRb[s], [QRs.b])
        for hh in range(2):
            self.dma(KK.a[hh * 64:(hh + 1) * 64, :], self.KR[s, 0], self.KRb[s], [KK.b])
            self.dma(KI2.a[hh * 64:(hh + 1) * 64, :], self.KR[s, 1], self.KRb[s], [KI2.b])
        for c in range(2):
            self.dma(QIs.a[:, c, :], self.QI[s, c], self.QIb[s], [QIs.b])
        self.dma(Vs.a, self.V[s], self.Vb[s], [Vs.b])
        self.dma(WIs.a, self.WI[s], self.WIb[s], [WIs.b])
        score = self.ring(2, [128, SEQ], F32)
        mb = self.ring(4, [128, SEQ], BF16)
        junkD = self.alloc([128, SEQ], BF16)
        junkA = self.alloc([128, SEQ], BF16)
        rl = self.ring(3, [128, 256], F32)
        sm = self.ring(4, [128, 8 + NIT + 2], F32)
        pT = self.ring(2, [128, 1024], BF16)
        gb = self.ring(2, [64, 8, 128], BF16)
        rs = self.ring(2, [64, 1024], F32)
        yb = self.ring(2, [64, 8, 128], BF16)
        (tA, bA), (tO, bO), (tS, bS), (tI, bI) = self.PS
        nqb = int(os.environ.get('KQB', NT))

        def stageA1(qb):
            q0 = qb * 128
            L = q0 + 128
            SC, SM = score[qb % 2], sm[qb % 4]
            for c in range((L + 255) // 256):
                k0 = c * 256
                n = min(256, L - k0)
                col = lambda h: (h % 2) * 512 + (h // 2) * 256
                for h in range(4):
                    pr = slice((h % 2) * 64, (h % 2) * 64 + 64)
                    self.mm(tI[:, col(h):col(h) + n], QIs.a[pr, h // 2, q0:q0 + 128], KI2.a[pr, k0:k0 + n], True, True,
                            [QIs.b, KI2.b], [bI[h % 2]])
                dst = SC.a[:, k0:k0 + n]
                self.ts("dve", dst, tI[:, 0:n], 0.0, WIs.a[:, qb, 0:1], ALU.max, ALU.mult, [bI[0], WIs.b], [SC.b])
                for h in range(1, 4):
                    r = rl[h - 1]
                    self.act(r.a[:, 0:n], tI[:, col(h):col(h) + n], AF.Relu, [bI[h % 2]], [r.b])
                    self.stt(dst, r.a[:, 0:n], WIs.a[:, qb, h:h + 1], dst, ALU.mult, ALU.add, [r.b, WIs.b, SC.b], [SC.b])
            if qb >= 2:
                self.S.op("dve", lambda e: e.tensor_reduce(out=SM.a[:, 0:1], in_=SC.a[:, 0:L], axis=AX.X, op=ALU.max), [SC.b], [SM.b])
                self.S.op("dve", lambda e: e.tensor_reduce(out=SM.a[:, 1:2], in_=SC.a[:, 0:L], axis=AX.X, op=ALU.min), [SC.b], [SM.b])
            self.tt("dve", SC.a[:, q0:L], SC.a[:, q0:L], self.negtri.a[:], ALU.add, [SC.b, self.negtri.b], [SC.b])
            if qb >= 2:
                hi, lo, W0, mid = (SM.a[:, i:i + 1] for i in range(4))
                self.tt("dve", W0, hi, lo, ALU.subtract, [SM.b], [SM.b])
                self.ts("dve", SM.a[:, 8:8 + NIT + 2], self.pow2.a[:], W0, None, ALU.mult, None, [self.pow2.b, SM.b], [SM.b])
                self.tt("dve", mid, lo, SM.a[:, 9:10], ALU.add, [SM.b], [SM.b])
            else:
                self.S.op("dve", lambda e: e.memset(SM.a[:, 6:7], -1e29), (), [SM.b])

        def bisect_dve(qb):
            L = qb * 128 + 128
            SC, SM = score[qb % 2], sm[qb % 4]
            mid, cnt, tt_, thr = SM.a[:, 3:4], SM.a[:, 4:5], SM.a[:, 5:6], SM.a[:, 6:7]
            for k in range(1, NIT + 1):
                self.ts("dve", junkD.a[:, 0:L], SC.a[:, 0:L], mid, None, ALU.is_ge, ALU.add, [SC.b, SM.b], [junkD.b, SM.b], accum=cnt)
                if k < NIT:
                    self.ts("dve", tt_, cnt, 256.0, 0.5, ALU.is_ge, ALU.subtract, [SM.b], [SM.b])
                    self.stt(mid, tt_, SM.a[:, 8 + k:9 + k], mid, ALU.mult, ALU.add, [SM.b], [SM.b])
                else:
                    self.ts("dve", tt_, cnt, 256.0, 1.0, ALU.is_ge, ALU.subtract, [SM.b], [SM.b])
                    self.stt(thr, tt_, SM.a[:, 8 + k:9 + k], mid, ALU.mult, ALU.add, [SM.b], [SM.b])

        def stageA3(qb):
            L = qb * 128 + 128
            SC, SM, MB = score[qb % 2], sm[qb % 4], mb[qb % 4]
            self.ts("dve", MB.a[:, 0:L], SC.a[:, 0:L], SM.a[:, 6:7], NEG, ALU.is_lt, ALU.mult, [SC.b, SM.b], [MB.b])

        osb = self.ring(2, [64, 1024], F32)

        def stageB_steps(qb):
            q0 = qb * 128
            MB = mb[qb % 4]
            G, RS, YB, OS = (r[qb % 2] for r in (gb, rs, yb, osb))
            nsc = qb + 1
            steps = []

            def chunk(sc):
                if sc == 0:
                    self.dma(G.a, self.GB[s, :, :, q0:q0 + 128].rearrange("h d t -> d h t"), self.GBb[s], [G.b])
                P = pT[sc % 2]
                for par in range(2):
                    pr = slice(par * 64, par * 64 + 64)
                    o = tA[:, par * 512:(par + 1) * 512]
                    self.mm(o.rearrange("p (a b) -> p a b", a=4), KK.a[pr, sc * 128:(sc + 1) * 128], QRs.a[pr, :, q0:q0 + 128], True, False, [KK.b, QRs.b], [bA[par]])
                    self.mm(o, MB.a[:, sc * 128:(sc + 1) * 128], self.ident4.a[:], False, True, [MB.b, self.ident4.b], [bA[par]])
                self.act(P.a, tA[:, :], AF.Exp, [bA[0], bA[1]], [P.b], scale=0.125)
                for par in range(2):
                    cs = slice(par * 512, (par + 1) * 512)
                    self.mm(tO[0:64, cs], Vs.a[:, sc, :], P.a[:, cs], sc == 0, sc == nsc - 1, [Vs.b, P.b], [bO[par]])
                    self.mm(tS[0:64, cs], self.ones.a[:, 0:64], P.a[:, cs], sc == 0, sc == nsc - 1, [self.ones.b, P.b], [bS[par]])
                if sc == nsc - 1:
                    self.act(RS.a, tS[0:64, :], AF.Ln, [bS[0], bS[1]], [RS.b])
                    self.act(RS.a, RS.a, AF.Exp, [RS.b], [RS.b], scale=-1.0)
                    self.copy("act", OS.a, tO[0:64, :], [bO[0], bO[1]], [OS.b])
                    self.tt("pool", OS.a, OS.a, RS.a, ALU.mult, [OS.b, RS.b], [OS.b])
                    ybv = YB.a.rearrange("d (pair par) t -> d par pair t", par=2)
                    gv = G.a.rearrange("d (pair par) t -> d par pair t", par=2)
                    for par in range(2):
                        self.tt("pool", ybv[:, par], OS.a[:, par * 512:(par + 1) * 512].rearrange("d (pair t) -> d pair t", pair=4), gv[:, par],
                                ALU.mult, [OS.b, G.b], [YB.b])
                    self.dma(self.CATB[s, :, :, q0:q0 + 128].rearrange("h d t -> d h t"), YB.a, [YB.b], [self.CATBb[s][qb]])
            for sc in range(nsc):
                steps.append(lambda sc=sc: chunk(sc))
            return steps

        def bisect_act_steps(qb):
            L = qb * 128 + 128
            SC, SM = score[qb % 2], sm[qb % 4]
            mid, sg, g, thr = SM.a[:, 3:4], SM.a[:, 4:5], SM.a[:, 5:6], SM.a[:, 6:7]

            def it(k):
                self.S.op("act", lambda e: e.activation(out=junkA.a[:, 0:L], in_=SC.a[:, 0:L], func=AF.Sign, bias=mid, scale=-1.0, accum_out=sg),
                          [SC.b, SM.b], [junkA.b, SM.b])
                self.act(g, sg, AF.Sign, [SM.b], [SM.b], bias=float(L - 511), scale=-1.0)
                self.act(mid, g, AF.Identity, [SM.b], [SM.b], bias=mid, scale=SM.a[:, 9 + k:10 + k])
                if k == NIT:
                    self.act(thr, SM.a[:, 9 + NIT:10 + NIT], AF.Identity, [SM.b], [SM.b], bias=mid, scale=-1.0)
            return [lambda k=k: it(k) for k in range(1, NIT + 1)]

        def interleave(chunks, iters):
            nC, nI = len(chunks), len(iters)
            head = (nC * 45 + 99) // 100 if nI else nC
            for c in chunks[:head]:
                c()
            rest = chunks[head:]
            ci = ii = 0
            while ci < len(rest) or ii < nI:
                if ii < nI and (ci >= len(rest) or ii * max(len(rest), 1) <= ci * nI):
                    iters[ii]()
                    ii += 1
                else:
                    rest[ci]()
                    ci += 1

        npair = nqb // 2
        for r in range(npair + 1):
            iters = []
            if r < npair:
                a2, b2 = 2 * r, 2 * r + 1
                stageA1(a2)
                stageA1(b2)
                if a2 >= 2:
                    bisect_dve(a2)
                    iters = bisect_act_steps(b2)
            chunks = []
            if r >= 1:
                chunks = stageB_steps(2 * r - 2) + stageB_steps(2 * r - 1)
            interleave(chunks, iters)
            if r < npair:
                stageA3(2 * r)
                stageA3(2 * r + 1)

    def mem_attn(self, layer, s):
        self.phase()
        ml = self.ring(2, [128, D], F32)
        mbf = self.ring(2, [128, D], BF16)
        memT = self.alloc([128, 8, MEM], BF16)
        for i in range(2):
            a, b = ml[i], mbf[i]
            self.dma(a.a, self.mem_in[s * MEM + i * 128:s * MEM + (i + 1) * 128, :], (), [a.b])
            self.copy("act", b.a, a.a, [a.b], [b.b])
            ps = self.psum()
            pv = ps.a.bitcast(BF16)
            for k in range(8):
                self.S.op("pe", lambda e, k=k, pv=pv, b=b: e.transpose(pv[:, k * 128:(k + 1) * 128], b.a[:, k * 128:(k + 1) * 128], self.identb.a[:]),
                          [b.b, self.identb.b], [ps.b])
            self.copy("dve", memT.a[:, :, i * 128:(i + 1) * 128], pv.rearrange("p (k t) -> p k t", k=8), [ps.b], [memT.b])
        w32 = self.alloc([128, 8 * 512], F32)
        wk = self.alloc([128, 8 * 512], BF16)
        wv = self.alloc([128, 8 * 512], BF16)
        self.dma(w32.a, self.m_wk[layer], (), [w32.b])
        self.copy("pool", wk.a, w32.a, [w32.b], [wk.b])
        self.dma(w32.a, self.m_wv[layer], [], [w32.b])
        self.copy("pool", wv.a, w32.a, [w32.b], [wv.b])
        kmT = self.alloc([128, 4, MEM], BF16)
        vm = self.alloc([128, 2, 512], BF16)
        for h in range(4):
            ps = self.psum()
            for k in range(8):
                self.mm(ps.a[:, 0:MEM], wk.a[:, k * 512 + h * 128:k * 512 + (h + 1) * 128], memT.a[:, k, :], k == 0, k == 7, [wk.b, memT.b], [ps.b])
            self.copy("act", kmT.a[:, h, :], ps.a[:, 0:MEM], [ps.b], [kmT.b])
        for mc in range(2):
            ps = self.psum()
            for k in range(8):
                self.mm(ps.a, memT.a[:, k, mc * 128:(mc + 1) * 128], wv.a[:, k * 512:(k + 1) * 512], k == 0, k == 7, [wv.b, memT.b], [ps.b])
            self.copy("act", vm.a[:, mc, :], ps.a, [ps.b], [vm.b])
        mq = self.ring(2, [128, 4, 512], BF16)
        gm = self.ring(2, [128, 4, 512], BF16)
        pT = self.ring(2, [128, 2, 512], BF16)
        rs = self.ring(2, [128, 512], F32)
        yo = self.ring(2, [128, 4, 512], BF16)
        scale = 128 ** -0.5
        for tc in range(8):
            t0 = tc * 512
            Q, G, YO = mq[tc % 2], gm[tc % 2], yo[tc % 2]
            self.dma(Q.a, self.MQ[s, :, :, t0:t0 + 512].rearrange("c p t -> p c t"), self.MQb[s], [Q.b])
            self.dma(G.a, self.GM[s, :, :, t0:t0 + 512].rearrange("c p t -> p c t"), self.GMb[s], [G.b])
            for h in range(4):
                P, R = pT[h % 2], rs[h % 2]
                for mc in range(2):
                    ps = self.psum()
                    self.mm(ps.a, kmT.a[:, h, mc * 128:(mc + 1) * 128], Q.a[:, h, :], True, True, [kmT.b, Q.b], [ps.b])
                    self.act(P.a[:, mc, :], ps.a, AF.Exp, [ps.b], [P.b], scale=scale)
                po, pS = self.psum(), self.psum()
                for mc in range(2):
                    self.mm(po.a, vm.a[:, mc, h * 128:(h + 1) * 128], P.a[:, mc, :], mc == 0, mc == 1, [vm.b, P.b], [po.b])
                for mc in range(2):
                    self.mm(pS.a, self.ones.a[:], P.a[:, mc, :], mc == 0, mc == 1, [self.ones.b, P.b], [pS.b])
                self.act(R.a, pS.a, AF.Ln, [pS.b], [R.b])
                self.act(R.a, R.a, AF.Exp, [R.b], [R.b], scale=-1.0)
                self.tt("dve", R.a, po.a, R.a, ALU.mult, [po.b, R.b], [R.b])
                self.tt("pool", YO.a[:, h, :], R.a, G.a[:, h, :], ALU.mult, [R.b, G.b], [YO.b])
            self.dma(self.CATM[s, :, :, t0:t0 + 512].rearrange("c p t -> p c t"), YO.a, [YO.b], self.CATMb[s])

    def inproj_odd(self, layer, s):
        j = layer // 2
        for half in range(2):
            self.phase()
            t0 = half * 2048
            xT = self.alloc([128, 8, 2048], BF16)
            self.load_xT(layer, s, half, xT)
            wl = self.ring(2, [128, 1024], F32)
            wb = self.ring(2, [128, 1024], BF16)
            stage = self.ring(3, [128, 2048], BF16)
            sl = lambda tc: slice(tc * 512, (tc + 1) * 512)
            specs = [(i, AF.Gelu_apprx_tanh, self.U, self.Ub, i) for i in range(8)]
            specs += [(8 + i, AF.Silu, self.GC, self.GCb, i) for i in range(8)]
            specs += [(16 + i, AF.Copy, self.MQ, self.MQb, i) for i in range(4)]
            specs += [(20 + i, AF.Silu, self.GM, self.GMb, i) for i in range(4)]
            for ci, (idx, fn, dst, dstb, di) in enumerate(specs):
                sg = stage[ci % 3]
                for tc, ps in enumerate(self.fm_chunk(self.o_wfm[j, idx], xT, wl, wb, ci)):
                    self.act(sg.a[:, sl(tc)], ps.a, fn, [ps.b], [sg.b])
                self.dma(dst[s, di, :, t0:t0 + 2048], sg.a, [sg.b], [dstb[s][di]])
            vg = self.alloc([128, 2048], F32)
            self.dma(vg.a[:, 0:1024], self.o_vln[j, 0:1, :].partition_broadcast(128), (), [vg.b])
            self.dma(vg.a[:, 1024:2048], self.o_vln[j, 1:2, :].partition_broadcast(128), (), [vg.b])
            v32 = self.ring(2, [128, 1024], F32)
            vo = self.ring(2, [128, 1024], BF16)
            stt_ = self.ring(2, [128, 16], F32)
            w32 = self.alloc([128, 8, 512], F32)
            wts = [self.alloc([128, 8, 512], BF16), self.alloc([128, 8, 512], BF16)]
            for hh in range(2):
                self.dma(w32.a, self.o_wtm[j].rearrange("p (k n) -> p k n", k=8)[:, :, hh * 512:(hh + 1) * 512], (), [w32.b])
                self.copy("pool", wts[hh].a, w32.a, [w32.b], [wts[hh].b])
            for i in range(16):
                ti = half * 16 + i
                Vt, VO, SS = v32[i % 2], vo[i % 2], stt_[i % 2]
                for hh in range(2):
                    ps = self.psum()
                    for k in range(8):
                        self.mm(ps.a, xT.a[:, k, i * 128:(i + 1) * 128], wts[hh].a[:, k, :], k == 0, k == 7, [xT.b, wts[hh].b], [ps.b])
                    self.act(Vt.a[:, hh * 512:(hh + 1) * 512], ps.a, AF.Gelu_apprx_tanh, [ps.b], [Vt.b])
                self.layernorm(Vt, SS, vg, 0, VO.a, VO.b)
                row = ti * 128
                self.dma(self.VLN[s, row:row + 128, :], VO.a, [VO.b], [self.VLNb[s][ti]])

    def layernorm(self, Z, SS, gb, goff, out, outb):
        for hh in range(2):
            self.S.op("dve", lambda e, hh=hh: e.bn_stats(out=SS.a[:, hh * 6:(hh + 1) * 6], in_=Z.a[:, hh * 512:(hh + 1) * 512]), [Z.b], [SS.b])
        self.S.op("dve", lambda e: e.bn_aggr(out=SS.a[:, 12:14], in_=SS.a[:, 0:12]), [SS.b], [SS.b])
        self.act(SS.a[:, 14:15], SS.a[:, 13:14], AF.Ln, [SS.b, self.epsc.b], [SS.b], bias=self.epsc.a[:, 0:1])
        self.act(SS.a[:, 14:15], SS.a[:, 14:15], AF.Exp, [SS.b], [SS.b], scale=-0.5)
        self.ts("dve", Z.a, Z.a, SS.a[:, 12:13], SS.a[:, 14:15], ALU.subtract, ALU.mult, [Z.b, SS.b], [Z.b])
        self.tt("pool", Z.a, Z.a, gb.a[:, goff:goff + 1024], ALU.mult, [Z.b, gb.b], [Z.b])
        self.tt("pool", out, Z.a, gb.a[:, goff + 1024:goff + 2048], ALU.add, [Z.b, gb.b], [outb])

    def sgu(self, layer, s):
        j = layer // 2
        self.phase()
        w32 = self.alloc([128, 8, 128], F32)
        wsb = self.alloc([128, 8, 128], BF16)
        self.dma(w32.a, self.o_wsT[j].rearrange("p (g t) -> p g t", g=8), (), [w32.b])
        for g in range(8):
            self.tt("dve", wsb.a[:, g, :], w32.a[:, g, :], self.tril.a[:], ALU.mult, [w32.b, self.tril.b], [wsb.b])
        bsb = self.alloc([128, 8, 128], F32)
        self.dma(bsb.a.rearrange("p g t -> p (g t)"), self.o_bs[j].partition_broadcast(128), (), [bsb.b])
        vt = self.ring(2, [128, 4, 1024], BF16)
        uu = self.ring(2, [128, 8, 512], BF16)
        gc = self.ring(2, [128, 8, 512], BF16)
        tm = self.ring(2, [128, 512], F32)
        yo = self.ring(2, [128, 8, 512], BF16)
        for tc in range(8):
            t0 = tc * 512
            Vt, Uu, Gc, YO = vt[tc % 2], uu[tc % 2], gc[tc % 2], yo[tc % 2]
            self.dma(Vt.a, self.VLN[s, t0:t0 + 512, :].rearrange("(c p) n -> p c n", p=128), self.VLNb[s][tc * 4:tc * 4 + 4], [Vt.b])
            self.dma(Uu.a, self.U[s, :, :, t0:t0 + 512].rearrange("c p t -> p c t"), self.Ub[s], [Uu.b])
            self.dma(Gc.a, self.GC[s, :, :, t0:t0 + 512].rearrange("c p t -> p c t"), self.GCb[s], [Gc.b])
            for g in range(8):
                ps = self.psum()
                for c in range(4):
                    self.mm(ps.a[:, c * 128:(c + 1) * 128], Vt.a[:, c, g * 128:(g + 1) * 128], wsb.a[:, g, :], True, True, [Vt.b, wsb.b], [ps.b])
                Tm = tm[g % 2]
                self.tt("dve", Tm.a.rearrange("p (c t) -> p c t", c=4), ps.a.rearrange("p (c t) -> p c t", c=4),
                        bsb.a[:, g:g + 1, :].to_broadcast([128, 4, 128]), ALU.add, [ps.b, bsb.b], [Tm.b])
                self.tt("pool", Tm.a, Tm.a, Uu.a[:, g, :], ALU.mult, [Tm.b, Uu.b], [Tm.b])
                self.tt("pool", YO.a[:, g, :], Tm.a, Gc.a[:, g, :], ALU.mult, [Tm.b, Gc.b], [YO.b])
            self.dma(self.CATA[s, :, :, t0:t0 + 512].rearrange("c p t -> p c t"), YO.a, [YO.b], self.CATAb[s])

    def outproj(self, layer, s):
        j = layer // 2
        even = layer % 2 == 0
        self.phase()
        last = layer == self.layers - 1
        xsrc = self.x_in if layer == 0 else self.X
        xdst = self.y_out if last else self.X
        gb = self.alloc([128, 2048], F32)
        self.dma(gb.a[:, 0:1024], self.ln_gb[layer, 0:1, :].partition_broadcast(128), (), [gb.b])
        self.dma(gb.a[:, 1024:2048], self.ln_gb[layer, 1:2, :].partition_broadcast(128), (), [gb.b])
        w32 = self.alloc([128, 4 * 1024], F32)
        wo = self.alloc([128, 12, 1024], BF16)
        if even:
            srcs = [(self.e_woA[j], 0, 128), (self.e_woM[j], 8, 128)]
            for (src, c0, np_) in srcs:
                self.dma(w32.a, src, (), [w32.b])
                self.copy("pool", wo.a[:, c0:c0 + 4, :], w32.a.rearrange("p (c n) -> p c n", c=4), [w32.b], [wo.b])
        woB = None
        if even:
            woB = self.alloc([64, 8, 1024], BF16)
            for hh in range(2):
                self.dma(w32.a[0:64, :], self.e_woB[j, :, hh * 4096:(hh + 1) * 4096], (), [w32.b])
                self.copy("pool", woB.a[:, hh * 4:(hh + 1) * 4, :], w32.a[0:64, :].rearrange("p (c n) -> p c n", c=4), [w32.b], [woB.b])
        else:
            for c0 in range(0, 12, 4):
                self.dma(w32.a, self.o_wo[j, :, c0 * 1024:(c0 + 4) * 1024], (), [w32.b])
                self.copy("pool", wo.a[:, c0:c0 + 4, :], w32.a.rearrange("p (c n) -> p c n", c=4), [w32.b], [wo.b])
        ca = self.ring(2, [128, 12, 512], BF16)
        cb = self.ring(2, [64, 8, 512], BF16)
        xt = self.ring(2, [128, D], F32)
        zz = self.ring(2, [128, D], F32)
        xo = self.ring(2, [128, D], F32)
        ss = self.ring(2, [128, 16], F32)
        for tc in range(8):
            t0 = tc * 512
            CA, CB = ca[tc % 2], cb[tc % 2]
            if even:
                self.dma(CA.a[:, 0:4, :], self.CATA[s, 0:4, :, t0:t0 + 512].rearrange("c p t -> p c t"), self.CATAb[s][0:4], [CA.b])
                self.dma(CB.a, self.CATB[s, :, :, t0:t0 + 512].rearrange("h d t -> d h t"), self.CATBb[s][tc * 4:tc * 4 + 4], [CB.b])
            else:
                self.dma(CA.a[:, 0:8, :], self.CATA[s, :, :, t0:t0 + 512].rearrange("c p t -> p c t"), self.CATAb[s], [CA.b])
            self.dma(CA.a[:, 8:12, :], self.CATM[s, :, :, t0:t0 + 512].rearrange("c p t -> p c t"), self.CATMb[s], [CA.b])
            for i in range(4):
                ti = tc * 4 + i
                row = s * SEQ + ti * 128
                Xt, Z, XO, SS = (r[ti % 2] for r in (xt, zz, xo, ss))
                self.dma(Xt.a, xsrc[row:row + 128, :], [self.Xb[s][ti]], [Xt.b])
                tsl = slice(i * 128, (i + 1) * 128)
                for nh in range(2):
                    ps = self.psum()
                    ns = slice(nh * 512, (nh + 1) * 512)
                    ops = []
                    for c in ([0, 1, 2, 3, 8, 9, 10, 11] if even else range(12)):
                        ops.append((CA.a[:, c, tsl], wo.a[:, c, ns], [CA.b, wo.b]))
                    if even:
                        for h in range(8):
                            ops.append((CB.a[:, h, tsl], woB.a[:, h, ns], [CB.b, woB.b]))
                    for oi, (l, r, rd) in enumerate(ops):
                        self.mm(ps.a, l, r, oi == 0, oi == len(ops) - 1, rd, [ps.b])
                    self.stt(Z.a[:, ns], Xt.a[:, ns], float(DN_ALPHA), ps.a, ALU.mult, ALU.add, [Xt.b, ps.b], [Z.b])
                self.layernorm(Z, SS, gb, 0, XO.a, XO.b)
                self.dma(xdst[row:row + 128, :], XO.a, [XO.b], [self.Xb[s][ti]])

    def build(self, only=None):
        on = lambda n: only is None or n in only
        for s in range(self.nseq):
            if on("rope"):
                self.rope_tables(s)
        for layer in range(self.layers):
            for s in range(self.nseq):
                if layer % 2 == 0:
                    if on("inproj"):
                        self.inproj_even(layer, s)
                    if on("conv"):
                        self.conv_branch(layer, s)
                    if on("dsa"):
                        self.dsa(layer, s)
                else:
                    if on("inproj"):
                        self.inproj_odd(layer, s)
                    if on("sgu"):
                        self.sgu(layer, s)
                if on("mem"):
                    self.mem_attn(layer, s)
                if on("out"):
                    self.outproj(layer, s)
        self.S.barrier()
        self.S.emit()
        return self.nc


def _fm(W, cols):
    w = W[:, cols]
    return np.ascontiguousarray(w.reshape(8, 128, 128).transpose(1, 0, 2).reshape(128, 8 * 128))


def _kmajor(W):
    K, N = W.shape
    return np.ascontiguousarray(W.reshape(K // 128, 128, N).transpose(1, 0, 2).reshape(128, (K // 128) * N))


def host_consts():
    inv = (10000.0 ** (-np.arange(0, 64, 2, dtype=np.float32) / np.float32(64))).astype(np.float32)
    invf = np.zeros((128, 2), np.float32)
    for p in range(128):
        invf[p, 0] = inv[p % 32]
        invf[p, 1] = -1.0 if (p % 64) < 32 else 1.0
    t = np.arange(128)
    negtri = np.where(t[None, :] <= t[:, None], 0.0, -1e30).astype(np.float32)
    tril = (t[None, :] >= t[:, None]).astype(np.float32)
    pow2 = np.tile((2.0 ** -np.arange(NIT + 2, dtype=np.float32))[None, :], (128, 1)).astype(np.float32)
    return {"c_ident": np.eye(128, dtype=np.float32), "c_invf": invf, "c_negtri": negtri, "c_tril": tril, "c_pow2": pow2}


def host_weights(e_w_in, e_conv_w, e_conv_b, e_cln_g, e_cln_b, e_pw2_w, e_pw2_b, e_w_out,
                 o_w_in, o_vln_g, o_vln_b, o_ws, o_bs, o_w_out, mem_wk, mem_wv, ln_g, ln_b):
    f = lambda a: np.asarray(a, dtype=np.float32)
    nE, nO = e_w_in.shape[0], o_w_in.shape[0]
    ar = np.arange
    sw = (ar(64) + 32) % 64
    out = {}
    wfm = np.zeros((nE, NCH_E, 128, 1024), np.float32)
    wtm = np.zeros((nE, 128, 8 * 68), np.float32)
    for j in range(nE):
        W = f(e_w_in[j])
        ch = []
        ch += [0 + 128 * i + ar(128) for i in range(4)]
        ch += [512 + 128 * i + ar(128) for i in range(4)]
        ch += [1024 + 128 * i + ar(128) for i in range(4)]
        ch += [1536 + 128 * i + ar(128) for i in range(4)]
        ch += [1536 + 128 * i + np.concatenate([sw, 64 + sw]) for i in range(4)]
        ch += [np.concatenate([2048 + ar(64), 2432 + ar(64)])]
        ch += [np.concatenate([2048 + sw, 2432 + sw])]
        ch += [2176 + 128 * i + ar(128) for i in range(2)]
        ch += [2176 + 128 * i + np.concatenate([sw, 64 + sw]) for i in range(2)]
        ch += [3012 + 128 * i + ar(128) for i in range(4)]
        ch += [3524 + 128 * i + ar(128) for i in range(4)]
        ch += [2500 + 128 * i + ar(128) for i in range(4)]
        assert len(ch) == NCH_E
        for c, cols in enumerate(ch):
            wfm[j, c] = _fm(W, cols)
        wtm[j] = _kmajor(W[:, np.concatenate([2112 + ar(64), 2496 + ar(4)])])
    out["e_wfm"], out["e_wtm"] = wfm, wtm
    cw = f(e_conv_w)
    out["e_convw"] = np.ascontiguousarray(cw.reshape(nE, 31, 4, 128).transpose(0, 3, 2, 1).reshape(nE, 128, 4 * 31))
    pc = lambda v: f(v).reshape(nE, 4, 128).transpose(0, 2, 1)
    out["e_vec"] = np.ascontiguousarray(np.concatenate([pc(e_conv_b), pc(e_cln_g), pc(e_cln_b), pc(e_pw2_b)], axis=2))
    out["e_pw2"] = np.stack([_kmajor(f(e_pw2_w[j])) for j in range(nE)])
    wo = f(e_w_out)
    out["e_woA"] = np.stack([_kmajor(wo[j, 0:512]) for j in range(nE)])
    out["e_woB"] = np.stack([np.ascontiguousarray(wo[j, 512:1024].reshape(8, 64, 1024).transpose(1, 0, 2).reshape(64, 8 * 1024)) for j in range(nE)])
    out["e_woM"] = np.stack([_kmajor(wo[j, 1024:1536]) for j in range(nE)])
    ofm = np.zeros((nO, NCH_O, 128, 1024), np.float32)
    otm = np.zeros((nO, 128, 8 * 1024), np.float32)
    for j in range(nO):
        W = f(o_w_in[j])
        ch = [0 + 128 * i + ar(128) for i in range(8)] + [2048 + 128 * i + ar(128) for i in range(8)]
        ch += [3072 + 128 * i + ar(128) for i in range(4)] + [3584 + 128 * i + ar(128) for i in range(4)]
        for c, cols in enumerate(ch):
            ofm[j, c] = _fm(W, cols)
        otm[j] = _kmajor(W[:, 1024:2048])
    out["o_wfm"], out["o_wtm"] = ofm, otm
    out["o_vln"] = np.ascontiguousarray(np.stack([f(o_vln_g), f(o_vln_b)], axis=1))
    out["o_wsT"] = np.ascontiguousarray(f(o_ws).transpose(0, 3, 1, 2).reshape(nO, 128, 8 * 128))
    out["o_bs"] = np.ascontiguousarray(f(o_bs).reshape(nO, 1, 8 * 128))
    out["o_wo"] = np.stack([_kmajor(f(o_w_out[j])) for j in range(nO)])
    out["m_wk"] = np.stack([_kmajor(f(mem_wk[l])) for l in range(DEPTH)])
    out["m_wv"] = np.stack([_kmajor(f(mem_wv[l])) for l in range(DEPTH)])
    out["ln_gb"] = np.ascontiguousarray(np.stack([f(ln_g), f(ln_b)], axis=1))
    return out


_PROG_CACHE = {}


def kernel(x, mem, positions, e_w_in, e_conv_w, e_conv_b, e_cln_g, e_cln_b, e_pw2_w, e_pw2_b, e_w_out,
           o_w_in, o_vln_g, o_vln_b, o_ws, o_bs, o_w_out, mem_wk, mem_wv, ln_g, ln_b):
    x = np.asarray(x, dtype=np.float32)
    mem = np.asarray(mem, dtype=np.float32)
    positions = np.asarray(positions, dtype=np.int32)
    B = x.shape[0]
    nseq = B // NCORES
    shared = host_consts()
    shared.update(host_weights(e_w_in, e_conv_w, e_conv_b, e_cln_g, e_cln_b, e_pw2_w, e_pw2_b, e_w_out,
                               o_w_in, o_vln_g, o_vln_b, o_ws, o_bs, o_w_out, mem_wk, mem_wv, ln_g, ln_b))
    if "p" not in _PROG_CACHE:
        _PROG_CACHE["p"] = Prog(nseq, DEPTH).build()
    nc = _PROG_CACHE["p"]
    in_maps = []
    for c in range(NCORES):
        d = dict(shared)
        d["x"] = np.ascontiguousarray(x[c * nseq:(c + 1) * nseq].reshape(nseq * SEQ, D))
        d["mem"] = np.ascontiguousarray(mem[c * nseq:(c + 1) * nseq].reshape(nseq * MEM, D))
        d["pos"] = np.ascontiguousarray(positions[c * nseq:(c + 1) * nseq])
        in_maps.append(d)
    res = run_bass_kernel_spmd(nc, in_maps, core_ids=list(range(NCORES)))
    out = np.concatenate([r["y"].reshape(nseq, SEQ, D) for r in res.results], axis=0)
    return out.astype(np.float32)
```

```python
import os
import numpy as np
import concourse.bass as bass
import concourse.mybir as mybir
from concourse.bass_utils import run_bass_kernel_spmd

F32 = mybir.dt.float32
BF16 = mybir.dt.bfloat16
I32 = mybir.dt.int32
AF = mybir.ActivationFunctionType
ALU = mybir.AluOpType
AX = mybir.AxisListType

NCORES = 8
D = 1024
SEQ = 4096
MEM = 256
DEPTH = 4
NT = SEQ // 128
LN_EPS = 1e-5
DN_ALPHA = (2 * DEPTH) ** 0.25
NIT = 22
NEG = -30000.0
SEM_EPOCH = 30000
DMA_K = 16
NCH_E = 38
NCH_O = 24


class Buf:
    __slots__ = ("name", "lw", "rd", "excl")

    def __init__(self, name="", excl=False):
        self.name = name
        self.lw = {}
        self.rd = {}
        self.excl = excl


class Stream:
    def __init__(self, S, name, eng, inc, k):
        self.S, self.name, self.eng, self.inc, self.k = S, name, eng, inc, k
        self.i = 0
        self.vals = [0] * k
        self.epochs = [0] * k
        self.sems = {}
        for j in range(k):
            self.sems[(name, j, 0)] = S.nc.alloc_semaphore(name=f"s_{name}_{j}_0")

    def next(self):
        j = self.i % self.k
        self.i += 1
        if self.vals[j] + self.inc > SEM_EPOCH:
            self.epochs[j] += 1
            self.vals[j] = 0
            self.sems[(self.name, j, self.epochs[j])] = self.S.nc.alloc_semaphore(
                name=f"s_{self.name}_{j}_{self.epochs[j]}")
        self.vals[j] += self.inc
        return ((self.name, j, self.epochs[j]), self.vals[j])

    def current(self):
        return [((self.name, j, self.epochs[j]), self.vals[j]) for j in range(self.k) if self.vals[j] > 0]


class Sched:
    def __init__(self, nc):
        self.nc = nc
        self.lists = {e: [] for e in ("pe", "dve", "act", "pool", "sp")}
        self.seen = {e: {} for e in self.lists}
        self.pending = {e: {} for e in self.lists}
        self.streams = {}
        for nm, eng, inc, k in (("pe", "pe", 1, 1), ("dve", "dve", 1, 1), ("act", "act", 1, 1),
                                ("pool", "pool", 1, 1), ("spd", "sp", 16, DMA_K), ("actd", "act", 16, DMA_K)):
            self.streams[nm] = Stream(self, nm, eng, inc, k)
        self.n = 0

    def semh(self, key):
        return self.streams[key[0]].sems[key]

    def op(self, stream, fn, reads=(), writes=()):
        st = self.streams[stream]
        eng = st.eng
        if any(b.excl for b in reads):
            writes = list(writes) + [b for b in reads if b.excl and b not in writes]
            reads = [b for b in reads if not b.excl]
        deps = dict(self.pending[eng])
        self.pending[eng] = {}
        if st.k > 1:
            jj = st.i % st.k
            if st.vals[jj] > 0:
                deps[(st.name, jj, st.epochs[jj])] = st.vals[jj]

        def add(d):
            for k, v in d.items():
                if deps.get(k, 0) < v:
                    deps[k] = v
        for b in reads:
            add(b.lw)
        for b in writes:
            add(b.lw)
            add(b.rd)
        seen = self.seen[eng]
        waits = []
        for k, v in deps.items():
            if stream == "pe" and k[0] == "pe":
                continue
            if seen.get(k, 0) >= v:
                continue
            seen[k] = v
            waits.append((k, v))
        key, val = st.next()
        self.lists[eng].append((waits, fn, key, st.inc))
        for b in reads:
            if b.rd.get(key, 0) < val:
                b.rd[key] = val
        for b in writes:
            b.lw = {key: val}
            b.rd = {}
        self.n += 1

    def barrier(self):
        cur = {}
        for s in self.streams.values():
            for k, v in s.current():
                cur[k] = v
        for e in self.lists:
            p = self.pending[e]
            for k, v in cur.items():
                if p.get(k, 0) < v:
                    p[k] = v

    def emit(self):
        nc = self.nc
        engmap = {"pe": "tensor", "dve": "vector", "act": "scalar", "pool": "gpsimd", "sp": "sync"}
        fin = []
        for s in self.streams.values():
            fin += s.current()
        with nc.Block() as block:
            for e, lst in self.lists.items():
                def body(engine, lst=lst, e=e):
                    for waits, fn, key, inc in lst:
                        for (wk, wv) in waits:
                            engine.wait_ge(self.semh(wk), wv)
                        fn(engine).then_inc(self.semh(key), inc)
                    if e == "sp":
                        for (wk, wv) in fin:
                            engine.wait_ge(self.semh(wk), wv)
                getattr(block, engmap[e])(body)


class T:
    __slots__ = ("a", "b")

    def __init__(self, a, b):
        self.a, self.b = a, b


class Prog:
    def __init__(self, nseq, layers, debug=False):
        self.nseq, self.layers, self.debug = nseq, layers, debug
        self.nc = nc = bass.Bass("TRN2", target_bir_lowering=False)
        self.S = Sched(nc)
        self.dbg_names = []
        ntok = nseq * SEQ
        ein = lambda n, sh, dt=F32: nc.dram_tensor(n, list(sh), dt, kind="ExternalInput").ap()
        self.x_in = ein("x", [ntok, D])
        self.mem_in = ein("mem", [nseq * MEM, D])
        self.pos_in = ein("pos", [nseq, SEQ], I32)
        self.c_ident = ein("c_ident", [128, 128])
        self.c_invf = ein("c_invf", [128, 2])
        self.c_negtri = ein("c_negtri", [128, 128])
        self.c_tril = ein("c_tril", [128, 128])
        self.c_pow2 = ein("c_pow2", [128, NIT + 2])
        nE, nO = (DEPTH + 1) // 2, DEPTH // 2
        self.e_wfm = ein("e_wfm", [nE, NCH_E, 128, 8 * 128])
        self.e_wtm = ein("e_wtm", [nE, 128, 8 * 68])
        self.e_convw = ein("e_convw", [nE, 128, 4 * 31])
        self.e_vec = ein("e_vec", [nE, 128, 16])
        self.e_pw2 = ein("e_pw2", [nE, 128, 4 * 512])
        self.e_woA = ein("e_woA", [nE, 128, 4 * 1024])
        self.e_woB = ein("e_woB", [nE, 64, 8 * 1024])
        self.e_woM = ein("e_woM", [nE, 128, 4 * 1024])
        self.o_wfm = ein("o_wfm", [nO, NCH_O, 128, 8 * 128])
        self.o_wtm = ein("o_wtm", [nO, 128, 8 * 1024])
        self.o_vln = ein("o_vln", [nO, 2, 1024])
        self.o_wsT = ein("o_wsT", [nO, 128, 8 * 128])
        self.o_bs = ein("o_bs", [nO, 1, 8 * 128])
        self.o_wo = ein("o_wo", [nO, 128, 12 * 1024])
        self.m_wk = ein("m_wk", [DEPTH, 128, 8 * 512])
        self.m_wv = ein("m_wv", [DEPTH, 128, 8 * 512])
        self.ln_gb = ein("ln_gb", [DEPTH, 2, 1024])
        self.y_out = nc.dram_tensor("y", [ntok, D], F32, kind="ExternalOutput").ap()
        self.X = self.scr("X", [ntok, D], F32)
        self.Xb = [[Buf() for _ in range(NT)] for _ in range(nseq)]
        mk = lambda n: [[Buf() for _ in range(n)] for _ in range(nseq)]
        self.CT = self.scr("CT", [nseq, 128, SEQ], F32); self.CTb = mk(1)
        self.ST = self.scr("ST", [nseq, 128, SEQ], F32); self.STb = mk(1)
        self.H = self.scr("H", [nseq, 4, 128, SEQ], BF16); self.Hb = mk(4)
        self.GA = self.scr("GA", [nseq, 4, 128, SEQ], BF16); self.GAb = mk(4)
        self.QR = self.scr("QR", [nseq, 4, 128, SEQ], BF16); self.QRb = mk(4)
        self.KR = self.scr("KR", [nseq, 2, 64, SEQ], BF16); self.KRb = mk(1)
        self.QI = self.scr("QI", [nseq, 2, 128, SEQ], BF16); self.QIb = mk(2)
        self.V = self.scr("V", [nseq, 128, NT, 64], BF16); self.Vb = mk(1)
        self.WI = self.scr("WI", [nseq, 128, NT, 4], F32); self.WIb = mk(1)
        self.GB = self.scr("GB", [nseq, 8, 64, SEQ], BF16); self.GBb = mk(4)
        self.MQ = self.scr("MQ", [nseq, 4, 128, SEQ], BF16); self.MQb = mk(4)
        self.GM = self.scr("GM", [nseq, 4, 128, SEQ], BF16); self.GMb = mk(4)
        self.U = self.scr("U", [nseq, 8, 128, SEQ], BF16); self.Ub = mk(8)
        self.GC = self.scr("GC", [nseq, 8, 128, SEQ], BF16); self.GCb = mk(8)
        self.VLN = self.scr("VLN", [nseq, SEQ, D], BF16); self.VLNb = mk(NT)
        self.CATA = self.scr("CATA", [nseq, 8, 128, SEQ], BF16); self.CATAb = mk(8)
        self.CATB = self.scr("CATB", [nseq, 8, 64, SEQ], BF16); self.CATBb = mk(NT)
        self.CATM = self.scr("CATM", [nseq, 4, 128, SEQ], BF16); self.CATMb = mk(4)
        self.PS = []
        for i in range(4):
            t = nc.alloc_psum_tensor(f"ps{i}", [128, 1024], F32)
            self.PS.append((t, [Buf(f"ps{i}a", True), Buf(f"ps{i}b", True)]))
        self.psi = 0
        self.consts()
        self.arena_t = nc.alloc_sbuf_tensor("arena", [128, self.ARENA], BF16)
        self.aoff = 0

    ARENA = 90 * 1024

    def scr(self, name, shape, dt):
        kind = "ExternalOutput" if self.debug else "Internal"
        if self.debug:
            self.dbg_names.append(name)
        return self.nc.dram_tensor("scr_" + name, list(shape), dt, kind=kind).ap()

    def phase(self):
        self.S.barrier()
        self.aoff = 0

    def alloc(self, shape, dt):
        n = int(np.prod(shape[1:]))
        ne = n * (2 if dt == F32 or dt == I32 else 1)
        ne = (ne + 15) // 16 * 16
        assert self.aoff + ne <= self.ARENA, ("arena overflow", self.aoff, ne)
        v = self.arena_t[0:shape[0], self.aoff:self.aoff + ne]
        self.aoff += ne
        if dt != BF16:
            v = v.bitcast(dt)
        v = v[:, 0:n]
        if len(shape) == 3:
            v = v.rearrange("p (a b) -> p a b", a=shape[1])
        return T(v, Buf())

    def ring(self, n, shape, dt):
        return [self.alloc(shape, dt) for _ in range(n)]

    def psum(self):
        i = self.psi % 8
        self.psi += 1
        t, bs = self.PS[i // 2]
        h = i % 2
        return T(t[:, h * 512:(h + 1) * 512], bs[h])

    def dma(self, out, in_, reads=(), writes=(), q=None):
        if q is None:
            q = "actd" if str(out.space) == "DRAM" else "spd"
        self.S.op(q, lambda e: e.dma_start(out=out, in_=in_), reads, writes)

    def mm(self, out, lhsT, rhs, start, stop, reads, writes):
        self.S.op("pe", lambda e: e.matmul(out, lhsT=lhsT, rhs=rhs, start=start, stop=stop), reads, writes)

    def act(self, out, in_, func, reads, writes, bias=0.0, scale=1.0):
        self.S.op("act", lambda e: e.activation(out=out, in_=in_, func=func, bias=bias, scale=scale), reads, writes)

    def ts(self, eng, out, in0, s1, s2, op0, op1, reads, writes, accum=None):
        if accum is None:
            if s2 is None:
                self.S.op(eng, lambda e: e.tensor_scalar(out=out, in0=in0, scalar1=s1, scalar2=None, op0=op0), reads, writes)
            else:
                self.S.op(eng, lambda e: e.tensor_scalar(out=out, in0=in0, scalar1=s1, scalar2=s2, op0=op0, op1=op1), reads, writes)
        else:
            self.S.op(eng, lambda e: e.tensor_scalar(out=out, in0=in0, scalar1=s1, scalar2=s2, op0=op0, op1=op1, accum_out=accum), reads, writes)

    def tt(self, eng, out, in0, in1, op, reads, writes):
        self.S.op(eng, lambda e: e.tensor_tensor(out=out, in0=in0, in1=in1, op=op), reads, writes)

    def stt(self, out, in0, scalar, in1, op0, op1, reads, writes):
        self.S.op("dve", lambda e: e.scalar_tensor_tensor(out=out, in0=in0, scalar=scalar, in1=in1, op0=op0, op1=op1), reads, writes)

    def copy(self, eng, out, in_, reads, writes):
        if eng == "act":
            self.S.op("act", lambda e: e.copy(out=out, in_=in_), reads, writes)
        else:
            self.S.op(eng, lambda e: e.tensor_copy(out=out, in_=in_), reads, writes)

    def consts(self):
        nc = self.nc
        al = lambda n, sh, dt: T(nc.alloc_sbuf_tensor(n, sh, dt), Buf(n))
        self.ident32 = al("ident32", [128, 128], F32)
        self.identb = al("identb", [128, 128], BF16)
        self.ident4 = al("ident4", [128, 512], BF16)
        self.ones = al("ones", [128, 128], BF16)
        self.negtri = al("negtri", [128, 128], F32)
        self.tril = al("tril", [128, 128], F32)
        self.pow2 = al("pow2", [128, NIT + 2], F32)
        self.invf = al("invf", [128, 2], F32)
        self.epsc = al("epsc", [128, 1], F32)
        self.dma(self.ident32.a[:], self.c_ident, writes=[self.ident32.b])
        self.dma(self.negtri.a[:], self.c_negtri, writes=[self.negtri.b])
        self.dma(self.tril.a[:], self.c_tril, writes=[self.tril.b])
        self.dma(self.pow2.a[:], self.c_pow2, writes=[self.pow2.b])
        self.dma(self.invf.a[:], self.c_invf, writes=[self.invf.b])
        self.copy("dve", self.identb.a[:], self.ident32.a[:], [self.ident32.b], [self.identb.b])
        for i in range(4):
            self.copy("dve", self.ident4.a[:, i * 128:(i + 1) * 128], self.ident32.a[:], [self.ident32.b], [self.ident4.b])
        self.S.op("dve", lambda e: e.memset(self.ones.a[:], 1.0), (), [self.ones.b])
        self.S.op("dve", lambda e: e.memset(self.epsc.a[:], LN_EPS), (), [self.epsc.b])

    def rope_tables(self, s):
        self.phase()
        posi = self.alloc([128, SEQ], I32)
        ang = self.alloc([128, SEQ], F32)
        t1 = self.alloc([128, SEQ], F32)
        t2 = self.alloc([128, SEQ], F32)
        self.dma(posi.a, self.pos_in[s:s + 1, :].partition_broadcast(128), writes=[posi.b])
        self.copy("dve", ang.a, posi.a, [posi.b], [ang.b])
        self.ts("dve", ang.a, ang.a, self.invf.a[:, 0:1], None, ALU.mult, None, [ang.b, self.invf.b], [ang.b])
        twopi = float(np.float32(2 * np.pi))
        magic = 12582912.0
        for (shift, dst, dstb, signed) in ((np.pi / 2, self.CT, self.CTb, False), (0.0, self.ST, self.STb, True)):
            self.ts("dve", t1.a, ang.a, float(shift), 1.0 / twopi, ALU.add, ALU.mult, [ang.b], [t1.b])
            self.ts("dve", t1.a, t1.a, magic, None, ALU.add, None, [t1.b], [t1.b])
            self.ts("dve", t1.a, t1.a, magic, None, ALU.subtract, None, [t1.b], [t1.b])
            self.stt(t2.a, t1.a, -twopi, ang.a, ALU.mult, ALU.add, [t1.b, ang.b], [t2.b])
            self.ts("dve", t2.a, t2.a, float(shift), 3.1415925, ALU.add, ALU.min, [t2.b], [t2.b])
            self.ts("dve", t2.a, t2.a, -3.1415925, None, ALU.max, None, [t2.b], [t2.b])
            self.act(t1.a, t2.a, AF.Sin, [t2.b], [t1.b])
            if signed:
                self.ts("dve", t1.a, t1.a, self.invf.a[:, 1:2], None, ALU.mult, None, [t1.b, self.invf.b], [t1.b])
            self.dma(dst[s], t1.a, [t1.b], [dstb[s][0]])

    def load_xT(self, layer, s, half, xT):
        xsrc = self.x_in if layer == 0 else self.X
        xl = self.ring(2, [128, D], F32)
        xb = self.ring(2, [128, D], BF16)
        for i in range(16):
            ti = half * 16 + i
            row = s * SEQ + ti * 128
            a, b = xl[i % 2], xb[i % 2]
            self.dma(a.a, xsrc[row:row + 128, :], [self.Xb[s][ti]], [a.b])
            self.copy("act", b.a, a.a, [a.b], [b.b])
            ps = self.psum()
            pv = ps.a.bitcast(BF16)
            for k in range(8):
                self.S.op("pe", lambda e, k=k, pv=pv, b=b: e.transpose(pv[:, k * 128:(k + 1) * 128], b.a[:, k * 128:(k + 1) * 128], self.identb.a[:]),
                          [b.b, self.identb.b], [ps.b])
            self.copy("dve", xT.a[:, :, i * 128:(i + 1) * 128], pv.rearrange("p (k t) -> p k t", k=8), [ps.b], [xT.b])

    def fm_chunk(self, wsrc, xT, wl, wb, ci):
        a, b = wl[ci % 2], wb[ci % 2]
        self.dma(a.a, wsrc, (), [a.b])
        self.copy("pool", b.a, a.a, [a.b], [b.b])
        outs = []
        for tc in range(4):
            ps = self.psum()
            for k in range(8):
                self.mm(ps.a, b.a[:, k * 128:(k + 1) * 128], xT.a[:, k, tc * 512:(tc + 1) * 512], k == 0, k == 7, [b.b, xT.b], [ps.b])
            outs.append(ps)
        return outs

    def inproj_even(self, layer, s):
        j = layer // 2
        for half in range(2):
            self.phase()
            t0 = half * 2048
            xT = self.alloc([128, 8, 2048], BF16)
            self.load_xT(layer, s, half, xT)
            cts = self.alloc([128, 2048], F32)
            sts = self.alloc([128, 2048], F32)
            self.dma(cts.a, self.CT[s, :, t0:t0 + 2048], [self.CTb[s][0]], [cts.b])
            self.dma(sts.a, self.ST[s, :, t0:t0 + 2048], [self.STb[s][0]], [sts.b])
            wl = self.ring(2, [128, 1024], F32)
            wb = self.ring(2, [128, 1024], BF16)
            stage = self.ring(3, [128, 2048], BF16)
            tmpa = self.ring(2, [128, 2048], F32)
            tmpb = self.ring(2, [128, 512], F32)
            st_i = [0]

            def next_stage():
                st_i[0] += 1
                return stage[st_i[0] % 3]
            ci = [0]

            def chunk(idx):
                r = self.fm_chunk(self.e_wfm[j, idx], xT, wl, wb, ci[0])
                ci[0] += 1
                return r
            sl = lambda tc: slice(tc * 512, (tc + 1) * 512)
            SEC = os.environ.get('KSEC', 'Asbpt')
            for i in range(4 if 'A' in SEC else 0):
                ta = tmpa[i % 2]
                for tc, ps in enumerate(chunk(4 + i)):
                    self.act(ta.a[:, sl(tc)], ps.a, AF.Sigmoid, [ps.b], [ta.b])
                sg = next_stage()
                for tc, ps in enumerate(chunk(0 + i)):
                    self.tt("dve", sg.a[:, sl(tc)], ps.a, ta.a[:, sl(tc)], ALU.mult, [ps.b, ta.b], [sg.b])
                self.dma(self.H[s, i, :, t0:t0 + 2048], sg.a, [sg.b], [self.Hb[s][i]])
            simple = [(8 + i, AF.Silu, self.GA, self.GAb, i) for i in range(4)]
            simple += [(26 + i, AF.Copy, self.MQ, self.MQb, i) for i in range(4)]
            simple += [(30 + i, AF.Silu, self.GM, self.GMb, i) for i in range(4)]
            for (idx, fn, dst, dstb, di) in (simple if 's' in SEC else []):
                sg = next_stage()
                for tc, ps in enumerate(chunk(idx)):
                    self.act(sg.a[:, sl(tc)], ps.a, fn, [ps.b], [sg.b])
                self.dma(dst[s, di, :, t0:t0 + 2048], sg.a, [sg.b], [dstb[s][di]])
            for i in range(4 if 'b' in SEC else 0):
                sg = next_stage()
                for tc, ps in enumerate(chunk(34 + i)):
                    self.act(sg.a[:, sl(tc)], ps.a, AF.Silu, [ps.b], [sg.b])
                self.dma(self.GB[s, 2 * i:2 * i + 2, :, t0:t0 + 2048].rearrange("h d t -> (h d) t"), sg.a, [sg.b], [self.GBb[s][i]])
            pairs = [(12 + i, 16 + i, self.QR[s, i, :, t0:t0 + 2048], self.QRb[s][i]) for i in range(4)]
            pairs += [(20, 21, self.KR[s, :, :, t0:t0 + 2048].rearrange("a d t -> (a d) t"), self.KRb[s][0])]
            pairs += [(22 + i, 24 + i, self.QI[s, i, :, t0:t0 + 2048], self.QIb[s][i]) for i in range(2)]
            KP = os.environ.get('KP', 'qki')
            pairs = [p for p, tag in zip(pairs, 'qqqqkii') if tag in KP]
            for pi, (i0, i1, dst, dstb) in enumerate(pairs if 'p' in SEC else []):
                ta = tmpa[pi % 2]
                for tc, ps in enumerate(chunk(i0)):
                    self.tt("dve", ta.a[:, sl(tc)], ps.a, cts.a[:, sl(tc)], ALU.mult, [ps.b, cts.b], [ta.b])
                sg = next_stage()
                for tc, ps in enumerate(chunk(i1)):
                    tb = tmpb[tc % 2]
                    self.tt("dve", tb.a, ps.a, sts.a[:, sl(tc)], ALU.mult, [ps.b, sts.b], [tb.b])
                    self.tt(os.environ.get("KPE", "pool"), sg.a[:, sl(tc)], tb.a, ta.a[:, sl(tc)], ALU.add, [tb.b, ta.b], [sg.b])
                self.dma(dst, sg.a, [sg.b], [dstb])
            if 't' not in SEC:
                continue
            wt32 = self.alloc([128, 8 * 68], F32)
            wtb = self.alloc([128, 8 * 68], BF16)
            self.dma(wt32.a, self.e_wtm[j], (), [wt32.b])
            self.copy("pool", wtb.a, wt32.a, [wt32.b], [wtb.b])
            vst = self.alloc([128, 16, 64], BF16)
            wst = self.alloc([128, 16, 4], F32)
            for i in range(16):
                ps = self.psum()
                for k in range(8):
                    self.mm(ps.a[:, 0:68], xT.a[:, k, i * 128:(i + 1) * 128], wtb.a[:, k * 68:(k + 1) * 68], k == 0, k == 7, [xT.b, wtb.b], [ps.b])
                KT = os.environ.get('KT', 'vwVW')
                if 'v' in KT:
                    self.copy("act", vst.a[:, i, :], ps.a[:, 0:64], [ps.b], [vst.b])
                if 'w' in KT:
                    self.ts("dve", wst.a[:, i, :], ps.a[:, 64:68], 1.0 / 16.0, None, ALU.mult, None, [ps.b], [wst.b])
            if 'V' in KT:
                self.dma(self.V[s, :, half * 16:half * 16 + 16, :], vst.a, [vst.b], [self.Vb[s][0]])
            if 'W' in KT:
                self.dma(self.WI[s, :, half * 16:half * 16 + 16, :], wst.a, [wst.b], [self.WIb[s][0]])

    def conv_branch(self, layer, s):
        j = layer // 2
        self.phase()
        vec = self.alloc([128, 16], F32)
        self.dma(vec.a, self.e_vec[j], (), [vec.b])
        cw = self.alloc([128, 4 * 31], F32)
        self.dma(cw.a, self.e_convw[j], (), [cw.b])
        dg = self.alloc([128, 4 * 31 * 128], BF16)
        for i in range(4 * 31):
            self.act(dg.a[:, i * 128:(i + 1) * 128], self.ident32.a[:], AF.Copy, [self.ident32.b, cw.b], [dg.b], scale=cw.a[:, i:i + 1])
        p32 = self.alloc([128, 4 * 512], F32)
        pw = self.alloc([128, 4 * 512], BF16)
        self.dma(p32.a, self.e_pw2[j], (), [p32.b])
        self.copy("pool", pw.a, p32.a, [p32.b], [pw.b])
        hs = [self.alloc([128, 32 + SEQ], BF16) for _ in range(4)]
        for c in range(4):
            self.S.op("pool", lambda e, c=c: e.memset(hs[c].a[:, 0:32], 0.0), (), [hs[c].b])
            self.dma(hs[c].a[:, 32:32 + SEQ], self.H[s, c], [self.Hb[s][c]], [hs[c].b])
        y32 = self.ring(2, [128, 4, 512], F32)
        ybf = self.ring(2, [128, 4, 512], BF16)
        ysq = self.ring(2, [128, 4, 512], BF16)
        st = self.ring(2, [128, 4, 512], F32)
        sb = self.ring(2, [128, 4, 512], BF16)
        ga = self.ring(2, [128, 4, 512], BF16)
        yo = self.ring(2, [128, 4, 512], BF16)
        for tc in range(8):
            t0 = tc * 512
            Y, YB, YQ, ST, SB, G, YO = (r[tc % 2] for r in (y32, ybf, ysq, st, sb, ga, yo))
            self.dma(G.a, self.GA[s, :, :, t0:t0 + 512].rearrange("c p t -> p c t"), [self.GAb[s][c] for c in range(4)], [G.b])
            for c in range(4):
                ps = self.psum()
                for k in range(31):
                    self.mm(ps.a, dg.a[:, (c * 31 + k) * 128:(c * 31 + k + 1) * 128], hs[c].a[:, 2 + k + t0:2 + k + t0 + 512],
                            k == 0, k == 30, [dg.b, hs[c].b], [ps.b])
                self.act(Y.a[:, c, :], ps.a, AF.Identity, [ps.b, vec.b], [Y.b], bias=vec.a[:, c:c + 1])
                self.copy("pool", YB.a[:, c, :], Y.a[:, c, :], [Y.b], [YB.b])
                self.act(YQ.a[:, c, :], Y.a[:, c, :], AF.Square, [Y.b], [YQ.b])
            p1, p2 = self.psum(), self.psum()
            for c in range(4):
                self.mm(p1.a, self.ones.a[:], YB.a[:, c, :], c == 0, c == 3, [self.ones.b, YB.b], [p1.b])
            for c in range(4):
                self.mm(p2.a, self.ones.a[:], YQ.a[:, c, :], c == 0, c == 3, [self.ones.b, YQ.b], [p2.b])
            mean, var, rstd = ST.a[:, 0, :], ST.a[:, 1, :], ST.a[:, 2, :]
            self.ts("dve", mean, p1.a, 1.0 / 512, None, ALU.mult, None, [p1.b], [ST.b])
            self.tt("dve", var, mean, mean, ALU.mult, [ST.b], [ST.b])
            self.stt(var, p2.a, 1.0 / 512, var, ALU.mult, ALU.subtract, [p2.b, ST.b], [ST.b])
            self.act(rstd, var, AF.Ln, [ST.b, self.epsc.b], [ST.b], bias=self.epsc.a[:, 0:1])
            self.act(rstd, rstd, AF.Exp, [ST.b], [ST.b], scale=-0.5)
            for c in range(4):
                self.tt("dve", ST.a[:, 3, :], Y.a[:, c, :], mean, ALU.subtract, [Y.b, ST.b], [ST.b])
                self.tt("pool", Y.a[:, c, :], ST.a[:, 3, :], rstd, ALU.mult, [ST.b], [Y.b])
                self.act(SB.a[:, c, :], Y.a[:, c, :], AF.Silu, [Y.b, vec.b], [SB.b], bias=vec.a[:, 8 + c:9 + c], scale=vec.a[:, 4 + c:5 + c])
            for n in range(4):
                ps = self.psum()
                for c in range(4):
                    self.mm(ps.a, pw.a[:, c * 512 + n * 128:c * 512 + (n + 1) * 128], SB.a[:, c, :], c == 0, c == 3, [pw.b, SB.b], [ps.b])
                self.stt(YO.a[:, n, :], ps.a, vec.a[:, 12 + n:13 + n], G.a[:, n, :], ALU.add, ALU.mult, [ps.b, vec.b, G.b], [YO.b])
            self.dma(self.CATA[s, 0:4, :, t0:t0 + 512].rearrange("c p t -> p c t"), YO.a, [YO.b], [self.CATAb[s][c] for c in range(4)])

    def dsa(self, layer, s):
        self.phase()
        QRs = self.alloc([128, 4, SEQ], BF16)
        KK = self.alloc([128, SEQ], BF16)
        KI2 = self.alloc([128, SEQ], BF16)
        QIs = self.alloc([128, 2, SEQ], BF16)
        Vs = self.alloc([128, NT, 64], BF16)
        WIs = self.alloc([128, NT, 4], F32)
        for c in range(4):
            self.dma(QRs.a[:, c, :], self.QR[s, c], self.QRb[s], [QRs.b])
        for hh in range(2):
            self.dma(KK.a[hh * 64:(hh + 1) * 64, :], self.KR[s, 0], self.KRb[s], [KK.b])
            self.dma(KI2.a[hh * 64:(hh + 1) * 64, :], self.KR[s, 1], self.KRb[s], [KI2.b])
        for c in range(2):
            self.dma(QIs.a[:, c, :], self.QI[s, c], self.QIb[s], [QIs.b])
        self.dma(Vs.a, self.V[s], self.Vb[s], [Vs.b])
        self.dma(WIs.a, self.WI[s], self.WIb[s], [WIs.b])
        score = self.ring(2, [128, SEQ], F32)
        mb = self.ring(4, [128, SEQ], BF16)
        junkD = self.alloc([128, SEQ], BF16)
        junkA = self.alloc([128, SEQ], BF16)
        rl = self.ring(3, [128, 256], F32)
        sm = self.ring(4, [128, 8 + NIT + 2], F32)
        pT = self.ring(2, [128, 1024], BF16)
        gb = self.ring(2, [64, 8, 128], BF16)
        rs = self.ring(2, [64, 1024], F32)
        yb = self.ring(2, [64, 8, 128], BF16)
        (tA, bA), (tO, bO), (tS, bS), (tI, bI) = self.PS
        nqb = int(os.environ.get('KQB', NT))

        def stageA1(qb):
            q0 = qb * 128
            L = q0 + 128
            SC, SM = score[qb % 2], sm[qb % 4]
            for c in range((L + 255) // 256):
                k0 = c * 256
                n = min(256, L - k0)
                col = lambda h: (h % 2) * 512 + (h // 2) * 256
                for h in range(4):
                    pr = slice((h % 2) * 64, (h % 2) * 64 + 64)
                    self.mm(tI[:, col(h):col(h) + n], QIs.a[pr, h // 2, q0:q0 + 128], KI2.a[pr, k0:k0 + n], True, True,
                            [QIs.b, KI2.b], [bI[h % 2]])
                dst = SC.a[:, k0:k0 + n]
                self.ts("dve", dst, tI[:, 0:n], 0.0, WIs.a[:, qb, 0:1], ALU.max, ALU.mult, [bI[0], WIs.b], [SC.b])
                for h in range(1, 4):
                    r = rl[h - 1]
                    self.act(r.a[:, 0:n], tI[:, col(h):col(h) + n], AF.Relu, [bI[h % 2]], [r.b])
                    self.stt(dst, r.a[:, 0:n], WIs.a[:, qb, h:h + 1], dst, ALU.mult, ALU.add, [r.b, WIs.b, SC.b], [SC.b])
            if qb >= 2:
                self.S.op("dve", lambda e: e.tensor_reduce(out=SM.a[:, 0:1], in_=SC.a[:, 0:L], axis=AX.X, op=ALU.max), [SC.b], [SM.b])
                self.S.op("dve", lambda e: e.tensor_reduce(out=SM.a[:, 1:2], in_=SC.a[:, 0:L], axis=AX.X, op=ALU.min), [SC.b], [SM.b])
            self.tt("dve", SC.a[:, q0:L], SC.a[:, q0:L], self.negtri.a[:], ALU.add, [SC.b, self.negtri.b], [SC.b])
            if qb >= 2:
                hi, lo, W0, mid = (SM.a[:, i:i + 1] for i in range(4))
                self.tt("dve", W0, hi, lo, ALU.subtract, [SM.b], [SM.b])
                self.ts("dve", SM.a[:, 8:8 + NIT + 2], self.pow2.a[:], W0, None, ALU.mult, None, [self.pow2.b, SM.b], [SM.b])
                self.tt("dve", mid, lo, SM.a[:, 9:10], ALU.add, [SM.b], [SM.b])
            else:
                self.S.op("dve", lambda e: e.memset(SM.a[:, 6:7], -1e29), (), [SM.b])

        def bisect_dve(qb):
            L = qb * 128 + 128
            SC, SM = score[qb % 2], sm[qb % 4]
            mid, cnt, tt_, thr = SM.a[:, 3:4], SM.a[:, 4:5], SM.a[:, 5:6], SM.a[:, 6:7]
            for k in range(1, NIT + 1):
                self.ts("dve", junkD.a[:, 0:L], SC.a[:, 0:L], mid, None, ALU.is_ge, ALU.add, [SC.b, SM.b], [junkD.b, SM.b], accum=cnt)
                if k < NIT:
                    self.ts("dve", tt_, cnt, 256.0, 0.5, ALU.is_ge, ALU.subtract, [SM.b], [SM.b])
                    self.stt(mid, tt_, SM.a[:, 8 + k:9 + k], mid, ALU.mult, ALU.add, [SM.b], [SM.b])
                else:
                    self.ts("dve", tt_, cnt, 256.0, 1.0, ALU.is_ge, ALU.subtract, [SM.b], [SM.b])
                    self.stt(thr, tt_, SM.a[:, 8 + k:9 + k], mid, ALU.mult, ALU.add, [SM.b], [SM.b])

        def stageA3(qb):
            L = qb * 128 + 128
            SC, SM, MB = score[qb % 2], sm[qb % 4], mb[qb % 4]
            self.ts("dve", MB.a[:, 0:L], SC.a[:, 0:L], SM.a[:, 6:7], NEG, ALU.is_lt, ALU.mult, [SC.b, SM.b], [MB.b])

        osb = self.ring(2, [64, 1024], F32)

        def stageB_steps(qb):
            q0 = qb * 128
            MB = mb[qb % 4]
            G, RS, YB, OS = (r[qb % 2] for r in (gb, rs, yb, osb))
            nsc = qb + 1
            steps = []

            def qk_exp(sc):
                P = pT[sc % 2]
                for par in range(2):
                    pr = slice(par * 64, par * 64 + 64)
                    o = tA[:, par * 512:(par + 1) * 512]
                    self.mm(o.rearrange("p (a b) -> p a b", a=4), KK.a[pr, sc * 128:(sc + 1) * 128], QRs.a[pr, :, q0:q0 + 128], True, False, [KK.b, QRs.b], [bA[par]])
                    self.mm(o, MB.a[:, sc * 128:(sc + 1) * 128], self.ident4.a[:], False, True, [MB.b, self.ident4.b], [bA[par]])
                for par in range(2):
                    cs = slice(par * 512, (par + 1) * 512)
                    self.act(P.a[:, cs], tA[:, cs], AF.Exp, [bA[par]], [P.b], scale=0.125)

            def chunk(sc):
                if sc == 0:
                    self.dma(G.a, self.GB[s, :, :, q0:q0 + 128].rearrange("h d t -> d h t"), self.GBb[s], [G.b])
                    qk_exp(0)
                if sc + 1 < nsc:
                    qk_exp(sc + 1)
                P = pT[sc % 2]
                for par in range(2):
                    cs = slice(par * 512, (par + 1) * 512)
                    self.mm(tO[0:64, cs], Vs.a[:, sc, :], P.a[:, cs], sc == 0, sc == nsc - 1, [Vs.b, P.b], [bO[par]])
                    self.mm(tS[0:64, cs], self.ones.a[:, 0:64], P.a[:, cs], sc == 0, sc == nsc - 1, [self.ones.b, P.b], [bS[par]])
                if sc == nsc - 1:
                    self.act(RS.a, tS[0:64, :], AF.Ln, [bS[0], bS[1]], [RS.b])
                    self.act(RS.a, RS.a, AF.Exp, [RS.b], [RS.b], scale=-1.0)
                    self.copy("act", OS.a, tO[0:64, :], [bO[0], bO[1]], [OS.b])
                    self.tt("pool", OS.a, OS.a, RS.a, ALU.mult, [OS.b, RS.b], [OS.b])
                    ybv = YB.a.rearrange("d (pair par) t -> d par pair t", par=2)
                    gv = G.a.rearrange("d (pair par) t -> d par pair t", par=2)
                    for par in range(2):
                        self.tt("pool", ybv[:, par], OS.a[:, par * 512:(par + 1) * 512].rearrange("d (pair t) -> d pair t", pair=4), gv[:, par],
                                ALU.mult, [OS.b, G.b], [YB.b])
                    self.dma(self.CATB[s, :, :, q0:q0 + 128].rearrange("h d t -> d h t"), YB.a, [YB.b], [self.CATBb[s][qb]])
            for sc in range(nsc):
                steps.append(lambda sc=sc: chunk(sc))
            return steps

        def bisect_act_steps(qb):
            L = qb * 128 + 128
            SC, SM = score[qb % 2], sm[qb % 4]
            mid, sg, g, thr = SM.a[:, 3:4], SM.a[:, 4:5], SM.a[:, 5:6], SM.a[:, 6:7]

            def it(k):
                self.S.op("act", lambda e: e.activation(out=junkA.a[:, 0:L], in_=SC.a[:, 0:L], func=AF.Sign, bias=mid, scale=-1.0, accum_out=sg),
                          [SC.b, SM.b], [junkA.b, SM.b])
                self.act(g, sg, AF.Sign, [SM.b], [SM.b], bias=float(L - 511), scale=-1.0)
                self.act(mid, g, AF.Identity, [SM.b], [SM.b], bias=mid, scale=SM.a[:, 9 + k:10 + k])
                if k == NIT:
                    self.act(thr, SM.a[:, 9 + NIT:10 + NIT], AF.Identity, [SM.b], [SM.b], bias=mid, scale=-1.0)
            return [lambda k=k: it(k) for k in range(1, NIT + 1)]

        def interleave(chunks, iters):
            nC, nI = len(chunks), len(iters)
            head = (nC * 45 + 99) // 100 if nI else nC
            for c in chunks[:head]:
                c()
            rest = chunks[head:]
            ci = ii = 0
            while ci < len(rest) or ii < nI:
                if ii < nI and (ci >= len(rest) or ii * max(len(rest), 1) <= ci * nI):
                    iters[ii]()
                    ii += 1
                else:
                    rest[ci]()
                    ci += 1

        npair = nqb // 2
        for r in range(npair + 1):
            iters = []
            if r < npair:
                a2, b2 = 2 * r, 2 * r + 1
                stageA1(a2)
                stageA1(b2)
                if a2 >= 2:
                    bisect_dve(a2)
                    iters = bisect_act_steps(b2)
            chunks = []
            if r >= 1:
                chunks = stageB_steps(2 * r - 2) + stageB_steps(2 * r - 1)
            interleave(chunks, iters)
            if r < npair:
                stageA3(2 * r)
                stageA3(2 * r + 1)

    def mem_attn(self, layer, s):
        self.phase()
        ml = self.ring(2, [128, D], F32)
        mbf = self.ring(2, [128, D], BF16)
        memT = self.alloc([128, 8, MEM], BF16)
        for i in range(2):
            a, b = ml[i], mbf[i]
            self.dma(a.a, self.mem_in[s * MEM + i * 128:s * MEM + (i + 1) * 128, :], (), [a.b])
            self.copy("act", b.a, a.a, [a.b], [b.b])
            ps = self.psum()
            pv = ps.a.bitcast(BF16)
            for k in range(8):
                self.S.op("pe", lambda e, k=k, pv=pv, b=b: e.transpose(pv[:, k * 128:(k + 1) * 128], b.a[:, k * 128:(k + 1) * 128], self.identb.a[:]),
                          [b.b, self.identb.b], [ps.b])
            self.copy("dve", memT.a[:, :, i * 128:(i + 1) * 128], pv.rearrange("p (k t) -> p k t", k=8), [ps.b], [memT.b])
        w32 = self.alloc([128, 8 * 512], F32)
        wk = self.alloc([128, 8 * 512], BF16)
        wv = self.alloc([128, 8 * 512], BF16)
        self.dma(w32.a, self.m_wk[layer], (), [w32.b])
        self.copy("pool", wk.a, w32.a, [w32.b], [wk.b])
        self.dma(w32.a, self.m_wv[layer], [], [w32.b])
        self.copy("pool", wv.a, w32.a, [w32.b], [wv.b])
        kmT = self.alloc([128, 4, MEM], BF16)
        vm = self.alloc([128, 2, 512], BF16)
        for h in range(4):
            ps = self.psum()
            for k in range(8):
                self.mm(ps.a[:, 0:MEM], wk.a[:, k * 512 + h * 128:k * 512 + (h + 1) * 128], memT.a[:, k, :], k == 0, k == 7, [wk.b, memT.b], [ps.b])
            self.copy("act", kmT.a[:, h, :], ps.a[:, 0:MEM], [ps.b], [kmT.b])
        for mc in range(2):
            ps = self.psum()
            for k in range(8):
                self.mm(ps.a, memT.a[:, k, mc * 128:(mc + 1) * 128], wv.a[:, k * 512:(k + 1) * 512], k == 0, k == 7, [wv.b, memT.b], [ps.b])
            self.copy("act", vm.a[:, mc, :], ps.a, [ps.b], [vm.b])
        mq = self.ring(2, [128, 4, 512], BF16)
        gm = self.ring(2, [128, 4, 512], BF16)
        pT = self.ring(2, [128, 2, 512], BF16)
        rs = self.ring(2, [128, 512], F32)
        yo = self.ring(2, [128, 4, 512], BF16)
        scale = 128 ** -0.5
        for tc in range(8):
            t0 = tc * 512
            Q, G, YO = mq[tc % 2], gm[tc % 2], yo[tc % 2]
            self.dma(Q.a, self.MQ[s, :, :, t0:t0 + 512].rearrange("c p t -> p c t"), self.MQb[s], [Q.b])
            self.dma(G.a, self.GM[s, :, :, t0:t0 + 512].rearrange("c p t -> p c t"), self.GMb[s], [G.b])
            for h in range(4):
                P, R = pT[h % 2], rs[h % 2]
                for mc in range(2):
                    ps = self.psum()
                    self.mm(ps.a, kmT.a[:, h, mc * 128:(mc + 1) * 128], Q.a[:, h, :], True, True, [kmT.b, Q.b], [ps.b])
                    self.act(P.a[:, mc, :], ps.a, AF.Exp, [ps.b], [P.b], scale=scale)
                po, pS = self.psum(), self.psum()
                for mc in range(2):
                    self.mm(po.a, vm.a[:, mc, h * 128:(h + 1) * 128], P.a[:, mc, :], mc == 0, mc == 1, [vm.b, P.b], [po.b])
                for mc in range(2):
                    self.mm(pS.a, self.ones.a[:], P.a[:, mc, :], mc == 0, mc == 1, [self.ones.b, P.b], [pS.b])
                self.act(R.a, pS.a, AF.Ln, [pS.b], [R.b])
                self.act(R.a, R.a, AF.Exp, [R.b], [R.b], scale=-1.0)
                self.tt("dve", R.a, po.a, R.a, ALU.mult, [po.b, R.b], [R.b])
                self.tt("pool", YO.a[:, h, :], R.a, G.a[:, h, :], ALU.mult, [R.b, G.b], [YO.b])
            self.dma(self.CATM[s, :, :, t0:t0 + 512].rearrange("c p t -> p c t"), YO.a, [YO.b], self.CATMb[s])

    def inproj_odd(self, layer, s):
        j = layer // 2
        for half in range(2):
            self.phase()
            t0 = half * 2048
            xT = self.alloc([128, 8, 2048], BF16)
            self.load_xT(layer, s, half, xT)
            wl = self.ring(2, [128, 1024], F32)
            wb = self.ring(2, [128, 1024], BF16)
            stage = self.ring(3, [128, 2048], BF16)
            sl = lambda tc: slice(tc * 512, (tc + 1) * 512)
            specs = [(i, AF.Gelu_apprx_tanh, self.U, self.Ub, i) for i in range(8)]
            specs += [(8 + i, AF.Silu, self.GC, self.GCb, i) for i in range(8)]
            specs += [(16 + i, AF.Copy, self.MQ, self.MQb, i) for i in range(4)]
            specs += [(20 + i, AF.Silu, self.GM, self.GMb, i) for i in range(4)]
            for ci, (idx, fn, dst, dstb, di) in enumerate(specs):
                sg = stage[ci % 3]
                for tc, ps in enumerate(self.fm_chunk(self.o_wfm[j, idx], xT, wl, wb, ci)):
                    self.act(sg.a[:, sl(tc)], ps.a, fn, [ps.b], [sg.b])
                self.dma(dst[s, di, :, t0:t0 + 2048], sg.a, [sg.b], [dstb[s][di]])
            vg = self.alloc([128, 2048], F32)
            self.dma(vg.a[:, 0:1024], self.o_vln[j, 0:1, :].partition_broadcast(128), (), [vg.b])
            self.dma(vg.a[:, 1024:2048], self.o_vln[j, 1:2, :].partition_broadcast(128), (), [vg.b])
            v32 = self.ring(2, [128, 1024], F32)
            vo = self.ring(2, [128, 1024], BF16)
            stt_ = self.ring(2, [128, 16], F32)
            w32 = self.alloc([128, 8, 512], F32)
            wts = [self.alloc([128, 8, 512], BF16), self.alloc([128, 8, 512], BF16)]
            for hh in range(2):
                self.dma(w32.a, self.o_wtm[j].rearrange("p (k n) -> p k n", k=8)[:, :, hh * 512:(hh + 1) * 512], (), [w32.b])
                self.copy("pool", wts[hh].a, w32.a, [w32.b], [wts[hh].b])
            for i in range(16):
                ti = half * 16 + i
                Vt, VO, SS = v32[i % 2], vo[i % 2], stt_[i % 2]
                for hh in range(2):
                    ps = self.psum()
                    for k in range(8):
                        self.mm(ps.a, xT.a[:, k, i * 128:(i + 1) * 128], wts[hh].a[:, k, :], k == 0, k == 7, [xT.b, wts[hh].b], [ps.b])
                    self.act(Vt.a[:, hh * 512:(hh + 1) * 512], ps.a, AF.Gelu_apprx_tanh, [ps.b], [Vt.b])
                self.layernorm(Vt, SS, vg, 0, VO.a, VO.b)
                row = ti * 128
                self.dma(self.VLN[s, row:row + 128, :], VO.a, [VO.b], [self.VLNb[s][ti]])

    def layernorm(self, Z, SS, gb, goff, out, outb):
        for hh in range(2):
            self.S.op("dve", lambda e, hh=hh: e.bn_stats(out=SS.a[:, hh * 6:(hh + 1) * 6], in_=Z.a[:, hh * 512:(hh + 1) * 512]), [Z.b], [SS.b])
        self.S.op("dve", lambda e: e.bn_aggr(out=SS.a[:, 12:14], in_=SS.a[:, 0:12]), [SS.b], [SS.b])
        self.act(SS.a[:, 14:15], SS.a[:, 13:14], AF.Ln, [SS.b, self.epsc.b], [SS.b], bias=self.epsc.a[:, 0:1])
        self.act(SS.a[:, 14:15], SS.a[:, 14:15], AF.Exp, [SS.b], [SS.b], scale=-0.5)
        self.ts("dve", Z.a, Z.a, SS.a[:, 12:13], SS.a[:, 14:15], ALU.subtract, ALU.mult, [Z.b, SS.b], [Z.b])
        self.tt("pool", Z.a, Z.a, gb.a[:, goff:goff + 1024], ALU.mult, [Z.b, gb.b], [Z.b])
        self.tt("pool", out, Z.a, gb.a[:, goff + 1024:goff + 2048], ALU.add, [Z.b, gb.b], [outb])

    def sgu(self, layer, s):
        j = layer // 2
        self.phase()
        w32 = self.alloc([128, 8, 128], F32)
        wsb = self.alloc([128, 8, 128], BF16)
        self.dma(w32.a, self.o_wsT[j].rearrange("p (g t) -> p g t", g=8), (), [w32.b])
        for g in range(8):
            self.tt("dve", wsb.a[:, g, :], w32.a[:, g, :], self.tril.a[:], ALU.mult, [w32.b, self.tril.b], [wsb.b])
        bsb = self.alloc([128, 8, 128], F32)
        self.dma(bsb.a.rearrange("p g t -> p (g t)"), self.o_bs[j].partition_broadcast(128), (), [bsb.b])
        vt = self.ring(2, [128, 4, 1024], BF16)
        uu = self.ring(2, [128, 8, 512], BF16)
        gc = self.ring(2, [128, 8, 512], BF16)
        tm = self.ring(2, [128, 512], F32)
        yo = self.ring(2, [128, 8, 512], BF16)
        for tc in range(8):
            t0 = tc * 512
            Vt, Uu, Gc, YO = vt[tc % 2], uu[tc % 2], gc[tc % 2], yo[tc % 2]
            self.dma(Vt.a, self.VLN[s, t0:t0 + 512, :].rearrange("(c p) n -> p c n", p=128), self.VLNb[s][tc * 4:tc * 4 + 4], [Vt.b])
            self.dma(Uu.a, self.U[s, :, :, t0:t0 + 512].rearrange("c p t -> p c t"), self.Ub[s], [Uu.b])
            self.dma(Gc.a, self.GC[s, :, :, t0:t0 + 512].rearrange("c p t -> p c t"), self.GCb[s], [Gc.b])
            for g in range(8):
                ps = self.psum()
                for c in range(4):
                    self.mm(ps.a[:, c * 128:(c + 1) * 128], Vt.a[:, c, g * 128:(g + 1) * 128], wsb.a[:, g, :], True, True, [Vt.b, wsb.b], [ps.b])
                Tm = tm[g % 2]
                self.tt("dve", Tm.a.rearrange("p (c t) -> p c t", c=4), ps.a.rearrange("p (c t) -> p c t", c=4),
                        bsb.a[:, g:g + 1, :].to_broadcast([128, 4, 128]), ALU.add, [ps.b, bsb.b], [Tm.b])
                self.tt("pool", Tm.a, Tm.a, Uu.a[:, g, :], ALU.mult, [Tm.b, Uu.b], [Tm.b])
                self.tt("pool", YO.a[:, g, :], Tm.a, Gc.a[:, g, :], ALU.mult, [Tm.b, Gc.b], [YO.b])
            self.dma(self.CATA[s, :, :, t0:t0 + 512].rearrange("c p t -> p c t"), YO.a, [YO.b], self.CATAb[s])

    def outproj(self, layer, s):
        j = layer // 2
        even = layer % 2 == 0
        self.phase()
        last = layer == self.layers - 1
        xsrc = self.x_in if layer == 0 else self.X
        xdst = self.y_out if last else self.X
        gb = self.alloc([128, 2048], F32)
        self.dma(gb.a[:, 0:1024], self.ln_gb[layer, 0:1, :].partition_broadcast(128), (), [gb.b])
        self.dma(gb.a[:, 1024:2048], self.ln_gb[layer, 1:2, :].partition_broadcast(128), (), [gb.b])
        w32 = self.alloc([128, 4 * 1024], F32)
        wo = self.alloc([128, 12, 1024], BF16)
        if even:
            srcs = [(self.e_woA[j], 0, 128), (self.e_woM[j], 8, 128)]
            for (src, c0, np_) in srcs:
                self.dma(w32.a, src, (), [w32.b])
                self.copy("pool", wo.a[:, c0:c0 + 4, :], w32.a.rearrange("p (c n) -> p c n", c=4), [w32.b], [wo.b])
        woB = None
        if even:
            woB = self.alloc([64, 8, 1024], BF16)
            for hh in range(2):
                self.dma(w32.a[0:64, :], self.e_woB[j, :, hh * 4096:(hh + 1) * 4096], (), [w32.b])
                self.copy("pool", woB.a[:, hh * 4:(hh + 1) * 4, :], w32.a[0:64, :].rearrange("p (c n) -> p c n", c=4), [w32.b], [woB.b])
        else:
            for c0 in range(0, 12, 4):
                self.dma(w32.a, self.o_wo[j, :, c0 * 1024:(c0 + 4) * 1024], (), [w32.b])
                self.copy("pool", wo.a[:, c0:c0 + 4, :], w32.a.rearrange("p (c n) -> p c n", c=4), [w32.b], [wo.b])
        ca = self.ring(2, [128, 12, 512], BF16)
        cb = self.ring(2, [64, 8, 512], BF16)
        xt = self.ring(2, [128, D], F32)
        zz = self.ring(2, [128, D], F32)
        xo = self.ring(2, [128, D], F32)
        ss = self.ring(2, [128, 16], F32)
        for tc in range(8):
            t0 = tc * 512
            CA, CB = ca[tc % 2], cb[tc % 2]
            if even:
                self.dma(CA.a[:, 0:4, :], self.CATA[s, 0:4, :, t0:t0 + 512].rearrange("c p t -> p c t"), self.CATAb[s][0:4], [CA.b])
                self.dma(CB.a, self.CATB[s, :, :, t0:t0 + 512].rearrange("h d t -> d h t"), self.CATBb[s][tc * 4:tc * 4 + 4], [CB.b])
            else:
                self.dma(CA.a[:, 0:8, :], self.CATA[s, :, :, t0:t0 + 512].rearrange("c p t -> p c t"), self.CATAb[s], [CA.b])
            self.dma(CA.a[:, 8:12, :], self.CATM[s, :, :, t0:t0 + 512].rearrange("c p t -> p c t"), self.CATMb[s], [CA.b])
            for i in range(4):
                ti = tc * 4 + i
                row = s * SEQ + ti * 128
                Xt, Z, XO, SS = (r[ti % 2] for r in (xt, zz, xo, ss))
                self.dma(Xt.a, xsrc[row:row + 128, :], [self.Xb[s][ti]], [Xt.b])
                tsl = slice(i * 128, (i + 1) * 128)
                for nh in range(2):
                    ps = self.psum()
                    ns = slice(nh * 512, (nh + 1) * 512)
                    ops = []
                    for c in ([0, 1, 2, 3, 8, 9, 10, 11] if even else range(12)):
                        ops.append((CA.a[:, c, tsl], wo.a[:, c, ns], [CA.b, wo.b]))
                    if even:
                        for h in range(8):
                            ops.append((CB.a[:, h, tsl], woB.a[:, h, ns], [CB.b, woB.b]))
                    for oi, (l, r, rd) in enumerate(ops):
                        self.mm(ps.a, l, r, oi == 0, oi == len(ops) - 1, rd, [ps.b])
                    self.stt(Z.a[:, ns], Xt.a[:, ns], float(DN_ALPHA), ps.a, ALU.mult, ALU.add, [Xt.b, ps.b], [Z.b])
                self.layernorm(Z, SS, gb, 0, XO.a, XO.b)
                self.dma(xdst[row:row + 128, :], XO.a, [XO.b], [self.Xb[s][ti]])

    def build(self, only=None):
        on = lambda n: only is None or n in only
        for s in range(self.nseq):
            if on("rope"):
                self.rope_tables(s)
        for layer in range(self.layers):
            for s in range(self.nseq):
                if layer % 2 == 0:
                    if on("inproj"):
                        self.inproj_even(layer, s)
                    if on("conv"):
                        self.conv_branch(layer, s)
                    if on("dsa"):
                        self.dsa(layer, s)
                else:
                    if on("inproj"):
                        self.inproj_odd(layer, s)
                    if on("sgu"):
                        self.sgu(layer, s)
                if on("mem"):
                    self.mem_attn(layer, s)
                if on("out"):
                    self.outproj(layer, s)
        self.S.barrier()
        self.S.emit()
        return self.nc


def _fm(W, cols):
    w = W[:, cols]
    return np.ascontiguousarray(w.reshape(8, 128, 128).transpose(1, 0, 2).reshape(128, 8 * 128))


def _kmajor(W):
    K, N = W.shape
    return np.ascontiguousarray(W.reshape(K // 128, 128, N).transpose(1, 0, 2).reshape(128, (K // 128) * N))


def host_consts():
    inv = (10000.0 ** (-np.arange(0, 64, 2, dtype=np.float32) / np.float32(64))).astype(np.float32)
    invf = np.zeros((128, 2), np.float32)
    for p in range(128):
        invf[p, 0] = inv[p % 32]
        invf[p, 1] = -1.0 if (p % 64) < 32 else 1.0
    t = np.arange(128)
    negtri = np.where(t[None, :] <= t[:, None], 0.0, -1e30).astype(np.float32)
    tril = (t[None, :] >= t[:, None]).astype(np.float32)
    pow2 = np.tile((2.0 ** -np.arange(NIT + 2, dtype=np.float32))[None, :], (128, 1)).astype(np.float32)
    return {"c_ident": np.eye(128, dtype=np.float32), "c_invf": invf, "c_negtri": negtri, "c_tril": tril, "c_pow2": pow2}


def host_weights(e_w_in, e_conv_w, e_conv_b, e_cln_g, e_cln_b, e_pw2_w, e_pw2_b, e_w_out,
                 o_w_in, o_vln_g, o_vln_b, o_ws, o_bs, o_w_out, mem_wk, mem_wv, ln_g, ln_b):
    f = lambda a: np.asarray(a, dtype=np.float32)
    nE, nO = e_w_in.shape[0], o_w_in.shape[0]
    ar = np.arange
    sw = (ar(64) + 32) % 64
    out = {}
    wfm = np.zeros((nE, NCH_E, 128, 1024), np.float32)
    wtm = np.zeros((nE, 128, 8 * 68), np.float32)
    for j in range(nE):
        W = f(e_w_in[j])
        ch = []
        ch += [0 + 128 * i + ar(128) for i in range(4)]
        ch += [512 + 128 * i + ar(128) for i in range(4)]
        ch += [1024 + 128 * i + ar(128) for i in range(4)]
        ch += [1536 + 128 * i + ar(128) for i in range(4)]
        ch += [1536 + 128 * i + np.concatenate([sw, 64 + sw]) for i in range(4)]
        ch += [np.concatenate([2048 + ar(64), 2432 + ar(64)])]
        ch += [np.concatenate([2048 + sw, 2432 + sw])]
        ch += [2176 + 128 * i + ar(128) for i in range(2)]
        ch += [2176 + 128 * i + np.concatenate([sw, 64 + sw]) for i in range(2)]
        ch += [3012 + 128 * i + ar(128) for i in range(4)]
        ch += [3524 + 128 * i + ar(128) for i in range(4)]
        ch += [2500 + 128 * i + ar(128) for i in range(4)]
        assert len(ch) == NCH_E
        for c, cols in enumerate(ch):
            wfm[j, c] = _fm(W, cols)
        wtm[j] = _kmajor(W[:, np.concatenate([2112 + ar(64), 2496 + ar(4)])])
    out["e_wfm"], out["e_wtm"] = wfm, wtm
    cw = f(e_conv_w)
    out["e_convw"] = np.ascontiguousarray(cw.reshape(nE, 31, 4, 128).transpose(0, 3, 2, 1).reshape(nE, 128, 4 * 31))
    pc = lambda v: f(v).reshape(nE, 4, 128).transpose(0, 2, 1)
    out["e_vec"] = np.ascontiguousarray(np.concatenate([pc(e_conv_b), pc(e_cln_g), pc(e_cln_b), pc(e_pw2_b)], axis=2))
    out["e_pw2"] = np.stack([_kmajor(f(e_pw2_w[j])) for j in range(nE)])
    wo = f(e_w_out)
    out["e_woA"] = np.stack([_kmajor(wo[j, 0:512]) for j in range(nE)])
    out["e_woB"] = np.stack([np.ascontiguousarray(wo[j, 512:1024].reshape(8, 64, 1024).transpose(1, 0, 2).reshape(64, 8 * 1024)) for j in range(nE)])
    out["e_woM"] = np.stack([_kmajor(wo[j, 1024:1536]) for j in range(nE)])
    ofm = np.zeros((nO, NCH_O, 128, 1024), np.float32)
    otm = np.zeros((nO, 128, 8 * 1024), np.float32)
    for j in range(nO):
        W = f(o_w_in[j])
        ch = [0 + 128 * i + ar(128) for i in range(8)] + [2048 + 128 * i + ar(128) for i in range(8)]
        ch += [3072 + 128 * i + ar(128) for i in range(4)] + [3584 + 128 * i + ar(128) for i in range(4)]
        for c, cols in enumerate(ch):
            ofm[j, c] = _fm(W, cols)
        otm[j] = _kmajor(W[:, 1024:2048])
    out["o_wfm"], out["o_wtm"] = ofm, otm
    out["o_vln"] = np.ascontiguousarray(np.stack([f(o_vln_g), f(o_vln_b)], axis=1))
    out["o_wsT"] = np.ascontiguousarray(f(o_ws).transpose(0, 3, 1, 2).reshape(nO, 128, 8 * 128))
    out["o_bs"] = np.ascontiguousarray(f(o_bs).reshape(nO, 1, 8 * 128))
    out["o_wo"] = np.stack([_kmajor(f(o_w_out[j])) for j in range(nO)])
    out["m_wk"] = np.stack([_kmajor(f(mem_wk[l])) for l in range(DEPTH)])
    out["m_wv"] = np.stack([_kmajor(f(mem_wv[l])) for l in range(DEPTH)])
    out["ln_gb"] = np.ascontiguousarray(np.stack([f(ln_g), f(ln_b)], axis=1))
    return out


_PROG_CACHE = {}


def kernel(x, mem, positions, e_w_in, e_conv_w, e_conv_b, e_cln_g, e_cln_b, e_pw2_w, e_pw2_b, e_w_out,
           o_w_in, o_vln_g, o_vln_b, o_ws, o_bs, o_w_out, mem_wk, mem_wv, ln_g, ln_b):
    x = np.asarray(x, dtype=np.float32)
    mem = np.asarray(mem, dtype=np.float32)
    positions = np.asarray(positions, dtype=np.int32)
    B = x.shape[0]
    nseq = B // NCORES
    shared = host_consts()
    shared.update(host_weights(e_w_in, e_conv_w, e_conv_b, e_cln_g, e_cln_b, e_pw2_w, e_pw2_b, e_w_out,
                               o_w_in, o_vln_g, o_vln_b, o_ws, o_bs, o_w_out, mem_wk, mem_wv, ln_g, ln_b))
    if "p" not in _PROG_CACHE:
        _PROG_CACHE["p"] = Prog(nseq, DEPTH).build()
    nc = _PROG_CACHE["p"]
    in_maps = []
    for c in range(NCORES):
        d = dict(shared)
        d["x"] = np.ascontiguousarray(x[c * nseq:(c + 1) * nseq].reshape(nseq * SEQ, D))
        d["mem"] = np.ascontiguousarray(mem[c * nseq:(c + 1) * nseq].reshape(nseq * MEM, D))
        d["pos"] = np.ascontiguousarray(positions[c * nseq:(c + 1) * nseq])
        in_maps.append(d)
    res = run_bass_kernel_spmd(nc, in_maps, core_ids=list(range(NCORES)))
    out = np.concatenate([r["y"].reshape(nseq, SEQ, D) for r in res.results], axis=0)
    return out.astype(np.float32)
```

```python
import os
import numpy as np
import concourse.bass as bass
import concourse.mybir as mybir
from concourse.bass_utils import run_bass_kernel_spmd

F32 = mybir.dt.float32
BF16 = mybir.dt.bfloat16
I32 = mybir.dt.int32
AF = mybir.ActivationFunctionType
ALU = mybir.AluOpType
AX = mybir.AxisListType

NCORES = 8
D = 1024
SEQ = 4096
MEM = 256
DEPTH = 4
NT = SEQ // 128
LN_EPS = 1e-5
DN_ALPHA = (2 * DEPTH) ** 0.25
NIT = 22
NEG = -30000.0
SEM_EPOCH = 30000
DMA_K = 16
NCH_E = 38
NCH_O = 24


class Buf:
    __slots__ = ("name", "lw", "rd", "excl")

    def __init__(self, name="", excl=False):
        self.name = name
        self.lw = {}
        self.rd = {}
        self.excl = excl


class Stream:
    def __init__(self, S, name, eng, inc, k):
        self.S, self.name, self.eng, self.inc, self.k = S, name, eng, inc, k
        self.i = 0
        self.vals = [0] * k
        self.epochs = [0] * k
        self.sems = {}
        for j in range(k):
            self.sems[(name, j, 0)] = S.nc.alloc_semaphore(name=f"s_{name}_{j}_0")

    def next(self):
        j = self.i % self.k
        self.i += 1
        if self.vals[j] + self.inc > SEM_EPOCH:
            self.epochs[j] += 1
            self.vals[j] = 0
            self.sems[(self.name, j, self.epochs[j])] = self.S.nc.alloc_semaphore(
                name=f"s_{self.name}_{j}_{self.epochs[j]}")
        self.vals[j] += self.inc
        return ((self.name, j, self.epochs[j]), self.vals[j])

    def current(self):
        return [((self.name, j, self.epochs[j]), self.vals[j]) for j in range(self.k) if self.vals[j] > 0]


class Sched:
    def __init__(self, nc):
        self.nc = nc
        self.lists = {e: [] for e in ("pe", "dve", "act", "pool", "sp")}
        self.seen = {e: {} for e in self.lists}
        self.pending = {e: {} for e in self.lists}
        self.streams = {}
        for nm, eng, inc, k in (("pe", "pe", 1, 1), ("dve", "dve", 1, 1), ("act", "act", 1, 1),
                                ("pool", "pool", 1, 1), ("spd", "sp", 16, DMA_K), ("actd", "act", 16, DMA_K)):
            self.streams[nm] = Stream(self, nm, eng, inc, k)
        self.n = 0

    def semh(self, key):
        return self.streams[key[0]].sems[key]

    def op(self, stream, fn, reads=(), writes=()):
        st = self.streams[stream]
        eng = st.eng
        if any(b.excl for b in reads):
            writes = list(writes) + [b for b in reads if b.excl and b not in writes]
            reads = [b for b in reads if not b.excl]
        deps = dict(self.pending[eng])
        self.pending[eng] = {}
        if st.k > 1:
            jj = st.i % st.k
            if st.vals[jj] > 0:
                deps[(st.name, jj, st.epochs[jj])] = st.vals[jj]

        def add(d):
            for k, v in d.items():
                if deps.get(k, 0) < v:
                    deps[k] = v
        for b in reads:
            add(b.lw)
        for b in writes:
            add(b.lw)
            add(b.rd)
        seen = self.seen[eng]
        waits = []
        for k, v in deps.items():
            if stream == "pe" and k[0] == "pe":
                continue
            if seen.get(k, 0) >= v:
                continue
            seen[k] = v
            waits.append((k, v))
        key, val = st.next()
        self.lists[eng].append((waits, fn, key, st.inc))
        for b in reads:
            if b.rd.get(key, 0) < val:
                b.rd[key] = val
        for b in writes:
            b.lw = {key: val}
            b.rd = {}
        self.n += 1

    def barrier(self):
        cur = {}
        for s in self.streams.values():
            for k, v in s.current():
                cur[k] = v
        for e in self.lists:
            p = self.pending[e]
            for k, v in cur.items():
                if p.get(k, 0) < v:
                    p[k] = v

    def emit(self):
        nc = self.nc
        engmap = {"pe": "tensor", "dve": "vector", "act": "scalar", "pool": "gpsimd", "sp": "sync"}
        fin = []
        for s in self.streams.values():
            fin += s.current()
        with nc.Block() as block:
            for e, lst in self.lists.items():
                def body(engine, lst=lst, e=e):
                    for waits, fn, key, inc in lst:
                        for (wk, wv) in waits:
                            engine.wait_ge(self.semh(wk), wv)
                        fn(engine).then_inc(self.semh(key), inc)
                    if e == "sp":
                        for (wk, wv) in fin:
                            engine.wait_ge(self.semh(wk), wv)
                getattr(block, engmap[e])(body)


class T:
    __slots__ = ("a", "b")

    def __init__(self, a, b):
        self.a, self.b = a, b


class Prog:
    def __init__(self, nseq, layers, debug=False):
        self.nseq, self.layers, self.debug = nseq, layers, debug
        self.nc = nc = bass.Bass("TRN2", target_bir_lowering=False)
        self.S = Sched(nc)
        self.dbg_names = []
        ntok = nseq * SEQ
        ein = lambda n, sh, dt=F32: nc.dram_tensor(n, list(sh), dt, kind="ExternalInput").ap()
        self.x_in = ein("x", [ntok, D])
        self.mem_in = ein("mem", [nseq * MEM, D])
        self.pos_in = ein("pos", [nseq, SEQ], I32)
        self.c_ident = ein("c_ident", [128, 128])
        self.c_invf = ein("c_invf", [128, 2])
        self.c_negtri = ein("c_negtri", [128, 128])
        self.c_tril = ein("c_tril", [128, 128])
        self.c_pow2 = ein("c_pow2", [128, NIT + 2])
        nE, nO = (DEPTH + 1) // 2, DEPTH // 2
        self.e_wfm = ein("e_wfm", [nE, NCH_E, 128, 8 * 128])
        self.e_wtm = ein("e_wtm", [nE, 128, 8 * 68])
        self.e_convw = ein("e_convw", [nE, 128, 4 * 31])
        self.e_vec = ein("e_vec", [nE, 128, 16])
        self.e_pw2 = ein("e_pw2", [nE, 128, 4 * 512])
        self.e_woA = ein("e_woA", [nE, 128, 4 * 1024])
        self.e_woB = ein("e_woB", [nE, 64, 8 * 1024])
        self.e_woM = ein("e_woM", [nE, 128, 4 * 1024])
        self.o_wfm = ein("o_wfm", [nO, NCH_O, 128, 8 * 128])
        self.o_wtm = ein("o_wtm", [nO, 128, 8 * 1024])
        self.o_vln = ein("o_vln", [nO, 2, 1024])
        self.o_wsT = ein("o_wsT", [nO, 128, 8 * 128])
        self.o_bs = ein("o_bs", [nO, 1, 8 * 128])
        self.o_wo = ein("o_wo", [nO, 128, 12 * 1024])
        self.m_wk = ein("m_wk", [DEPTH, 128, 8 * 512])
        self.m_wv = ein("m_wv", [DEPTH, 128, 8 * 512])
        self.ln_gb = ein("ln_gb", [DEPTH, 2, 1024])
        self.y_out = nc.dram_tensor("y", [ntok, D], F32, kind="ExternalOutput").ap()
        self.X = self.scr("X", [ntok, D], F32)
        self.Xb = [[Buf() for _ in range(NT)] for _ in range(nseq)]
        mk = lambda n: [[Buf() for _ in range(n)] for _ in range(nseq)]
        self.CT = self.scr("CT", [nseq, 128, SEQ], F32); self.CTb = mk(1)
        self.ST = self.scr("ST", [nseq, 128, SEQ], F32); self.STb = mk(1)
        self.H = self.scr("H", [nseq, 4, 128, SEQ], BF16); self.Hb = mk(4)
        self.GA = self.scr("GA", [nseq, 4, 128, SEQ], BF16); self.GAb = mk(4)
        self.QR = self.scr("QR", [nseq, 4, 128, SEQ], BF16); self.QRb = mk(4)
        self.KR = self.scr("KR", [nseq, 2, 64, SEQ], BF16); self.KRb = mk(1)
        self.QI = self.scr("QI", [nseq, 2, 128, SEQ], BF16); self.QIb = mk(2)
        self.V = self.scr("V", [nseq, 128, NT, 64], BF16); self.Vb = mk(1)
        self.WI = self.scr("WI", [nseq, 128, NT, 4], F32); self.WIb = mk(1)
        self.GB = self.scr("GB", [nseq, 8, 64, SEQ], BF16); self.GBb = mk(4)
        self.MQ = self.scr("MQ", [nseq, 4, 128, SEQ], BF16); self.MQb = mk(4)
        self.GM = self.scr("GM", [nseq, 4, 128, SEQ], BF16); self.GMb = mk(4)
        self.U = self.scr("U", [nseq, 8, 128, SEQ], BF16); self.Ub = mk(8)
        self.GC = self.scr("GC", [nseq, 8, 128, SEQ], BF16); self.GCb = mk(8)
        self.VLN = self.scr("VLN", [nseq, SEQ, D], BF16); self.VLNb = mk(NT)
        self.CATA = self.scr("CATA", [nseq, 8, 128, SEQ], BF16); self.CATAb = mk(8)
        self.CATB = self.scr("CATB", [nseq, 8, 64, SEQ], BF16); self.CATBb = mk(NT)
        self.CATM = self.scr("CATM", [nseq, 4, 128, SEQ], BF16); self.CATMb = mk(4)
        self.PS = []
        for i in range(4):
            t = nc.alloc_psum_tensor(f"ps{i}", [128, 1024], F32)
            self.PS.append((t, [Buf(f"ps{i}a", True), Buf(f"ps{i}b", True)]))
        self.psi = 0
        self.consts()
        self.arena_t = nc.alloc_sbuf_tensor("arena", [128, self.ARENA], BF16)
        self.aoff = 0

    ARENA = 90 * 1024

    def scr(self, name, shape, dt):
        kind = "ExternalOutput" if self.debug else "Internal"
        if self.debug:
            self.dbg_names.append(name)
        return self.nc.dram_tensor("scr_" + name, list(shape), dt, kind=kind).ap()

    def phase(self):
        self.S.barrier()
        self.aoff = 0

    def alloc(self, shape, dt):
        n = int(np.prod(shape[1:]))
        ne = n * (2 if dt == F32 or dt == I32 else 1)
        ne = (ne + 15) // 16 * 16
        assert self.aoff + ne <= self.ARENA, ("arena overflow", self.aoff, ne)
        v = self.arena_t[0:shape[0], self.aoff:self.aoff + ne]
        self.aoff += ne
        if dt != BF16:
            v = v.bitcast(dt)
        v = v[:, 0:n]
        if len(shape) == 3:
            v = v.rearrange("p (a b) -> p a b", a=shape[1])
        return T(v, Buf())

    def ring(self, n, shape, dt):
        return [self.alloc(shape, dt) for _ in range(n)]

    def psum(self):
        i = self.psi % 8
        self.psi += 1
        t, bs = self.PS[i // 2]
        h = i % 2
        return T(t[:, h * 512:(h + 1) * 512], bs[h])

    def dma(self, out, in_, reads=(), writes=(), q=None):
        if q is None:
            q = "actd" if str(out.space) == "DRAM" else "spd"
        self.S.op(q, lambda e: e.dma_start(out=out, in_=in_), reads, writes)

    def mm(self, out, lhsT, rhs, start, stop, reads, writes):
        self.S.op("pe", lambda e: e.matmul(out, lhsT=lhsT, rhs=rhs, start=start, stop=stop), reads, writes)

    def act(self, out, in_, func, reads, writes, bias=0.0, scale=1.0):
        self.S.op("act", lambda e: e.activation(out=out, in_=in_, func=func, bias=bias, scale=scale), reads, writes)

    def ts(self, eng, out, in0, s1, s2, op0, op1, reads, writes, accum=None):
        if accum is None:
            if s2 is None:
                self.S.op(eng, lambda e: e.tensor_scalar(out=out, in0=in0, scalar1=s1, scalar2=None, op0=op0), reads, writes)
            else:
                self.S.op(eng, lambda e: e.tensor_scalar(out=out, in0=in0, scalar1=s1, scalar2=s2, op0=op0, op1=op1), reads, writes)
        else:
            self.S.op(eng, lambda e: e.tensor_scalar(out=out, in0=in0, scalar1=s1, scalar2=s2, op0=op0, op1=op1, accum_out=accum), reads, writes)

    def tt(self, eng, out, in0, in1, op, reads, writes):
        self.S.op(eng, lambda e: e.tensor_tensor(out=out, in0=in0, in1=in1, op=op), reads, writes)

    def stt(self, out, in0, scalar, in1, op0, op1, reads, writes):
        self.S.op("dve", lambda e: e.scalar_tensor_tensor(out=out, in0=in0, scalar=scalar, in1=in1, op0=op0, op1=op1), reads, writes)

    def copy(self, eng, out, in_, reads, writes):
        if eng == "act":
            self.S.op("act", lambda e: e.copy(out=out, in_=in_), reads, writes)
        else:
            self.S.op(eng, lambda e: e.tensor_copy(out=out, in_=in_), reads, writes)

    def consts(self):
        nc = self.nc
        al = lambda n, sh, dt: T(nc.alloc_sbuf_tensor(n, sh, dt), Buf(n))
        self.ident32 = al("ident32", [128, 128], F32)
        self.identb = al("identb", [128, 128], BF16)
        self.ident4 = al("ident4", [128, 512], BF16)
        self.ones = al("ones", [128, 128], BF16)
        self.negtri = al("negtri", [128, 128], F32)
        self.tril = al("tril", [128, 128], F32)
        self.pow2 = al("pow2", [128, NIT + 2], F32)
        self.invf = al("invf", [128, 2], F32)
        self.epsc = al("epsc", [128, 1], F32)
        self.dma(self.ident32.a[:], self.c_ident, writes=[self.ident32.b])
        self.dma(self.negtri.a[:], self.c_negtri, writes=[self.negtri.b])
        self.dma(self.tril.a[:], self.c_tril, writes=[self.tril.b])
        self.dma(self.pow2.a[:], self.c_pow2, writes=[self.pow2.b])
        self.dma(self.invf.a[:], self.c_invf, writes=[self.invf.b])
        self.copy("dve", self.identb.a[:], self.ident32.a[:], [self.ident32.b], [self.identb.b])
        for i in range(4):
            self.copy("dve", self.ident4.a[:, i * 128:(i + 1) * 128], self.ident32.a[:], [self.ident32.b], [self.ident4.b])
        self.S.op("dve", lambda e: e.memset(self.ones.a[:], 1.0), (), [self.ones.b])
        self.S.op("dve", lambda e: e.memset(self.epsc.a[:], LN_EPS), (), [self.epsc.b])

    def rope_tables(self, s):
        self.phase()
        posi = self.alloc([128, SEQ], I32)
        ang = self.alloc([128, SEQ], F32)
        t1 = self.alloc([128, SEQ], F32)
        t2 = self.alloc([128, SEQ], F32)
        self.dma(posi.a, self.pos_in[s:s + 1, :].partition_broadcast(128), writes=[posi.b])
        self.copy("dve", ang.a, posi.a, [posi.b], [ang.b])
        self.ts("dve", ang.a, ang.a, self.invf.a[:, 0:1], None, ALU.mult, None, [ang.b, self.invf.b], [ang.b])
        twopi = float(np.float32(2 * np.pi))
        magic = 12582912.0
        for (shift, dst, dstb, signed) in ((np.pi / 2, self.CT, self.CTb, False), (0.0, self.ST, self.STb, True)):
            self.ts("dve", t1.a, ang.a, float(shift), 1.0 / twopi, ALU.add, ALU.mult, [ang.b], [t1.b])
            self.ts("dve", t1.a, t1.a, magic, None, ALU.add, None, [t1.b], [t1.b])
            self.ts("dve", t1.a, t1.a, magic, None, ALU.subtract, None, [t1.b], [t1.b])
            self.stt(t2.a, t1.a, -twopi, ang.a, ALU.mult, ALU.add, [t1.b, ang.b], [t2.b])
            self.ts("dve", t2.a, t2.a, float(shift), 3.1415925, ALU.add, ALU.min, [t2.b], [t2.b])
            self.ts("dve", t2.a, t2.a, -3.1415925, None, ALU.max, None, [t2.b], [t2.b])
            self.act(t1.a, t2.a, AF.Sin, [t2.b], [t1.b])
            if signed:
                self.ts("dve", t1.a, t1.a, self.invf.a[:, 1:2], None, ALU.mult, None, [t1.b, self.invf.b], [t1.b])
            self.dma(dst[s], t1.a, [t1.b], [dstb[s][0]])

    def load_xT(self, layer, s, half, xT):
        xsrc = self.x_in if layer == 0 else self.X
        xl = self.ring(2, [128, D], F32)
        xb = self.ring(2, [128, D], BF16)
        for i in range(16):
            ti = half * 16 + i
            row = s * SEQ + ti * 128
            a, b = xl[i % 2], xb[i % 2]
            self.dma(a.a, xsrc[row:row + 128, :], [self.Xb[s][ti]], [a.b])
            self.copy("act", b.a, a.a, [a.b], [b.b])
            ps = self.psum()
            pv = ps.a.bitcast(BF16)
            for k in range(8):
                self.S.op("pe", lambda e, k=k, pv=pv, b=b: e.transpose(pv[:, k * 128:(k + 1) * 128], b.a[:, k * 128:(k + 1) * 128], self.identb.a[:]),
                          [b.b, self.identb.b], [ps.b])
            self.copy("dve", xT.a[:, :, i * 128:(i + 1) * 128], pv.rearrange("p (k t) -> p k t", k=8), [ps.b], [xT.b])

    def fm_chunk(self, wsrc, xT, wl, wb, ci):
        a, b = wl[ci % 2], wb[ci % 2]
        self.dma(a.a, wsrc, (), [a.b])
        self.copy("pool", b.a, a.a, [a.b], [b.b])
        outs = []
        for tc in range(4):
            ps = self.psum()
            for k in range(8):
                self.mm(ps.a, b.a[:, k * 128:(k + 1) * 128], xT.a[:, k, tc * 512:(tc + 1) * 512], k == 0, k == 7, [b.b, xT.b], [ps.b])
            outs.append(ps)
        return outs

    def inproj_even(self, layer, s):
        j = layer // 2
        for half in range(2):
            self.phase()
            t0 = half * 2048
            xT = self.alloc([128, 8, 2048], BF16)
            self.load_xT(layer, s, half, xT)
            cts = self.alloc([128, 2048], F32)
            sts = self.alloc([128, 2048], F32)
            self.dma(cts.a, self.CT[s, :, t0:t0 + 2048], [self.CTb[s][0]], [cts.b])
            self.dma(sts.a, self.ST[s, :, t0:t0 + 2048], [self.STb[s][0]], [sts.b])
            wl = self.ring(2, [128, 1024], F32)
            wb = self.ring(2, [128, 1024], BF16)
            stage = self.ring(3, [128, 2048], BF16)
            tmpa = self.ring(2, [128, 2048], F32)
            tmpb = self.ring(2, [128, 512], F32)
            st_i = [0]

            def next_stage():
                st_i[0] += 1
                return stage[st_i[0] % 3]
            ci = [0]

            def chunk(idx):
                r = self.fm_chunk(self.e_wfm[j, idx], xT, wl, wb, ci[0])
                ci[0] += 1
                return r
            sl = lambda tc: slice(tc * 512, (tc + 1) * 512)
            SEC = os.environ.get('KSEC', 'Asbpt')
            for i in range(4 if 'A' in SEC else 0):
                ta = tmpa[i % 2]
                for tc, ps in enumerate(chunk(4 + i)):
                    self.act(ta.a[:, sl(tc)], ps.a, AF.Sigmoid, [ps.b], [ta.b])
                sg = next_stage()
                for tc, ps in enumerate(chunk(0 + i)):
                    self.tt("dve", sg.a[:, sl(tc)], ps.a, ta.a[:, sl(tc)], ALU.mult, [ps.b, ta.b], [sg.b])
                self.dma(self.H[s, i, :, t0:t0 + 2048], sg.a, [sg.b], [self.Hb[s][i]])
            simple = [(8 + i, AF.Silu, self.GA, self.GAb, i) for i in range(4)]
            simple += [(26 + i, AF.Copy, self.MQ, self.MQb, i) for i in range(4)]
            simple += [(30 + i, AF.Silu, self.GM, self.GMb, i) for i in range(4)]
            for (idx, fn, dst, dstb, di) in (simple if 's' in SEC else []):
                sg = next_stage()
                for tc, ps in enumerate(chunk(idx)):
                    self.act(sg.a[:, sl(tc)], ps.a, fn, [ps.b], [sg.b])
                self.dma(dst[s, di, :, t0:t0 + 2048], sg.a, [sg.b], [dstb[s][di]])
            for i in range(4 if 'b' in SEC else 0):
                sg = next_stage()
                for tc, ps in enumerate(chunk(34 + i)):
                    self.act(sg.a[:, sl(tc)], ps.a, AF.Silu, [ps.b], [sg.b])
                self.dma(self.GB[s, 2 * i:2 * i + 2, :, t0:t0 + 2048].rearrange("h d t -> (h d) t"), sg.a, [sg.b], [self.GBb[s][i]])
            pairs = [(12 + i, 16 + i, self.QR[s, i, :, t0:t0 + 2048], self.QRb[s][i]) for i in range(4)]
            pairs += [(20, 21, self.KR[s, :, :, t0:t0 + 2048].rearrange("a d t -> (a d) t"), self.KRb[s][0])]
            pairs += [(22 + i, 24 + i, self.QI[s, i, :, t0:t0 + 2048], self.QIb[s][i]) for i in range(2)]
            KP = os.environ.get('KP', 'qki')
            pairs = [p for p, tag in zip(pairs, 'qqqqkii') if tag in KP]
            for pi, (i0, i1, dst, dstb) in enumerate(pairs if 'p' in SEC else []):
                ta = tmpa[pi % 2]
                for tc, ps in enumerate(chunk(i0)):
                    self.tt("dve", ta.a[:, sl(tc)], ps.a, cts.a[:, sl(tc)], ALU.mult, [ps.b, cts.b], [ta.b])
                sg = next_stage()
                for tc, ps in enumerate(chunk(i1)):
                    tb = tmpb[tc % 2]
                    self.tt("dve", tb.a, ps.a, sts.a[:, sl(tc)], ALU.mult, [ps.b, sts.b], [tb.b])
                    self.tt(os.environ.get("KPE", "pool"), sg.a[:, sl(tc)], tb.a, ta.a[:, sl(tc)], ALU.add, [tb.b, ta.b], [sg.b])
                self.dma(dst, sg.a, [sg.b], [dstb])
            if 't' not in SEC:
                continue
            wt32 = self.alloc([128, 8 * 68], F32)
            wtb = self.alloc([128, 8 * 68], BF16)
            self.dma(wt32.a, self.e_wtm[j], (), [wt32.b])
            self.copy("pool", wtb.a, wt32.a, [wt32.b], [wtb.b])
            vst = self.alloc([128, 16, 64], BF16)
            wst = self.alloc([128, 16, 4], F32)
            for i in range(16):
                ps = self.psum()
                for k in range(8):
                    self.mm(ps.a[:, 0:68], xT.a[:, k, i * 128:(i + 1) * 128], wtb.a[:, k * 68:(k + 1) * 68], k == 0, k == 7, [xT.b, wtb.b], [ps.b])
                KT = os.environ.get('KT', 'vwVW')
                if 'v' in KT:
                    self.copy("act", vst.a[:, i, :], ps.a[:, 0:64], [ps.b], [vst.b])
                if 'w' in KT:
                    self.ts("dve", wst.a[:, i, :], ps.a[:, 64:68], 1.0 / 16.0, None, ALU.mult, None, [ps.b], [wst.b])
            if 'V' in KT:
                self.dma(self.V[s, :, half * 16:half * 16 + 16, :], vst.a, [vst.b], [self.Vb[s][0]])
            if 'W' in KT:
                self.dma(self.WI[s, :, half * 16:half * 16 + 16, :], wst.a, [wst.b], [self.WIb[s][0]])

    def conv_branch(self, layer, s):
        j = layer // 2
        self.phase()
        vec = self.alloc([128, 16], F32)
        self.dma(vec.a, self.e_vec[j], (), [vec.b])
        cw = self.alloc([128, 4 * 31], F32)
        self.dma(cw.a, self.e_convw[j], (), [cw.b])
        dg = self.alloc([128, 4 * 31 * 128], BF16)
        for i in range(4 * 31):
            self.act(dg.a[:, i * 128:(i + 1) * 128], self.ident32.a[:], AF.Copy, [self.ident32.b, cw.b], [dg.b], scale=cw.a[:, i:i + 1])
        p32 = self.alloc([128, 4 * 512], F32)
        pw = self.alloc([128, 4 * 512], BF16)
        self.dma(p32.a, self.e_pw2[j], (), [p32.b])
        self.copy("pool", pw.a, p32.a, [p32.b], [pw.b])
        hs = [self.alloc([128, 32 + SEQ], BF16) for _ in range(4)]
        for c in range(4):
            self.S.op("pool", lambda e, c=c: e.memset(hs[c].a[:, 0:32], 0.0), (), [hs[c].b])
            self.dma(hs[c].a[:, 32:32 + SEQ], self.H[s, c], [self.Hb[s][c]], [hs[c].b])
        y32 = self.ring(2, [128, 4, 512], F32)
        ybf = self.ring(2, [128, 4, 512], BF16)
        ysq = self.ring(2, [128, 4, 512], BF16)
        st = self.ring(2, [128, 4, 512], F32)
        sb = self.ring(2, [128, 4, 512], BF16)
        ga = self.ring(2, [128, 4, 512], BF16)
        yo = self.ring(2, [128, 4, 512], BF16)
        for tc in range(8):
            t0 = tc * 512
            Y, YB, YQ, ST, SB, G, YO = (r[tc % 2] for r in (y32, ybf, ysq, st, sb, ga, yo))
            self.dma(G.a, self.GA[s, :, :, t0:t0 + 512].rearrange("c p t -> p c t"), [self.GAb[s][c] for c in range(4)], [G.b])
            for c in range(4):
                ps = self.psum()
                for k in range(31):
                    self.mm(ps.a, dg.a[:, (c * 31 + k) * 128:(c * 31 + k + 1) * 128], hs[c].a[:, 2 + k + t0:2 + k + t0 + 512],
                            k == 0, k == 30, [dg.b, hs[c].b], [ps.b])
                self.act(Y.a[:, c, :], ps.a, AF.Identity, [ps.b, vec.b], [Y.b], bias=vec.a[:, c:c + 1])
                self.copy("pool", YB.a[:, c, :], Y.a[:, c, :], [Y.b], [YB.b])
                self.act(YQ.a[:, c, :], Y.a[:, c, :], AF.Square, [Y.b], [YQ.b])
            p1, p2 = self.psum(), self.psum()
            for c in range(4):
                self.mm(p1.a, self.ones.a[:], YB.a[:, c, :], c == 0, c == 3, [self.ones.b, YB.b], [p1.b])
            for c in range(4):
                self.mm(p2.a, self.ones.a[:], YQ.a[:, c, :], c == 0, c == 3, [self.ones.b, YQ.b], [p2.b])
            mean, var, rstd = ST.a[:, 0, :], ST.a[:, 1, :], ST.a[:, 2, :]
            self.ts("dve", mean, p1.a, 1.0 / 512, None, ALU.mult, None, [p1.b], [ST.b])
            self.tt("dve", var, mean, mean, ALU.mult, [ST.b], [ST.b])
            self.stt(var, p2.a, 1.0 / 512, var, ALU.mult, ALU.subtract, [p2.b, ST.b], [ST.b])
            self.act(rstd, var, AF.Ln, [ST.b, self.epsc.b], [ST.b], bias=self.epsc.a[:, 0:1])
            self.act(rstd, rstd, AF.Exp, [ST.b], [ST.b], scale=-0.5)
            for c in range(4):
                self.tt("dve", ST.a[:, 3, :], Y.a[:, c, :], mean, ALU.subtract, [Y.b, ST.b], [ST.b])
                self.tt("pool", Y.a[:, c, :], ST.a[:, 3, :], rstd, ALU.mult, [ST.b], [Y.b])
                self.act(SB.a[:, c, :], Y.a[:, c, :], AF.Silu, [Y.b, vec.b], [SB.b], bias=vec.a[:, 8 + c:9 + c], scale=vec.a[:, 4 + c:5 + c])
            for n in range(4):
                ps = self.psum()
                for c in range(4):
                    self.mm(ps.a, pw.a[:, c * 512 + n * 128:c * 512 + (n + 1) * 128], SB.a[:, c, :], c == 0, c == 3, [pw.b, SB.b], [ps.b])
                self.stt(YO.a[:, n, :], ps.a, vec.a[:, 12 + n:13 + n], G.a[:, n, :], ALU.add, ALU.mult, [ps.b, vec.b, G.b], [YO.b])
            self.dma(self.CATA[s, 0:4, :, t0:t0 + 512].rearrange("c p t -> p c t"), YO.a, [YO.b], [self.CATAb[s][c] for c in range(4)])

    def dsa(self, layer, s):
        self.phase()
        QRs = self.alloc([128, 4, SEQ], BF16)
        KK = self.alloc([128, SEQ], BF16)
        KI2 = self.alloc([128, SEQ], BF16)
        QIs = self.alloc([128, 2, SEQ], BF16)
        Vs = self.alloc([128, NT, 64], BF16)
        WIs = self.alloc([128, NT, 4], F32)
        for c in range(4):
            self.dma(QRs.a[:, c, :], self.QR[s, c], self.QRb[s], [QRs.b])
        for hh in range(2):
            self.dma(KK.a[hh * 64:(hh + 1) * 64, :], self.KR[s, 0], self.KRb[s], [KK.b])
            self.dma(KI2.a[hh * 64:(hh + 1) * 64, :], self.KR[s, 1], self.KRb[s], [KI2.b])
        for c in range(2):
            self.dma(QIs.a[:, c, :], self.QI[s, c], self.QIb[s], [QIs.b])
        self.dma(Vs.a, self.V[s], self.Vb[s], [Vs.b])
        self.dma(WIs.a, self.WI[s], self.WIb[s], [WIs.b])
        score = self.ring(2, [128, SEQ], F32)
        mb = self.ring(4, [128, SEQ], BF16)
        junkD = self.alloc([128, SEQ], BF16)
        junkA = self.alloc([128, SEQ], BF16)
        rl = self.ring(3, [128, 256], F32)
        sm = self.ring(4, [128, 8 + NIT + 2], F32)
        pT = self.ring(2, [128, 1024], BF16)
        gb = self.ring(2, [64, 8, 128], BF16)
        rs = self.ring(2, [64, 1024], F32)
        yb = self.ring(2, [64, 8, 128], BF16)
        (tA, bA), (tO, bO), (tS, bS), (tI, bI) = self.PS
        nqb = int(os.environ.get('KQB', NT))

        def stageA1(qb):
            q0 = qb * 128
            L = q0 + 128
            SC, SM = score[qb % 2], sm[qb % 4]
            for c in range((L + 255) // 256):
                k0 = c * 256
                n = min(256, L - k0)
                col = lambda h: (h % 2) * 512 + (h // 2) * 256
                for h in range(4):
                    pr = slice((h % 2) * 64, (h % 2) * 64 + 64)
                    self.mm(tI[:, col(h):col(h) + n], QIs.a[pr, h // 2, q0:q0 + 128], KI2.a[pr, k0:k0 + n], True, True,
                            [QIs.b, KI2.b], [bI[h % 2]])
                dst = SC.a[:, k0:k0 + n]
                self.ts("dve", dst, tI[:, 0:n], 0.0, WIs.a[:, qb, 0:1], ALU.max, ALU.mult, [bI[0], WIs.b], [SC.b])
                for h in range(1, 4):
                    r = rl[h - 1]
                    self.act(r.a[:, 0:n], tI[:, col(h):col(h) + n], AF.Relu, [bI[h % 2]], [r.b])
                    self.stt(dst, r.a[:, 0:n], WIs.a[:, qb, h:h + 1], dst, ALU.mult, ALU.add, [r.b, WIs.b, SC.b], [SC.b])
            if qb >= 2:
                self.S.op("dve", lambda e: e.tensor_reduce(out=SM.a[:, 0:1], in_=SC.a[:, 0:L], axis=AX.X, op=ALU.max), [SC.b], [SM.b])
                self.S.op("dve", lambda e: e.tensor_reduce(out=SM.a[:, 1:2], in_=SC.a[:, 0:L], axis=AX.X, op=ALU.min), [SC.b], [SM.b])
            self.tt("dve", SC.a[:, q0:L], SC.a[:, q0:L], self.negtri.a[:], ALU.add, [SC.b, self.negtri.b], [SC.b])
            if qb >= 2:
                hi, lo, W0, mid = (SM.a[:, i:i + 1] for i in range(4))
                self.tt("dve", W0, hi, lo, ALU.subtract, [SM.b], [SM.b])
                self.ts("dve", SM.a[:, 8:8 + NIT + 2], self.pow2.a[:], W0, None, ALU.mult, None, [self.pow2.b, SM.b], [SM.b])
                self.tt("dve", mid, lo, SM.a[:, 9:10], ALU.add, [SM.b], [SM.b])
            else:
                self.S.op("dve", lambda e: e.memset(SM.a[:, 6:7], -1e29), (), [SM.b])

        def bisect_dve(qb):
            L = qb * 128 + 128
            SC, SM = score[qb % 2], sm[qb % 4]
            mid, cnt, tt_, thr = SM.a[:, 3:4], SM.a[:, 4:5], SM.a[:, 5:6], SM.a[:, 6:7]
            for k in range(1, NIT + 1):
                self.ts("dve", junkD.a[:, 0:L], SC.a[:, 0:L], mid, None, ALU.is_ge, ALU.add, [SC.b, SM.b], [junkD.b, SM.b], accum=cnt)
                if k < NIT:
                    self.ts("dve", tt_, cnt, 256.0, 0.5, ALU.is_ge, ALU.subtract, [SM.b], [SM.b])
                    self.stt(mid, tt_, SM.a[:, 8 + k:9 + k], mid, ALU.mult, ALU.add, [SM.b], [SM.b])
                else:
                    self.ts("dve", tt_, cnt, 256.0, 1.0, ALU.is_ge, ALU.subtract, [SM.b], [SM.b])
                    self.stt(thr, tt_, SM.a[:, 8 + k:9 + k], mid, ALU.mult, ALU.add, [SM.b], [SM.b])

        def stageA3(qb):
            L = qb * 128 + 128
            SC, SM, MB = score[qb % 2], sm[qb % 4], mb[qb % 4]
            self.ts("dve", MB.a[:, 0:L], SC.a[:, 0:L], SM.a[:, 6:7], NEG, ALU.is_lt, ALU.mult, [SC.b, SM.b], [MB.b])

        osb = self.ring(2, [64, 1024], F32)

        def stageB_steps(qb):
            q0 = qb * 128
            MB = mb[qb % 4]
            G, RS, YB, OS = (r[qb % 2] for r in (gb, rs, yb, osb))
            nsc = qb + 1
            steps = []

            def qk_exp(sc):
                P = pT[sc % 2]
                for par in range(2):
                    pr = slice(par * 64, par * 64 + 64)
                    o = tA[:, par * 512:(par + 1) * 512]
                    self.mm(o.rearrange("p (a b) -> p a b", a=4), KK.a[pr, sc * 128:(sc + 1) * 128], QRs.a[pr, :, q0:q0 + 128], True, False, [KK.b, QRs.b], [bA[par]])
                    self.mm(o, MB.a[:, sc * 128:(sc + 1) * 128], self.ident4.a[:], False, True, [MB.b, self.ident4.b], [bA[par]])
                for par in range(2):
                    cs = slice(par * 512, (par + 1) * 512)
                    self.act(P.a[:, cs], tA[:, cs], AF.Exp, [bA[par]], [P.b], scale=0.125)

            def chunk(sc):
                if sc == 0:
                    self.dma(G.a, self.GB[s, :, :, q0:q0 + 128].rearrange("h d t -> d h t"), self.GBb[s], [G.b])
                    qk_exp(0)
                if sc + 1 < nsc:
                    qk_exp(sc + 1)
                P = pT[sc % 2]
                for par in range(2):
                    cs = slice(par * 512, (par + 1) * 512)
                    self.mm(tO[0:64, cs], Vs.a[:, sc, :], P.a[:, cs], sc == 0, sc == nsc - 1, [Vs.b, P.b], [bO[par]])
                    self.mm(tS[0:64, cs], self.ones.a[:, 0:64], P.a[:, cs], sc == 0, sc == nsc - 1, [self.ones.b, P.b], [bS[par]])
                if sc == nsc - 1:
                    self.act(RS.a, tS[0:64, :], AF.Ln, [bS[0], bS[1]], [RS.b])
                    self.act(RS.a, RS.a, AF.Exp, [RS.b], [RS.b], scale=-1.0)
                    self.copy("act", OS.a, tO[0:64, :], [bO[0], bO[1]], [OS.b])
                    self.tt("pool", OS.a, OS.a, RS.a, ALU.mult, [OS.b, RS.b], [OS.b])
                    ybv = YB.a.rearrange("d (pair par) t -> d par pair t", par=2)
                    gv = G.a.rearrange("d (pair par) t -> d par pair t", par=2)
                    for par in range(2):
                        self.tt("pool", ybv[:, par], OS.a[:, par * 512:(par + 1) * 512].rearrange("d (pair t) -> d pair t", pair=4), gv[:, par],
                                ALU.mult, [OS.b, G.b], [YB.b])
                    self.dma(self.CATB[s, :, :, q0:q0 + 128].rearrange("h d t -> d h t"), YB.a, [YB.b], [self.CATBb[s][qb]])
            for sc in range(nsc):
                steps.append(lambda sc=sc: chunk(sc))
            return steps

        def bisect_act_steps(qb):
            L = qb * 128 + 128
            SC, SM = score[qb % 2], sm[qb % 4]
            mid, sg, g, thr = SM.a[:, 3:4], SM.a[:, 4:5], SM.a[:, 5:6], SM.a[:, 6:7]

            def it(k):
                self.S.op("act", lambda e: e.activation(out=junkA.a[:, 0:L], in_=SC.a[:, 0:L], func=AF.Sign, bias=mid, scale=-1.0, accum_out=sg),
                          [SC.b, SM.b], [junkA.b, SM.b])
                self.act(g, sg, AF.Sign, [SM.b], [SM.b], bias=float(L - 511), scale=-1.0)
                self.act(mid, g, AF.Identity, [SM.b], [SM.b], bias=mid, scale=SM.a[:, 9 + k:10 + k])
                if k == NIT:
                    self.act(thr, SM.a[:, 9 + NIT:10 + NIT], AF.Identity, [SM.b], [SM.b], bias=mid, scale=-1.0)
            return [lambda k=k: it(k) for k in range(1, NIT + 1)]

        def interleave(chunks, iters):
            nC, nI = len(chunks), len(iters)
            head = (nC * 45 + 99) // 100 if nI else nC
            for c in chunks[:head]:
                c()
            rest = chunks[head:]
            ci = ii = 0
            while ci < len(rest) or ii < nI:
                if ii < nI and (ci >= len(rest) or ii * max(len(rest), 1) <= ci * nI):
                    iters[ii]()
                    ii += 1
                else:
                    rest[ci]()
                    ci += 1

        npair = nqb // 2
        for r in range(npair + 1):
            iters = []
            if r < npair:
                a2, b2 = 2 * r, 2 * r + 1
                stageA1(a2)
                stageA1(b2)
                if a2 >= 2:
                    bisect_dve(a2)
                    iters = bisect_act_steps(b2)
            chunks = []
            if r >= 1:
                chunks = stageB_steps(2 * r - 2) + stageB_steps(2 * r - 1)
            interleave(chunks, iters)
            if r < npair:
                stageA3(2 * r)
                stageA3(2 * r + 1)

    def mem_attn(self, layer, s):
        self.phase()
        ml = self.ring(2, [128, D], F32)
        mbf = self.ring(2, [128, D], BF16)
        memT = self.alloc([128, 8, MEM], BF16)
        for i in range(2):
            a, b = ml[i], mbf[i]
            self.dma(a.a, self.mem_in[s * MEM + i * 128:s * MEM + (i + 1) * 128, :], (), [a.b])
            self.copy("act", b.a, a.a, [a.b], [b.b])
            ps = self.psum()
            pv = ps.a.bitcast(BF16)
            for k in range(8):
                self.S.op("pe", lambda e, k=k, pv=pv, b=b: e.transpose(pv[:, k * 128:(k + 1) * 128], b.a[:, k * 128:(k + 1) * 128], self.identb.a[:]),
                          [b.b, self.identb.b], [ps.b])
            self.copy("dve", memT.a[:, :, i * 128:(i + 1) * 128], pv.rearrange("p (k t) -> p k t", k=8), [ps.b], [memT.b])
        w32 = self.alloc([128, 8 * 512], F32)
        wk = self.alloc([128, 8 * 512], BF16)
        wv = self.alloc([128, 8 * 512], BF16)
        self.dma(w32.a, self.m_wk[layer], (), [w32.b])
        self.copy("pool", wk.a, w32.a, [w32.b], [wk.b])
        self.dma(w32.a, self.m_wv[layer], [], [w32.b])
        self.copy("pool", wv.a, w32.a, [w32.b], [wv.b])
        kmT = self.alloc([128, 4, MEM], BF16)
        vm = self.alloc([128, 2, 512], BF16)
        for h in range(4):
            ps = self.psum()
            for k in range(8):
                self.mm(ps.a[:, 0:MEM], wk.a[:, k * 512 + h * 128:k * 512 + (h + 1) * 128], memT.a[:, k, :], k == 0, k == 7, [wk.b, memT.b], [ps.b])
            self.copy("act", kmT.a[:, h, :], ps.a[:, 0:MEM], [ps.b], [kmT.b])
        for mc in range(2):
            ps = self.psum()
            for k in range(8):
                self.mm(ps.a, memT.a[:, k, mc * 128:(mc + 1) * 128], wv.a[:, k * 512:(k + 1) * 512], k == 0, k == 7, [wv.b, memT.b], [ps.b])
            self.copy("act", vm.a[:, mc, :], ps.a, [ps.b], [vm.b])
        mq = self.ring(2, [128, 4, 512], BF16)
        gm = self.ring(2, [128, 4, 512], BF16)
        pT = self.ring(2, [128, 2, 512], BF16)
        rs = self.ring(2, [128, 512], F32)
        yo = self.ring(2, [128, 4, 512], BF16)
        scale = 128 ** -0.5
        for tc in range(8):
            t0 = tc * 512
            Q, G, YO = mq[tc % 2], gm[tc % 2], yo[tc % 2]
            self.dma(Q.a, self.MQ[s, :, :, t0:t0 + 512].rearrange("c p t -> p c t"), self.MQb[s], [Q.b])
            self.dma(G.a, self.GM[s, :, :, t0:t0 + 512].rearrange("c p t -> p c t"), self.GMb[s], [G.b])
            for h in range(4):
                P, R = pT[h % 2], rs[h % 2]
                for mc in range(2):
                    ps = self.psum()
                    self.mm(ps.a, kmT.a[:, h, mc * 128:(mc + 1) * 128], Q.a[:, h, :], True, True, [kmT.b, Q.b], [ps.b])
                    self.act(P.a[:, mc, :], ps.a, AF.Exp, [ps.b], [P.b], scale=scale)
                po, pS = self.psum(), self.psum()
                for mc in range(2):
                    self.mm(po.a, vm.a[:, mc, h * 128:(h + 1) * 128], P.a[:, mc, :], mc == 0, mc == 1, [vm.b, P.b], [po.b])
                for mc in range(2):
                    self.mm(pS.a, self.ones.a[:], P.a[:, mc, :], mc == 0, mc == 1, [self.ones.b, P.b], [pS.b])
                self.act(R.a, pS.a, AF.Ln, [pS.b], [R.b])
                self.act(R.a, R.a, AF.Exp, [R.b], [R.b], scale=-1.0)
                self.tt("dve", R.a, po.a, R.a, ALU.mult, [po.b, R.b], [R.b])
                self.tt("pool", YO.a[:, h, :], R.a, G.a[:, h, :], ALU.mult, [R.b, G.b], [YO.b])
            self.dma(self.CATM[s, :, :, t0:t0 + 512].rearrange("c p t -> p c t"), YO.a, [YO.b], self.CATMb[s])

    def inproj_odd(self, layer, s):
        j = layer // 2
        for half in range(2):
            self.phase()
            t0 = half * 2048
            xT = self.alloc([128, 8, 2048], BF16)
            self.load_xT(layer, s, half, xT)
            wl = self.ring(2, [128, 1024], F32)
            wb = self.ring(2, [128, 1024], BF16)
            stage = self.ring(3, [128, 2048], BF16)
            sl = lambda tc: slice(tc * 512, (tc + 1) * 512)
            specs = [(i, AF.Gelu_apprx_tanh, self.U, self.Ub, i) for i in range(8)]
            specs += [(8 + i, AF.Silu, self.GC, self.GCb, i) for i in range(8)]
            specs += [(16 + i, AF.Copy, self.MQ, self.MQb, i) for i in range(4)]
            specs += [(20 + i, AF.Silu, self.GM, self.GMb, i) for i in range(4)]
            for ci, (idx, fn, dst, dstb, di) in enumerate(specs):
                sg = stage[ci % 3]
                for tc, ps in enumerate(self.fm_chunk(self.o_wfm[j, idx], xT, wl, wb, ci)):
                    self.act(sg.a[:, sl(tc)], ps.a, fn, [ps.b], [sg.b])
                self.dma(dst[s, di, :, t0:t0 + 2048], sg.a, [sg.b], [dstb[s][di]])
            vg = self.alloc([128, 2048], F32)
            self.dma(vg.a[:, 0:1024], self.o_vln[j, 0:1, :].partition_broadcast(128), (), [vg.b])
            self.dma(vg.a[:, 1024:2048], self.o_vln[j, 1:2, :].partition_broadcast(128), (), [vg.b])
            v32 = self.ring(2, [128, 1024], F32)
            vo = self.ring(2, [128, 1024], BF16)
            stt_ = self.ring(2, [128, 16], F32)
            w32 = self.alloc([128, 8, 512], F32)
            wts = [self.alloc([128, 8, 512], BF16), self.alloc([128, 8, 512], BF16)]
            for hh in range(2):
                self.dma(w32.a, self.o_wtm[j].rearrange("p (k n) -> p k n", k=8)[:, :, hh * 512:(hh + 1) * 512], (), [w32.b])
                self.copy("pool", wts[hh].a, w32.a, [w32.b], [wts[hh].b])
            for i in range(16):
                ti = half * 16 + i
                Vt, VO, SS = v32[i % 2], vo[i % 2], stt_[i % 2]
                for hh in range(2):
                    ps = self.psum()
                    for k in range(8):
                        self.mm(ps.a, xT.a[:, k, i * 128:(i + 1) * 128], wts[hh].a[:, k, :], k == 0, k == 7, [xT.b, wts[hh].b], [ps.b])
                    self.act(Vt.a[:, hh * 512:(hh + 1) * 512], ps.a, AF.Gelu_apprx_tanh, [ps.b], [Vt.b])
                self.layernorm(Vt, SS, vg, 0, VO.a, VO.b)
                row = ti * 128
                self.dma(self.VLN[s, row:row + 128, :], VO.a, [VO.b], [self.VLNb[s][ti]])

    def layernorm(self, Z, SS, gb, goff, out, outb):
        for hh in range(2):
            self.S.op("dve", lambda e, hh=hh: e.bn_stats(out=SS.a[:, hh * 6:(hh + 1) * 6], in_=Z.a[:, hh * 512:(hh + 1) * 512]), [Z.b], [SS.b])
        self.S.op("dve", lambda e: e.bn_aggr(out=SS.a[:, 12:14], in_=SS.a[:, 0:12]), [SS.b], [SS.b])
        self.act(SS.a[:, 14:15], SS.a[:, 13:14], AF.Ln, [SS.b, self.epsc.b], [SS.b], bias=self.epsc.a[:, 0:1])
        self.act(SS.a[:, 14:15], SS.a[:, 14:15], AF.Exp, [SS.b], [SS.b], scale=-0.5)
        self.ts("dve", Z.a, Z.a, SS.a[:, 12:13], SS.a[:, 14:15], ALU.subtract, ALU.mult, [Z.b, SS.b], [Z.b])
        self.tt("dve", Z.a, Z.a, gb.a[:, goff:goff + 1024], ALU.mult, [Z.b, gb.b], [Z.b])
        self.tt("pool", out, Z.a, gb.a[:, goff + 1024:goff + 2048], ALU.add, [Z.b, gb.b], [outb])

    def sgu(self, layer, s):
        j = layer // 2
        self.phase()
        w32 = self.alloc([128, 8, 128], F32)
        wsb = self.alloc([128, 8, 128], BF16)
        self.dma(w32.a, self.o_wsT[j].rearrange("p (g t) -> p g t", g=8), (), [w32.b])
        for g in range(8):
            self.tt("dve", wsb.a[:, g, :], w32.a[:, g, :], self.tril.a[:], ALU.mult, [w32.b, self.tril.b], [wsb.b])
        bsb = self.alloc([128, 8, 128], F32)
        self.dma(bsb.a.rearrange("p g t -> p (g t)"), self.o_bs[j].partition_broadcast(128), (), [bsb.b])
        vt = self.ring(2, [128, 4, 1024], BF16)
        uu = self.ring(2, [128, 8, 512], BF16)
        gc = self.ring(2, [128, 8, 512], BF16)
        tm = self.ring(2, [128, 512], F32)
        yo = self.ring(2, [128, 8, 512], BF16)
        for tc in range(8):
            t0 = tc * 512
            Vt, Uu, Gc, YO = vt[tc % 2], uu[tc % 2], gc[tc % 2], yo[tc % 2]
            self.dma(Vt.a, self.VLN[s, t0:t0 + 512, :].rearrange("(c p) n -> p c n", p=128), self.VLNb[s][tc * 4:tc * 4 + 4], [Vt.b])
            self.dma(Uu.a, self.U[s, :, :, t0:t0 + 512].rearrange("c p t -> p c t"), self.Ub[s], [Uu.b])
            self.dma(Gc.a, self.GC[s, :, :, t0:t0 + 512].rearrange("c p t -> p c t"), self.GCb[s], [Gc.b])
            for g in range(8):
                ps = self.psum()
                for c in range(4):
                    self.mm(ps.a[:, c * 128:(c + 1) * 128], Vt.a[:, c, g * 128:(g + 1) * 128], wsb.a[:, g, :], True, True, [Vt.b, wsb.b], [ps.b])
                Tm = tm[g % 2]
                self.tt("dve", Tm.a.rearrange("p (c t) -> p c t", c=4), ps.a.rearrange("p (c t) -> p c t", c=4),
                        bsb.a[:, g:g + 1, :].to_broadcast([128, 4, 128]), ALU.add, [ps.b, bsb.b], [Tm.b])
                self.tt("dve", Tm.a, Tm.a, Uu.a[:, g, :], ALU.mult, [Tm.b, Uu.b], [Tm.b])
                self.tt("pool", YO.a[:, g, :], Tm.a, Gc.a[:, g, :], ALU.mult, [Tm.b, Gc.b], [YO.b])
            self.dma(self.CATA[s, :, :, t0:t0 + 512].rearrange("c p t -> p c t"), YO.a, [YO.b], self.CATAb[s])

    def outproj(self, layer, s):
        j = layer // 2
        even = layer % 2 == 0
        self.phase()
        last = layer == self.layers - 1
        xsrc = self.x_in if layer == 0 else self.X
        xdst = self.y_out if last else self.X
        gb = self.alloc([128, 2048], F32)
        self.dma(gb.a[:, 0:1024], self.ln_gb[layer, 0:1, :].partition_broadcast(128), (), [gb.b])
        self.dma(gb.a[:, 1024:2048], self.ln_gb[layer, 1:2, :].partition_broadcast(128), (), [gb.b])
        w32 = self.alloc([128, 4 * 1024], F32)
        wo = self.alloc([128, 12, 1024], BF16)
        if even:
            srcs = [(self.e_woA[j], 0, 128), (self.e_woM[j], 8, 128)]
            for (src, c0, np_) in srcs:
                self.dma(w32.a, src, (), [w32.b])
                self.copy("pool", wo.a[:, c0:c0 + 4, :], w32.a.rearrange("p (c n) -> p c n", c=4), [w32.b], [wo.b])
        woB = None
        if even:
            woB = self.alloc([64, 8, 1024], BF16)
            for hh in range(2):
                self.dma(w32.a[0:64, :], self.e_woB[j, :, hh * 4096:(hh + 1) * 4096], (), [w32.b])
                self.copy("pool", woB.a[:, hh * 4:(hh + 1) * 4, :], w32.a[0:64, :].rearrange("p (c n) -> p c n", c=4), [w32.b], [woB.b])
        else:
            for c0 in range(0, 12, 4):
                self.dma(w32.a, self.o_wo[j, :, c0 * 1024:(c0 + 4) * 1024], (), [w32.b])
                self.copy("pool", wo.a[:, c0:c0 + 4, :], w32.a.rearrange("p (c n) -> p c n", c=4), [w32.b], [wo.b])
        ca = self.ring(2, [128, 12, 512], BF16)
        cb = self.ring(2, [64, 8, 512], BF16)
        xt = self.ring(2, [128, D], F32)
        zz = self.ring(2, [128, D], F32)
        xo = self.ring(2, [128, D], F32)
        ss = self.ring(2, [128, 16], F32)
        for tc in range(8):
            t0 = tc * 512
            CA, CB = ca[tc % 2], cb[tc % 2]
            if even:
                self.dma(CA.a[:, 0:4, :], self.CATA[s, 0:4, :, t0:t0 + 512].rearrange("c p t -> p c t"), self.CATAb[s][0:4], [CA.b])
                self.dma(CB.a, self.CATB[s, :, :, t0:t0 + 512].rearrange("h d t -> d h t"), self.CATBb[s][tc * 4:tc * 4 + 4], [CB.b])
            else:
                self.dma(CA.a[:, 0:8, :], self.CATA[s, :, :, t0:t0 + 512].rearrange("c p t -> p c t"), self.CATAb[s], [CA.b])
            self.dma(CA.a[:, 8:12, :], self.CATM[s, :, :, t0:t0 + 512].rearrange("c p t -> p c t"), self.CATMb[s], [CA.b])
            for i in range(4):
                ti = tc * 4 + i
                row = s * SEQ + ti * 128
                Xt, Z, XO, SS = (r[ti % 2] for r in (xt, zz, xo, ss))
                self.dma(Xt.a, xsrc[row:row + 128, :], [self.Xb[s][ti]], [Xt.b])
                tsl = slice(i * 128, (i + 1) * 128)
                for nh in range(2):
                    ps = self.psum()
                    ns = slice(nh * 512, (nh + 1) * 512)
                    ops = []
                    for c in ([0, 1, 2, 3, 8, 9, 10, 11] if even else range(12)):
                        ops.append((CA.a[:, c, tsl], wo.a[:, c, ns], [CA.b, wo.b]))
                    if even:
                        for h in range(8):
                            ops.append((CB.a[:, h, tsl], woB.a[:, h, ns], [CB.b, woB.b]))
                    for oi, (l, r, rd) in enumerate(ops):
                        self.mm(ps.a, l, r, oi == 0, oi == len(ops) - 1, rd, [ps.b])
                    self.stt(Z.a[:, ns], Xt.a[:, ns], float(DN_ALPHA), ps.a, ALU.mult, ALU.add, [Xt.b, ps.b], [Z.b])
                self.layernorm(Z, SS, gb, 0, XO.a, XO.b)
                self.dma(xdst[row:row + 128, :], XO.a, [XO.b], [self.Xb[s][ti]])

    def build(self, only=None):
        on = lambda n: only is None or n in only
        for s in range(self.nseq):
            if on("rope"):
                self.rope_tables(s)
        for layer in range(self.layers):
            for s in range(self.nseq):
                if layer % 2 == 0:
                    if on("inproj"):
                        self.inproj_even(layer, s)
                    if on("conv"):
                        self.conv_branch(layer, s)
                    if on("dsa"):
                        self.dsa(layer, s)
                else:
                    if on("inproj"):
                        self.inproj_odd(layer, s)
                    if on("sgu"):
                        self.sgu(layer, s)
                if on("mem"):
                    self.mem_attn(layer, s)
                if on("out"):
                    self.outproj(layer, s)
        self.S.barrier()
        self.S.emit()
        return self.nc


def _fm(W, cols):
    w = W[:, cols]
    return np.ascontiguousarray(w.reshape(8, 128, 128).transpose(1, 0, 2).reshape(128, 8 * 128))


def _kmajor(W):
    K, N = W.shape
    return np.ascontiguousarray(W.reshape(K // 128, 128, N).transpose(1, 0, 2).reshape(128, (K // 128) * N))


def host_consts():
    inv = (10000.0 ** (-np.arange(0, 64, 2, dtype=np.float32) / np.float32(64))).astype(np.float32)
    invf = np.zeros((128, 2), np.float32)
    for p in range(128):
        invf[p, 0] = inv[p % 32]
        invf[p, 1] = -1.0 if (p % 64) < 32 else 1.0
    t = np.arange(128)
    negtri = np.where(t[None, :] <= t[:, None], 0.0, -1e30).astype(np.float32)
    tril = (t[None, :] >= t[:, None]).astype(np.float32)
    pow2 = np.tile((2.0 ** -np.arange(NIT + 2, dtype=np.float32))[None, :], (128, 1)).astype(np.float32)
    return {"c_ident": np.eye(128, dtype=np.float32), "c_invf": invf, "c_negtri": negtri, "c_tril": tril, "c_pow2": pow2}


def host_weights(e_w_in, e_conv_w, e_conv_b, e_cln_g, e_cln_b, e_pw2_w, e_pw2_b, e_w_out,
                 o_w_in, o_vln_g, o_vln_b, o_ws, o_bs, o_w_out, mem_wk, mem_wv, ln_g, ln_b):
    f = lambda a: np.asarray(a, dtype=np.float32)
    nE, nO = e_w_in.shape[0], o_w_in.shape[0]
    ar = np.arange
    sw = (ar(64) + 32) % 64
    out = {}
    wfm = np.zeros((nE, NCH_E, 128, 1024), np.float32)
    wtm = np.zeros((nE, 128, 8 * 68), np.float32)
    for j in range(nE):
        W = f(e_w_in[j])
        ch = []
        ch += [0 + 128 * i + ar(128) for i in range(4)]
        ch += [512 + 128 * i + ar(128) for i in range(4)]
        ch += [1024 + 128 * i + ar(128) for i in range(4)]
        ch += [1536 + 128 * i + ar(128) for i in range(4)]
        ch += [1536 + 128 * i + np.concatenate([sw, 64 + sw]) for i in range(4)]
        ch += [np.concatenate([2048 + ar(64), 2432 + ar(64)])]
        ch += [np.concatenate([2048 + sw, 2432 + sw])]
        ch += [2176 + 128 * i + ar(128) for i in range(2)]
        ch += [2176 + 128 * i + np.concatenate([sw, 64 + sw]) for i in range(2)]
        ch += [3012 + 128 * i + ar(128) for i in range(4)]
        ch += [3524 + 128 * i + ar(128) for i in range(4)]
        ch += [2500 + 128 * i + ar(128) for i in range(4)]
        assert len(ch) == NCH_E
        for c, cols in enumerate(ch):
            wfm[j, c] = _fm(W, cols)
        wtm[j] = _kmajor(W[:, np.concatenate([2112 + ar(64), 2496 + ar(4)])])
    out["e_wfm"], out["e_wtm"] = wfm, wtm
    cw = f(e_conv_w)
    out["e_convw"] = np.ascontiguousarray(cw.reshape(nE, 31, 4, 128).transpose(0, 3, 2, 1).reshape(nE, 128, 4 * 31))
    pc = lambda v: f(v).reshape(nE, 4, 128).transpose(0, 2, 1)
    out["e_vec"] = np.ascontiguousarray(np.concatenate([pc(e_conv_b), pc(e_cln_g), pc(e_cln_b), pc(e_pw2_b)], axis=2))
    out["e_pw2"] = np.stack([_kmajor(f(e_pw2_w[j])) for j in range(nE)])
    wo = f(e_w_out)
    out["e_woA"] = np.stack([_kmajor(wo[j, 0:512]) for j in range(nE)])
    out["e_woB"] = np.stack([np.ascontiguousarray(wo[j, 512:1024].reshape(8, 64, 1024).transpose(1, 0, 2).reshape(64, 8 * 1024)) for j in range(nE)])
    out["e_woM"] = np.stack([_kmajor(wo[j, 1024:1536]) for j in range(nE)])
    ofm = np.zeros((nO, NCH_O, 128, 1024), np.float32)
    otm = np.zeros((nO, 128, 8 * 1024), np.float32)
    for j in range(nO):
        W = f(o_w_in[j])
        ch = [0 + 128 * i + ar(128) for i in range(8)] + [2048 + 128 * i + ar(128) for i in range(8)]
        ch += [3072 + 128 * i + ar(128) for i in range(4)] + [3584 + 128 * i + ar(128) for i in range(4)]
        for c, cols in enumerate(ch):
            ofm[j, c] = _fm(W, cols)
        otm[j] = _kmajor(W[:, 1024:2048])
    out["o_wfm"], out["o_wtm"] = ofm, otm
    out["o_vln"] = np.ascontiguousarray(np.stack([f(o_vln_g), f(o_vln_b)], axis=1))
    out["o_wsT"] = np.ascontiguousarray(f(o_ws).transpose(0, 3, 1, 2).reshape(nO, 128, 8 * 128))
    out["o_bs"] = np.ascontiguousarray(f(o_bs).reshape(nO, 1, 8 * 128))
    out["o_wo"] = np.stack([_kmajor(f(o_w_out[j])) for j in range(nO)])
    out["m_wk"] = np.stack([_kmajor(f(mem_wk[l])) for l in range(DEPTH)])
    out["m_wv"] = np.stack([_kmajor(f(mem_wv[l])) for l in range(DEPTH)])
    out["ln_gb"] = np.ascontiguousarray(np.stack([f(ln_g), f(ln_b)], axis=1))
    return out


_PROG_CACHE = {}


def kernel(x, mem, positions, e_w_in, e_conv_w, e_conv_b, e_cln_g, e_cln_b, e_pw2_w, e_pw2_b, e_w_out,
           o_w_in, o_vln_g, o_vln_b, o_ws, o_bs, o_w_out, mem_wk, mem_wv, ln_g, ln_b):
    x = np.asarray(x, dtype=np.float32)
    mem = np.asarray(mem, dtype=np.float32)
    positions = np.asarray(positions, dtype=np.int32)
    B = x.shape[0]
    nseq = B // NCORES
    shared = host_consts()
    shared.update(host_weights(e_w_in, e_conv_w, e_conv_b, e_cln_g, e_cln_b, e_pw2_w, e_pw2_b, e_w_out,
                               o_w_in, o_vln_g, o_vln_b, o_ws, o_bs, o_w_out, mem_wk, mem_wv, ln_g, ln_b))
    if "p" not in _PROG_CACHE:
        _PROG_CACHE["p"] = Prog(nseq, DEPTH).build()
    nc = _PROG_CACHE["p"]
    in_maps = []
    for c in range(NCORES):
        d = dict(shared)
        d["x"] = np.ascontiguousarray(x[c * nseq:(c + 1) * nseq].reshape(nseq * SEQ, D))
        d["mem"] = np.ascontiguousarray(mem[c * nseq:(c + 1) * nseq].reshape(nseq * MEM, D))
        d["pos"] = np.ascontiguousarray(positions[c * nseq:(c + 1) * nseq])
        in_maps.append(d)
    res = run_bass_kernel_spmd(nc, in_maps, core_ids=list(range(NCORES)))
    out = np.concatenate([r["y"].reshape(nseq, SEQ, D) for r in res.results], axis=0)
    return out.astype(np.float32)
```

```python
import os
import numpy as np
import concourse.bass as bass
import concourse.mybir as mybir
from concourse.bass_utils import run_bass_kernel_spmd

F32 = mybir.dt.float32
BF16 = mybir.dt.bfloat16
I32 = mybir.dt.int32
AF = mybir.ActivationFunctionType
ALU = mybir.AluOpType
AX = mybir.AxisListType

NCORES = 8
D = 1024
SEQ = 4096
MEM = 256
DEPTH = 4
NT = SEQ // 128
LN_EPS = 1e-5
DN_ALPHA = (2 * DEPTH) ** 0.25
NIT = 22
NEG = -30000.0
SEM_EPOCH = 30000
DMA_K = 16
NCH_E = 38
NCH_O = 24


class Buf:
    __slots__ = ("name", "lw", "rd", "excl")

    def __init__(self, name="", excl=False):
        self.name = name
        self.lw = {}
        self.rd = {}
        self.excl = excl


class Stream:
    def __init__(self, S, name, eng, inc, k):
        self.S, self.name, self.eng, self.inc, self.k = S, name, eng, inc, k
        self.i = 0
        self.vals = [0] * k
        self.epochs = [0] * k
        self.sems = {}
        for j in range(k):
            self.sems[(name, j, 0)] = S.nc.alloc_semaphore(name=f"s_{name}_{j}_0")

    def next(self):
        j = self.i % self.k
        self.i += 1
        if self.vals[j] + self.inc > SEM_EPOCH:
            self.epochs[j] += 1
            self.vals[j] = 0
            self.sems[(self.name, j, self.epochs[j])] = self.S.nc.alloc_semaphore(
                name=f"s_{self.name}_{j}_{self.epochs[j]}")
        self.vals[j] += self.inc
        return ((self.name, j, self.epochs[j]), self.vals[j])

    def current(self):
        return [((self.name, j, self.epochs[j]), self.vals[j]) for j in range(self.k) if self.vals[j] > 0]


class Sched:
    def __init__(self, nc):
        self.nc = nc
        self.lists = {e: [] for e in ("pe", "dve", "act", "pool", "sp")}
        self.seen = {e: {} for e in self.lists}
        self.pending = {e: {} for e in self.lists}
        self.streams = {}
        for nm, eng, inc, k in (("pe", "pe", 1, 1), ("dve", "dve", 1, 1), ("act", "act", 1, 1),
                                ("pool", "pool", 1, 1), ("spd", "sp", 16, DMA_K), ("actd", "act", 16, DMA_K)):
            self.streams[nm] = Stream(self, nm, eng, inc, k)
        self.n = 0

    def semh(self, key):
        return self.streams[key[0]].sems[key]

    def op(self, stream, fn, reads=(), writes=()):
        st = self.streams[stream]
        eng = st.eng
        if any(b.excl for b in reads):
            writes = list(writes) + [b for b in reads if b.excl and b not in writes]
            reads = [b for b in reads if not b.excl]
        deps = dict(self.pending[eng])
        self.pending[eng] = {}
        if st.k > 1:
            jj = st.i % st.k
            if st.vals[jj] > 0:
                deps[(st.name, jj, st.epochs[jj])] = st.vals[jj]

        def add(d):
            for k, v in d.items():
                if deps.get(k, 0) < v:
                    deps[k] = v
        for b in reads:
            add(b.lw)
        for b in writes:
            add(b.lw)
            add(b.rd)
        seen = self.seen[eng]
        waits = []
        for k, v in deps.items():
            if stream == "pe" and k[0] == "pe":
                continue
            if seen.get(k, 0) >= v:
                continue
            seen[k] = v
            waits.append((k, v))
        key, val = st.next()
        self.lists[eng].append((waits, fn, key, st.inc))
        for b in reads:
            if b.rd.get(key, 0) < val:
                b.rd[key] = val
        for b in writes:
            b.lw = {key: val}
            b.rd = {}
        self.n += 1

    def barrier(self):
        cur = {}
        for s in self.streams.values():
            for k, v in s.current():
                cur[k] = v
        for e in self.lists:
            p = self.pending[e]
            for k, v in cur.items():
                if p.get(k, 0) < v:
                    p[k] = v

    def emit(self):
        nc = self.nc
        engmap = {"pe": "tensor", "dve": "vector", "act": "scalar", "pool": "gpsimd", "sp": "sync"}
        fin = []
        for s in self.streams.values():
            fin += s.current()
        with nc.Block() as block:
            for e, lst in self.lists.items():
                def body(engine, lst=lst, e=e):
                    for waits, fn, key, inc in lst:
                        for (wk, wv) in waits:
                            engine.wait_ge(self.semh(wk), wv)
                        fn(engine).then_inc(self.semh(key), inc)
                    if e == "sp":
                        for (wk, wv) in fin:
                            engine.wait_ge(self.semh(wk), wv)
                getattr(block, engmap[e])(body)


class T:
    __slots__ = ("a", "b")

    def __init__(self, a, b):
        self.a, self.b = a, b


class Prog:
    def __init__(self, nseq, layers, debug=False):
        self.nseq, self.layers, self.debug = nseq, layers, debug
        self.nc = nc = bass.Bass("TRN2", target_bir_lowering=False)
        self.S = Sched(nc)
        self.dbg_names = []
        ntok = nseq * SEQ
        ein = lambda n, sh, dt=F32: nc.dram_tensor(n, list(sh), dt, kind="ExternalInput").ap()
        self.x_in = ein("x", [ntok, D])
        self.mem_in = ein("mem", [nseq * MEM, D])
        self.pos_in = ein("pos", [nseq, SEQ], I32)
        self.c_ident = ein("c_ident", [128, 128])
        self.c_invf = ein("c_invf", [128, 2])
        self.c_negtri = ein("c_negtri", [128, 128])
        self.c_tril = ein("c_tril", [128, 128])
        self.c_pow2 = ein("c_pow2", [128, NIT + 2])
        nE, nO = (DEPTH + 1) // 2, DEPTH // 2
        self.e_wfm = ein("e_wfm", [nE, NCH_E, 128, 8 * 128])
        self.e_wtm = ein("e_wtm", [nE, 128, 8 * 68])
        self.e_convw = ein("e_convw", [nE, 128, 4 * 31])
        self.e_vec = ein("e_vec", [nE, 128, 16])
        self.e_pw2 = ein("e_pw2", [nE, 128, 4 * 512])
        self.e_woA = ein("e_woA", [nE, 128, 4 * 1024])
        self.e_woB = ein("e_woB", [nE, 64, 8 * 1024])
        self.e_woM = ein("e_woM", [nE, 128, 4 * 1024])
        self.o_wfm = ein("o_wfm", [nO, NCH_O, 128, 8 * 128])
        self.o_wtm = ein("o_wtm", [nO, 128, 8 * 1024])
        self.o_vln = ein("o_vln", [nO, 2, 1024])
        self.o_wsT = ein("o_wsT", [nO, 128, 8 * 128])
        self.o_bs = ein("o_bs", [nO, 1, 8 * 128])
        self.o_wo = ein("o_wo", [nO, 128, 12 * 1024])
        self.m_wk = ein("m_wk", [DEPTH, 128, 8 * 512])
        self.m_wv = ein("m_wv", [DEPTH, 128, 8 * 512])
        self.ln_gb = ein("ln_gb", [DEPTH, 2, 1024])
        self.y_out = nc.dram_tensor("y", [ntok, D], F32, kind="ExternalOutput").ap()
        self.X = self.scr("X", [ntok, D], F32)
        self.Xb = [[Buf() for _ in range(NT)] for _ in range(nseq)]
        mk = lambda n: [[Buf() for _ in range(n)] for _ in range(nseq)]
        self.CT = self.scr("CT", [nseq, 128, SEQ], F32); self.CTb = mk(1)
        self.ST = self.scr("ST", [nseq, 128, SEQ], F32); self.STb = mk(1)
        self.H = self.scr("H", [nseq, 4, 128, SEQ], BF16); self.Hb = mk(4)
        self.GA = self.scr("GA", [nseq, 4, 128, SEQ], BF16); self.GAb = mk(4)
        self.QR = self.scr("QR", [nseq, 4, 128, SEQ], BF16); self.QRb = mk(4)
        self.KR = self.scr("KR", [nseq, 2, 64, SEQ], BF16); self.KRb = mk(1)
        self.QI = self.scr("QI", [nseq, 2, 128, SEQ], BF16); self.QIb = mk(2)
        self.V = self.scr("V", [nseq, 128, NT, 64], BF16); self.Vb = mk(1)
        self.WI = self.scr("WI", [nseq, 128, NT, 4], F32); self.WIb = mk(1)
        self.GB = self.scr("GB", [nseq, 8, 64, SEQ], BF16); self.GBb = mk(4)
        self.MQ = self.scr("MQ", [nseq, 4, 128, SEQ], BF16); self.MQb = mk(4)
        self.GM = self.scr("GM", [nseq, 4, 128, SEQ], BF16); self.GMb = mk(4)
        self.U = self.scr("U", [nseq, 8, 128, SEQ], BF16); self.Ub = mk(8)
        self.GC = self.scr("GC", [nseq, 8, 128, SEQ], BF16); self.GCb = mk(8)
        self.VLN = self.scr("VLN", [nseq, SEQ, D], BF16); self.VLNb = mk(NT)
        self.CATA = self.scr("CATA", [nseq, 8, 128, SEQ], BF16); self.CATAb = mk(8)
        self.CATB = self.scr("CATB", [nseq, 8, 64, SEQ], BF16); self.CATBb = mk(NT)
        self.CATM = self.scr("CATM", [nseq, 4, 128, SEQ], BF16); self.CATMb = mk(4)
        self.PS = []
        for i in range(4):
            t = nc.alloc_psum_tensor(f"ps{i}", [128, 1024], F32)
            self.PS.append((t, [Buf(f"ps{i}a", True), Buf(f"ps{i}b", True)]))
        self.psi = 0
        self.consts()
        self.arena_t = nc.alloc_sbuf_tensor("arena", [128, self.ARENA], BF16)
        self.aoff = 0

    ARENA = 90 * 1024

    def scr(self, name, shape, dt):
        kind = "ExternalOutput" if self.debug else "Internal"
        if self.debug:
            self.dbg_names.append(name)
        return self.nc.dram_tensor("scr_" + name, list(shape), dt, kind=kind).ap()

    def phase(self):
        self.S.barrier()
        self.aoff = 0

    def alloc(self, shape, dt):
        n = int(np.prod(shape[1:]))
        ne = n * (2 if dt == F32 or dt == I32 else 1)
        ne = (ne + 15) // 16 * 16
        assert self.aoff + ne <= self.ARENA, ("arena overflow", self.aoff, ne)
        v = self.arena_t[0:shape[0], self.aoff:self.aoff + ne]
        self.aoff += ne
        if dt != BF16:
            v = v.bitcast(dt)
        v = v[:, 0:n]
        if len(shape) == 3:
            v = v.rearrange("p (a b) -> p a b", a=shape[1])
        return T(v, Buf())

    def ring(self, n, shape, dt):
        return [self.alloc(shape, dt) for _ in range(n)]

    def psum(self):
        i = self.psi % 8
        self.psi += 1
        t, bs = self.PS[i // 2]
        h = i % 2
        return T(t[:, h * 512:(h + 1) * 512], bs[h])

    def dma(self, out, in_, reads=(), writes=(), q=None):
        if q is None:
            q = "actd" if str(out.space) == "DRAM" else "spd"
        self.S.op(q, lambda e: e.dma_start(out=out, in_=in_), reads, writes)

    def mm(self, out, lhsT, rhs, start, stop, reads, writes):
        self.S.op("pe", lambda e: e.matmul(out, lhsT=lhsT, rhs=rhs, start=start, stop=stop), reads, writes)

    def act(self, out, in_, func, reads, writes, bias=0.0, scale=1.0):
        self.S.op("act", lambda e: e.activation(out=out, in_=in_, func=func, bias=bias, scale=scale), reads, writes)

    def ts(self, eng, out, in0, s1, s2, op0, op1, reads, writes, accum=None):
        if accum is None:
            if s2 is None:
                self.S.op(eng, lambda e: e.tensor_scalar(out=out, in0=in0, scalar1=s1, scalar2=None, op0=op0), reads, writes)
            else:
                self.S.op(eng, lambda e: e.tensor_scalar(out=out, in0=in0, scalar1=s1, scalar2=s2, op0=op0, op1=op1), reads, writes)
        else:
            self.S.op(eng, lambda e: e.tensor_scalar(out=out, in0=in0, scalar1=s1, scalar2=s2, op0=op0, op1=op1, accum_out=accum), reads, writes)

    def tt(self, eng, out, in0, in1, op, reads, writes):
        self.S.op(eng, lambda e: e.tensor_tensor(out=out, in0=in0, in1=in1, op=op), reads, writes)

    def stt(self, out, in0, scalar, in1, op0, op1, reads, writes):
        self.S.op("dve", lambda e: e.scalar_tensor_tensor(out=out, in0=in0, scalar=scalar, in1=in1, op0=op0, op1=op1), reads, writes)

    def copy(self, eng, out, in_, reads, writes):
        if eng == "act":
            self.S.op("act", lambda e: e.copy(out=out, in_=in_), reads, writes)
        else:
            self.S.op(eng, lambda e: e.tensor_copy(out=out, in_=in_), reads, writes)

    def consts(self):
        nc = self.nc
        al = lambda n, sh, dt: T(nc.alloc_sbuf_tensor(n, sh, dt), Buf(n))
        self.ident32 = al("ident32", [128, 128], F32)
        self.identb = al("identb", [128, 128], BF16)
        self.ident4 = al("ident4", [128, 512], BF16)
        self.ones = al("ones", [128, 128], BF16)
        self.negtri = al("negtri", [128, 128], F32)
        self.tril = al("tril", [128, 128], F32)
        self.pow2 = al("pow2", [128, NIT + 2], F32)
        self.invf = al("invf", [128, 2], F32)
        self.epsc = al("epsc", [128, 1], F32)
        self.dma(self.ident32.a[:], self.c_ident, writes=[self.ident32.b])
        self.dma(self.negtri.a[:], self.c_negtri, writes=[self.negtri.b])
        self.dma(self.tril.a[:], self.c_tril, writes=[self.tril.b])
        self.dma(self.pow2.a[:], self.c_pow2, writes=[self.pow2.b])
        self.dma(self.invf.a[:], self.c_invf, writes=[self.invf.b])
        self.copy("dve", self.identb.a[:], self.ident32.a[:], [self.ident32.b], [self.identb.b])
        for i in range(4):
            self.copy("dve", self.ident4.a[:, i * 128:(i + 1) * 128], self.ident32.a[:], [self.ident32.b], [self.ident4.b])
        self.S.op("dve", lambda e: e.memset(self.ones.a[:], 1.0), (), [self.ones.b])
        self.S.op("dve", lambda e: e.memset(self.epsc.a[:], LN_EPS), (), [self.epsc.b])

    def rope_tables(self, s):
        self.phase()
        posi = self.alloc([128, SEQ], I32)
        ang = self.alloc([128, SEQ], F32)
        t1 = self.alloc([128, SEQ], F32)
        t2 = self.alloc([128, SEQ], F32)
        self.dma(posi.a, self.pos_in[s:s + 1, :].partition_broadcast(128), writes=[posi.b])
        self.copy("dve", ang.a, posi.a, [posi.b], [ang.b])
        self.ts("dve", ang.a, ang.a, self.invf.a[:, 0:1], None, ALU.mult, None, [ang.b, self.invf.b], [ang.b])
        twopi = float(np.float32(2 * np.pi))
        magic = 12582912.0
        for (shift, dst, dstb, signed) in ((np.pi / 2, self.CT, self.CTb, False), (0.0, self.ST, self.STb, True)):
            self.ts("dve", t1.a, ang.a, float(shift), 1.0 / twopi, ALU.add, ALU.mult, [ang.b], [t1.b])
            self.ts("dve", t1.a, t1.a, magic, None, ALU.add, None, [t1.b], [t1.b])
            self.ts("dve", t1.a, t1.a, magic, None, ALU.subtract, None, [t1.b], [t1.b])
            self.stt(t2.a, t1.a, -twopi, ang.a, ALU.mult, ALU.add, [t1.b, ang.b], [t2.b])
            self.ts("dve", t2.a, t2.a, float(shift), 3.1415925, ALU.add, ALU.min, [t2.b], [t2.b])
            self.ts("dve", t2.a, t2.a, -3.1415925, None, ALU.max, None, [t2.b], [t2.b])
            self.act(t1.a, t2.a, AF.Sin, [t2.b], [t1.b])
            if signed:
                self.ts("dve", t1.a, t1.a, self.invf.a[:, 1:2], None, ALU.mult, None, [t1.b, self.invf.b], [t1.b])
            self.dma(dst[s], t1.a, [t1.b], [dstb[s][0]])

    def load_xT(self, layer, s, half, xT):
        xsrc = self.x_in if layer == 0 else self.X
        xl = self.ring(2, [128, D], F32)
        xb = self.ring(2, [128, D], BF16)
        for i in range(16):
            ti = half * 16 + i
            row = s * SEQ + ti * 128
            a, b = xl[i % 2], xb[i % 2]
            self.dma(a.a, xsrc[row:row + 128, :], [self.Xb[s][ti]], [a.b])
            self.copy("act", b.a, a.a, [a.b], [b.b])
            ps = self.psum()
            pv = ps.a.bitcast(BF16)
            for k in range(8):
                self.S.op("pe", lambda e, k=k, pv=pv, b=b: e.transpose(pv[:, k * 128:(k + 1) * 128], b.a[:, k * 128:(k + 1) * 128], self.identb.a[:]),
                          [b.b, self.identb.b], [ps.b])
            self.copy("dve", xT.a[:, :, i * 128:(i + 1) * 128], pv.rearrange("p (k t) -> p k t", k=8), [ps.b], [xT.b])

    def fm_chunk(self, wsrc, xT, wl, wb, ci):
        a, b = wl[ci % 2], wb[ci % 2]
        self.dma(a.a, wsrc, (), [a.b])
        self.copy("pool", b.a, a.a, [a.b], [b.b])
        outs = []
        for tc in range(4):
            ps = self.psum()
            for k in range(8):
                self.mm(ps.a, b.a[:, k * 128:(k + 1) * 128], xT.a[:, k, tc * 512:(tc + 1) * 512], k == 0, k == 7, [b.b, xT.b], [ps.b])
            outs.append(ps)
        return outs

    def inproj_even(self, layer, s):
        j = layer // 2
        for half in range(2):
            self.phase()
            t0 = half * 2048
            xT = self.alloc([128, 8, 2048], BF16)
            self.load_xT(layer, s, half, xT)
            cts = self.alloc([128, 2048], F32)
            sts = self.alloc([128, 2048], F32)
            self.dma(cts.a, self.CT[s, :, t0:t0 + 2048], [self.CTb[s][0]], [cts.b])
            self.dma(sts.a, self.ST[s, :, t0:t0 + 2048], [self.STb[s][0]], [sts.b])
            wl = self.ring(2, [128, 1024], F32)
            wb = self.ring(2, [128, 1024], BF16)
            stage = self.ring(3, [128, 2048], BF16)
            tmpa = self.ring(2, [128, 2048], F32)
            tmpb = self.ring(2, [128, 512], F32)
            st_i = [0]

            def next_stage():
                st_i[0] += 1
                return stage[st_i[0] % 3]
            ci = [0]

            def chunk(idx):
                r = self.fm_chunk(self.e_wfm[j, idx], xT, wl, wb, ci[0])
                ci[0] += 1
                return r
            sl = lambda tc: slice(tc * 512, (tc + 1) * 512)
            SEC = os.environ.get('KSEC', 'Asbpt')
            for i in range(4 if 'A' in SEC else 0):
                ta = tmpa[i % 2]
                for tc, ps in enumerate(chunk(4 + i)):
                    self.act(ta.a[:, sl(tc)], ps.a, AF.Sigmoid, [ps.b], [ta.b])
                sg = next_stage()
                for tc, ps in enumerate(chunk(0 + i)):
                    self.tt("dve", sg.a[:, sl(tc)], ps.a, ta.a[:, sl(tc)], ALU.mult, [ps.b, ta.b], [sg.b])
                self.dma(self.H[s, i, :, t0:t0 + 2048], sg.a, [sg.b], [self.Hb[s][i]])
            simple = [(8 + i, AF.Silu, self.GA, self.GAb, i) for i in range(4)]
            simple += [(26 + i, AF.Copy, self.MQ, self.MQb, i) for i in range(4)]
            simple += [(30 + i, AF.Silu, self.GM, self.GMb, i) for i in range(4)]
            for (idx, fn, dst, dstb, di) in (simple if 's' in SEC else []):
                sg = next_stage()
                for tc, ps in enumerate(chunk(idx)):
                    self.act(sg.a[:, sl(tc)], ps.a, fn, [ps.b], [sg.b])
                self.dma(dst[s, di, :, t0:t0 + 2048], sg.a, [sg.b], [dstb[s][di]])
            for i in range(4 if 'b' in SEC else 0):
                sg = next_stage()
                for tc, ps in enumerate(chunk(34 + i)):
                    self.act(sg.a[:, sl(tc)], ps.a, AF.Silu, [ps.b], [sg.b])
                self.dma(self.GB[s, 2 * i:2 * i + 2, :, t0:t0 + 2048].rearrange("h d t -> (h d) t"), sg.a, [sg.b], [self.GBb[s][i]])
            pairs = [(12 + i, 16 + i, self.QR[s, i, :, t0:t0 + 2048], self.QRb[s][i]) for i in range(4)]
            pairs += [(20, 21, self.KR[s, :, :, t0:t0 + 2048].rearrange("a d t -> (a d) t"), self.KRb[s][0])]
            pairs += [(22 + i, 24 + i, self.QI[s, i, :, t0:t0 + 2048], self.QIb[s][i]) for i in range(2)]
            KP = os.environ.get('KP', 'qki')
            pairs = [p for p, tag in zip(pairs, 'qqqqkii') if tag in KP]
            for pi, (i0, i1, dst, dstb) in enumerate(pairs if 'p' in SEC else []):
                ta = tmpa[pi % 2]
                for tc, ps in enumerate(chunk(i0)):
                    self.tt("dve", ta.a[:, sl(tc)], ps.a, cts.a[:, sl(tc)], ALU.mult, [ps.b, cts.b], [ta.b])
                sg = next_stage()
                for tc, ps in enumerate(chunk(i1)):
                    tb = tmpb[tc % 2]
                    self.tt("dve", tb.a, ps.a, sts.a[:, sl(tc)], ALU.mult, [ps.b, sts.b], [tb.b])
                    self.tt(os.environ.get("KPE", "pool"), sg.a[:, sl(tc)], tb.a, ta.a[:, sl(tc)], ALU.add, [tb.b, ta.b], [sg.b])
                self.dma(dst, sg.a, [sg.b], [dstb])
            if 't' not in SEC:
                continue
            wt32 = self.alloc([128, 8 * 68], F32)
            wtb = self.alloc([128, 8 * 68], BF16)
            self.dma(wt32.a, self.e_wtm[j], (), [wt32.b])
            self.copy("pool", wtb.a, wt32.a, [wt32.b], [wtb.b])
            vst = self.alloc([128, 16, 64], BF16)
            wst = self.alloc([128, 16, 4], F32)
            for i in range(16):
                ps = self.psum()
                for k in range(8):
                    self.mm(ps.a[:, 0:68], xT.a[:, k, i * 128:(i + 1) * 128], wtb.a[:, k * 68:(k + 1) * 68], k == 0, k == 7, [xT.b, wtb.b], [ps.b])
                KT = os.environ.get('KT', 'vwVW')
                if 'v' in KT:
                    self.copy("act", vst.a[:, i, :], ps.a[:, 0:64], [ps.b], [vst.b])
                if 'w' in KT:
                    self.ts("dve", wst.a[:, i, :], ps.a[:, 64:68], 1.0 / 16.0, None, ALU.mult, None, [ps.b], [wst.b])
            if 'V' in KT:
                self.dma(self.V[s, :, half * 16:half * 16 + 16, :], vst.a, [vst.b], [self.Vb[s][0]])
            if 'W' in KT:
                self.dma(self.WI[s, :, half * 16:half * 16 + 16, :], wst.a, [wst.b], [self.WIb[s][0]])

    def conv_branch(self, layer, s):
        j = layer // 2
        self.phase()
        vec = self.alloc([128, 16], F32)
        self.dma(vec.a, self.e_vec[j], (), [vec.b])
        cw = self.alloc([128, 4 * 31], F32)
        self.dma(cw.a, self.e_convw[j], (), [cw.b])
        dg = self.alloc([128, 4 * 31 * 128], BF16)
        for i in range(4 * 31):
            self.act(dg.a[:, i * 128:(i + 1) * 128], self.ident32.a[:], AF.Copy, [self.ident32.b, cw.b], [dg.b], scale=cw.a[:, i:i + 1])
        p32 = self.alloc([128, 4 * 512], F32)
        pw = self.alloc([128, 4 * 512], BF16)
        self.dma(p32.a, self.e_pw2[j], (), [p32.b])
        self.copy("pool", pw.a, p32.a, [p32.b], [pw.b])
        hs = [self.alloc([128, 32 + SEQ], BF16) for _ in range(4)]
        for c in range(4):
            self.S.op("pool", lambda e, c=c: e.memset(hs[c].a[:, 0:32], 0.0), (), [hs[c].b])
            self.dma(hs[c].a[:, 32:32 + SEQ], self.H[s, c], [self.Hb[s][c]], [hs[c].b])
        y32 = self.ring(2, [128, 4, 512], F32)
        ybf = self.ring(2, [128, 4, 512], BF16)
        ysq = self.ring(2, [128, 4, 512], BF16)
        st = self.ring(2, [128, 4, 512], F32)
        sb = self.ring(2, [128, 4, 512], BF16)
        ga = self.ring(2, [128, 4, 512], BF16)
        yo = self.ring(2, [128, 4, 512], BF16)
        for tc in range(8):
            t0 = tc * 512
            Y, YB, YQ, ST, SB, G, YO = (r[tc % 2] for r in (y32, ybf, ysq, st, sb, ga, yo))
            self.dma(G.a, self.GA[s, :, :, t0:t0 + 512].rearrange("c p t -> p c t"), [self.GAb[s][c] for c in range(4)], [G.b])
            for c in range(4):
                ps = self.psum()
                for k in range(31):
                    self.mm(ps.a, dg.a[:, (c * 31 + k) * 128:(c * 31 + k + 1) * 128], hs[c].a[:, 2 + k + t0:2 + k + t0 + 512],
                            k == 0, k == 30, [dg.b, hs[c].b], [ps.b])
                self.act(Y.a[:, c, :], ps.a, AF.Identity, [ps.b, vec.b], [Y.b], bias=vec.a[:, c:c + 1])
                self.copy("pool", YB.a[:, c, :], Y.a[:, c, :], [Y.b], [YB.b])
                self.act(YQ.a[:, c, :], Y.a[:, c, :], AF.Square, [Y.b], [YQ.b])
            p1, p2 = self.psum(), self.psum()
            for c in range(4):
                self.mm(p1.a, self.ones.a[:], YB.a[:, c, :], c == 0, c == 3, [self.ones.b, YB.b], [p1.b])
            for c in range(4):
                self.mm(p2.a, self.ones.a[:], YQ.a[:, c, :], c == 0, c == 3, [self.ones.b, YQ.b], [p2.b])
            mean, var, rstd = ST.a[:, 0, :], ST.a[:, 1, :], ST.a[:, 2, :]
            self.ts("dve", mean, p1.a, 1.0 / 512, None, ALU.mult, None, [p1.b], [ST.b])
            self.tt("dve", var, mean, mean, ALU.mult, [ST.b], [ST.b])
            self.stt(var, p2.a, 1.0 / 512, var, ALU.mult, ALU.subtract, [p2.b, ST.b], [ST.b])
            self.act(rstd, var, AF.Ln, [ST.b, self.epsc.b], [ST.b], bias=self.epsc.a[:, 0:1])
            self.act(rstd, rstd, AF.Exp, [ST.b], [ST.b], scale=-0.5)
            for c in range(4):
                self.tt("dve", ST.a[:, 3, :], Y.a[:, c, :], mean, ALU.subtract, [Y.b, ST.b], [ST.b])
                self.tt("pool", Y.a[:, c, :], ST.a[:, 3, :], rstd, ALU.mult, [ST.b], [Y.b])
                self.act(SB.a[:, c, :], Y.a[:, c, :], AF.Silu, [Y.b, vec.b], [SB.b], bias=vec.a[:, 8 + c:9 + c], scale=vec.a[:, 4 + c:5 + c])
            for n in range(4):
                ps = self.psum()
                for c in range(4):
                    self.mm(ps.a, pw.a[:, c * 512 + n * 128:c * 512 + (n + 1) * 128], SB.a[:, c, :], c == 0, c == 3, [pw.b, SB.b], [ps.b])
                self.stt(YO.a[:, n, :], ps.a, vec.a[:, 12 + n:13 + n], G.a[:, n, :], ALU.add, ALU.mult, [ps.b, vec.b, G.b], [YO.b])
            self.dma(self.CATA[s, 0:4, :, t0:t0 + 512].rearrange("c p t -> p c t"), YO.a, [YO.b], [self.CATAb[s][c] for c in range(4)])

    def dsa(self, layer, s):
        self.phase()
        QRs = self.alloc([128, 4, SEQ], BF16)
        KK = self.alloc([128, SEQ], BF16)
        KI2 = self.alloc([128, SEQ], BF16)
        QIs = self.alloc([128, 2, SEQ], BF16)
        Vs = self.alloc([128, NT, 64], BF16)
        WIs = self.alloc([128, NT, 4], F32)
        for c in range(4):
            self.dma(QRs.a[:, c, :], self.QR[s, c], self.QRb[s], [QRs.b])
        for hh in range(2):
            self.dma(KK.a[hh * 64:(hh + 1) * 64, :], self.KR[s, 0], self.KRb[s], [KK.b])
            self.dma(KI2.a[hh * 64:(hh + 1) * 64, :], self.KR[s, 1], self.KRb[s], [KI2.b])
        for c in range(2):
            self.dma(QIs.a[:, c, :], self.QI[s, c], self.QIb[s], [QIs.b])
        self.dma(Vs.a, self.V[s], self.Vb[s], [Vs.b])
        self.dma(WIs.a, self.WI[s], self.WIb[s], [WIs.b])
        score = self.ring(2, [128, SEQ], F32)
        mb = self.ring(4, [128, SEQ], BF16)
        junkD = self.alloc([128, SEQ], BF16)
        junkA = self.alloc([128, SEQ], BF16)
        rl = self.ring(3, [128, 256], F32)
        sm = self.ring(4, [128, 8 + NIT + 2], F32)
        pT = self.ring(2, [128, 1024], BF16)
        gb = self.ring(2, [64, 8, 128], BF16)
        rs = self.ring(2, [64, 1024], F32)
        yb = self.ring(2, [64, 8, 128], BF16)
        (tA, bA), (tO, bO), (tS, bS), (tI, bI) = self.PS
        nqb = int(os.environ.get('KQB', NT))

        def stageA1(qb):
            q0 = qb * 128
            L = q0 + 128
            SC, SM = score[qb % 2], sm[qb % 4]
            for c in range((L + 255) // 256):
                k0 = c * 256
                n = min(256, L - k0)
                col = lambda h: (h % 2) * 512 + (h // 2) * 256
                for h in range(4):
                    pr = slice((h % 2) * 64, (h % 2) * 64 + 64)
                    self.mm(tI[:, col(h):col(h) + n], QIs.a[pr, h // 2, q0:q0 + 128], KI2.a[pr, k0:k0 + n], True, True,
                            [QIs.b, KI2.b], [bI[h % 2]])
                dst = SC.a[:, k0:k0 + n]
                self.ts("dve", dst, tI[:, 0:n], 0.0, WIs.a[:, qb, 0:1], ALU.max, ALU.mult, [bI[0], WIs.b], [SC.b])
                for h in range(1, 4):
                    r = rl[h - 1]
                    self.act(r.a[:, 0:n], tI[:, col(h):col(h) + n], AF.Relu, [bI[h % 2]], [r.b])
                    self.stt(dst, r.a[:, 0:n], WIs.a[:, qb, h:h + 1], dst, ALU.mult, ALU.add, [r.b, WIs.b, SC.b], [SC.b])
            if qb >= 2:
                self.S.op("dve", lambda e: e.tensor_reduce(out=SM.a[:, 0:1], in_=SC.a[:, 0:L], axis=AX.X, op=ALU.max), [SC.b], [SM.b])
                self.S.op("dve", lambda e: e.tensor_reduce(out=SM.a[:, 1:2], in_=SC.a[:, 0:L], axis=AX.X, op=ALU.min), [SC.b], [SM.b])
            self.tt("dve", SC.a[:, q0:L], SC.a[:, q0:L], self.negtri.a[:], ALU.add, [SC.b, self.negtri.b], [SC.b])
            if qb >= 2:
                hi, lo, W0, mid = (SM.a[:, i:i + 1] for i in range(4))
                self.tt("dve", W0, hi, lo, ALU.subtract, [SM.b], [SM.b])
                self.ts("dve", SM.a[:, 8:8 + NIT + 2], self.pow2.a[:], W0, None, ALU.mult, None, [self.pow2.b, SM.b], [SM.b])
                self.tt("dve", mid, lo, SM.a[:, 9:10], ALU.add, [SM.b], [SM.b])
            else:
                self.S.op("dve", lambda e: e.memset(SM.a[:, 6:7], -1e29), (), [SM.b])

        def bisect_dve(qb):
            L = qb * 128 + 128
            SC, SM = score[qb % 2], sm[qb % 4]
            mid, cnt, tt_, thr = SM.a[:, 3:4], SM.a[:, 4:5], SM.a[:, 5:6], SM.a[:, 6:7]
            for k in range(1, NIT + 1):
                self.ts("dve", junkD.a[:, 0:L], SC.a[:, 0:L], mid, None, ALU.is_ge, ALU.add, [SC.b, SM.b], [junkD.b, SM.b], accum=cnt)
                if k < NIT:
                    self.ts("dve", tt_, cnt, 256.0, 0.5, ALU.is_ge, ALU.subtract, [SM.b], [SM.b])
                    self.stt(mid, tt_, SM.a[:, 8 + k:9 + k], mid, ALU.mult, ALU.add, [SM.b], [SM.b])
                else:
                    self.ts("dve", tt_, cnt, 256.0, 1.0, ALU.is_ge, ALU.subtract, [SM.b], [SM.b])
                    self.stt(thr, tt_, SM.a[:, 8 + k:9 + k], mid, ALU.mult, ALU.add, [SM.b], [SM.b])

        def stageA3(qb):
            L = qb * 128 + 128
            SC, SM, MB = score[qb % 2], sm[qb % 4], mb[qb % 4]
            self.ts("dve", MB.a[:, 0:L], SC.a[:, 0:L], SM.a[:, 6:7], NEG, ALU.is_lt, ALU.mult, [SC.b, SM.b], [MB.b])

        osb = self.ring(2, [64, 1024], F32)

        def stageB_steps(qb):
            q0 = qb * 128
            MB = mb[qb % 4]
            G, RS, YB, OS = (r[qb % 2] for r in (gb, rs, yb, osb))
            nsc = qb + 1
            steps = []

            def qk_exp(sc):
                P = pT[sc % 2]
                for par in range(2):
                    pr = slice(par * 64, par * 64 + 64)
                    o = tA[:, par * 512:(par + 1) * 512]
                    self.mm(o.rearrange("p (a b) -> p a b", a=4), KK.a[pr, sc * 128:(sc + 1) * 128], QRs.a[pr, :, q0:q0 + 128], True, False, [KK.b, QRs.b], [bA[par]])
                    self.mm(o, MB.a[:, sc * 128:(sc + 1) * 128], self.ident4.a[:], False, True, [MB.b, self.ident4.b], [bA[par]])
                for par in range(2):
                    cs = slice(par * 512, (par + 1) * 512)
                    self.act(P.a[:, cs], tA[:, cs], AF.Exp, [bA[par]], [P.b], scale=0.125)

            def chunk(sc):
                if sc == 0:
                    self.dma(G.a, self.GB[s, :, :, q0:q0 + 128].rearrange("h d t -> d h t"), self.GBb[s], [G.b])
                    qk_exp(0)
                if sc + 1 < nsc:
                    qk_exp(sc + 1)
                P = pT[sc % 2]
                for par in range(2):
                    cs = slice(par * 512, (par + 1) * 512)
                    self.mm(tO[0:64, cs], Vs.a[:, sc, :], P.a[:, cs], sc == 0, sc == nsc - 1, [Vs.b, P.b], [bO[par]])
                    self.mm(tS[0:64, cs], self.ones.a[:, 0:64], P.a[:, cs], sc == 0, sc == nsc - 1, [self.ones.b, P.b], [bS[par]])
                if sc == nsc - 1:
                    self.act(RS.a, tS[0:64, :], AF.Ln, [bS[0], bS[1]], [RS.b])
                    self.act(RS.a, RS.a, AF.Exp, [RS.b], [RS.b], scale=-1.0)
                    self.copy("act", OS.a, tO[0:64, :], [bO[0], bO[1]], [OS.b])
                    self.tt("pool", OS.a, OS.a, RS.a, ALU.mult, [OS.b, RS.b], [OS.b])
                    ybv = YB.a.rearrange("d (pair par) t -> d par pair t", par=2)
                    gv = G.a.rearrange("d (pair par) t -> d par pair t", par=2)
                    for par in range(2):
                        self.tt("pool", ybv[:, par], OS.a[:, par * 512:(par + 1) * 512].rearrange("d (pair t) -> d pair t", pair=4), gv[:, par],
                                ALU.mult, [OS.b, G.b], [YB.b])
                    self.dma(self.CATB[s, :, :, q0:q0 + 128].rearrange("h d t -> d h t"), YB.a, [YB.b], [self.CATBb[s][qb]])
            for sc in range(nsc):
                steps.append(lambda sc=sc: chunk(sc))
            return steps

        def bisect_act_steps(qb):
            L = qb * 128 + 128
            SC, SM = score[qb % 2], sm[qb % 4]
            mid, sg, g, thr = SM.a[:, 3:4], SM.a[:, 4:5], SM.a[:, 5:6], SM.a[:, 6:7]

            def it(k):
                self.S.op("act", lambda e: e.activation(out=junkA.a[:, 0:L], in_=SC.a[:, 0:L], func=AF.Sign, bias=mid, scale=-1.0, accum_out=sg),
                          [SC.b, SM.b], [junkA.b, SM.b])
                self.act(g, sg, AF.Sign, [SM.b], [SM.b], bias=float(L - 511), scale=-1.0)
                self.act(mid, g, AF.Identity, [SM.b], [SM.b], bias=mid, scale=SM.a[:, 9 + k:10 + k])
                if k == NIT:
                    self.act(thr, SM.a[:, 9 + NIT:10 + NIT], AF.Identity, [SM.b], [SM.b], bias=mid, scale=-1.0)
            return [lambda k=k: it(k) for k in range(1, NIT + 1)]

        def interleave(chunks, iters):
            nC, nI = len(chunks), len(iters)
            head = (nC * 25 + 99) // 100 if nI else nC
            for c in chunks[:head]:
                c()
            rest = chunks[head:]
            ci = ii = 0
            while ci < len(rest) or ii < nI:
                if ii < nI and (ci >= len(rest) or ii * max(len(rest), 1) <= ci * nI):
                    iters[ii]()
                    ii += 1
                else:
                    rest[ci]()
                    ci += 1

        npair = nqb // 2
        for r in range(npair + 1):
            iters = []
            if r < npair:
                a2, b2 = 2 * r, 2 * r + 1
                stageA1(b2)
                stageA1(a2)
                if a2 >= 2:
                    bisect_dve(a2)
                    iters = bisect_act_steps(b2)
            chunks = []
            if r >= 1:
                chunks = stageB_steps(2 * r - 2) + stageB_steps(2 * r - 1)
            interleave(chunks, iters)
            if r < npair:
                stageA3(2 * r)
                stageA3(2 * r + 1)

    def mem_attn(self, layer, s):
        self.phase()
        ml = self.ring(2, [128, D], F32)
        mbf = self.ring(2, [128, D], BF16)
        memT = self.alloc([128, 8, MEM], BF16)
        for i in range(2):
            a, b = ml[i], mbf[i]
            self.dma(a.a, self.mem_in[s * MEM + i * 128:s * MEM + (i + 1) * 128, :], (), [a.b])
            self.copy("act", b.a, a.a, [a.b], [b.b])
            ps = self.psum()
            pv = ps.a.bitcast(BF16)
            for k in range(8):
                self.S.op("pe", lambda e, k=k, pv=pv, b=b: e.transpose(pv[:, k * 128:(k + 1) * 128], b.a[:, k * 128:(k + 1) * 128], self.identb.a[:]),
                          [b.b, self.identb.b], [ps.b])
            self.copy("dve", memT.a[:, :, i * 128:(i + 1) * 128], pv.rearrange("p (k t) -> p k t", k=8), [ps.b], [memT.b])
        w32 = self.alloc([128, 8 * 512], F32)
        wk = self.alloc([128, 8 * 512], BF16)
        wv = self.alloc([128, 8 * 512], BF16)
        self.dma(w32.a, self.m_wk[layer], (), [w32.b])
        self.copy("pool", wk.a, w32.a, [w32.b], [wk.b])
        self.dma(w32.a, self.m_wv[layer], [], [w32.b])
        self.copy("pool", wv.a, w32.a, [w32.b], [wv.b])
        kmT = self.alloc([128, 4, MEM], BF16)
        vm = self.alloc([128, 2, 512], BF16)
        for h in range(4):
            ps = self.psum()
            for k in range(8):
                self.mm(ps.a[:, 0:MEM], wk.a[:, k * 512 + h * 128:k * 512 + (h + 1) * 128], memT.a[:, k, :], k == 0, k == 7, [wk.b, memT.b], [ps.b])
            self.copy("act", kmT.a[:, h, :], ps.a[:, 0:MEM], [ps.b], [kmT.b])
        for mc in range(2):
            ps = self.psum()
            for k in range(8):
                self.mm(ps.a, memT.a[:, k, mc * 128:(mc + 1) * 128], wv.a[:, k * 512:(k + 1) * 512], k == 0, k == 7, [wv.b, memT.b], [ps.b])
            self.copy("act", vm.a[:, mc, :], ps.a, [ps.b], [vm.b])
        mq = self.ring(2, [128, 4, 512], BF16)
        gm = self.ring(2, [128, 4, 512], BF16)
        pT = self.ring(2, [128, 2, 512], BF16)
        rs = self.ring(2, [128, 512], F32)
        yo = self.ring(2, [128, 4, 512], BF16)
        scale = 128 ** -0.5
        for tc in range(8):
            t0 = tc * 512
            Q, G, YO = mq[tc % 2], gm[tc % 2], yo[tc % 2]
            self.dma(Q.a, self.MQ[s, :, :, t0:t0 + 512].rearrange("c p t -> p c t"), self.MQb[s], [Q.b])
            self.dma(G.a, self.GM[s, :, :, t0:t0 + 512].rearrange("c p t -> p c t"), self.GMb[s], [G.b])
            for h in range(4):
                P, R = pT[h % 2], rs[h % 2]
                for mc in range(2):
                    ps = self.psum()
                    self.mm(ps.a, kmT.a[:, h, mc * 128:(mc + 1) * 128], Q.a[:, h, :], True, True, [kmT.b, Q.b], [ps.b])
                    self.act(P.a[:, mc, :], ps.a, AF.Exp, [ps.b], [P.b], scale=scale)
                po, pS = self.psum(), self.psum()
                for mc in range(2):
                    self.mm(po.a, vm.a[:, mc, h * 128:(h + 1) * 128], P.a[:, mc, :], mc == 0, mc == 1, [vm.b, P.b], [po.b])
                for mc in range(2):
                    self.mm(pS.a, self.ones.a[:], P.a[:, mc, :], mc == 0, mc == 1, [self.ones.b, P.b], [pS.b])
                self.act(R.a, pS.a, AF.Ln, [pS.b], [R.b])
                self.act(R.a, R.a, AF.Exp, [R.b], [R.b], scale=-1.0)
                self.tt("dve", R.a, po.a, R.a, ALU.mult, [po.b, R.b], [R.b])
                self.tt("pool", YO.a[:, h, :], R.a, G.a[:, h, :], ALU.mult, [R.b, G.b], [YO.b])
            self.dma(self.CATM[s, :, :, t0:t0 + 512].rearrange("c p t -> p c t"), YO.a, [YO.b], self.CATMb[s])

    def inproj_odd(self, layer, s):
        j = layer // 2
        for half in range(2):
            self.phase()
            t0 = half * 2048
            xT = self.alloc([128, 8, 2048], BF16)
            self.load_xT(layer, s, half, xT)
            wl = self.ring(2, [128, 1024], F32)
            wb = self.ring(2, [128, 1024], BF16)
            stage = self.ring(3, [128, 2048], BF16)
            sl = lambda tc: slice(tc * 512, (tc + 1) * 512)
            specs = [(i, AF.Gelu_apprx_tanh, self.U, self.Ub, i) for i in range(8)]
            specs += [(8 + i, AF.Silu, self.GC, self.GCb, i) for i in range(8)]
            specs += [(16 + i, AF.Copy, self.MQ, self.MQb, i) for i in range(4)]
            specs += [(20 + i, AF.Silu, self.GM, self.GMb, i) for i in range(4)]
            for ci, (idx, fn, dst, dstb, di) in enumerate(specs):
                sg = stage[ci % 3]
                for tc, ps in enumerate(self.fm_chunk(self.o_wfm[j, idx], xT, wl, wb, ci)):
                    self.act(sg.a[:, sl(tc)], ps.a, fn, [ps.b], [sg.b])
                self.dma(dst[s, di, :, t0:t0 + 2048], sg.a, [sg.b], [dstb[s][di]])
            vg = self.alloc([128, 2048], F32)
            self.dma(vg.a[:, 0:1024], self.o_vln[j, 0:1, :].partition_broadcast(128), (), [vg.b])
            self.dma(vg.a[:, 1024:2048], self.o_vln[j, 1:2, :].partition_broadcast(128), (), [vg.b])
            v32 = self.ring(2, [128, 1024], F32)
            vo = self.ring(2, [128, 1024], BF16)
            stt_ = self.ring(2, [128, 16], F32)
            w32 = self.alloc([128, 8, 512], F32)
            wts = [self.alloc([128, 8, 512], BF16), self.alloc([128, 8, 512], BF16)]
            for hh in range(2):
                self.dma(w32.a, self.o_wtm[j].rearrange("p (k n) -> p k n", k=8)[:, :, hh * 512:(hh + 1) * 512], (), [w32.b])
                self.copy("pool", wts[hh].a, w32.a, [w32.b], [wts[hh].b])
            for i in range(16):
                ti = half * 16 + i
                Vt, VO, SS = v32[i % 2], vo[i % 2], stt_[i % 2]
                for hh in range(2):
                    ps = self.psum()
                    for k in range(8):
                        self.mm(ps.a, xT.a[:, k, i * 128:(i + 1) * 128], wts[hh].a[:, k, :], k == 0, k == 7, [xT.b, wts[hh].b], [ps.b])
                    self.act(Vt.a[:, hh * 512:(hh + 1) * 512], ps.a, AF.Gelu_apprx_tanh, [ps.b], [Vt.b])
                self.layernorm(Vt, SS, vg, 0, VO.a, VO.b)
                row = ti * 128
                self.dma(self.VLN[s, row:row + 128, :], VO.a, [VO.b], [self.VLNb[s][ti]])

    def layernorm(self, Z, SS, gb, goff, out, outb):
        for hh in range(2):
            self.S.op("dve", lambda e, hh=hh: e.bn_stats(out=SS.a[:, hh * 6:(hh + 1) * 6], in_=Z.a[:, hh * 512:(hh + 1) * 512]), [Z.b], [SS.b])
        self.S.op("dve", lambda e: e.bn_aggr(out=SS.a[:, 12:14], in_=SS.a[:, 0:12]), [SS.b], [SS.b])
        self.act(SS.a[:, 14:15], SS.a[:, 13:14], AF.Ln, [SS.b, self.epsc.b], [SS.b], bias=self.epsc.a[:, 0:1])
        self.act(SS.a[:, 14:15], SS.a[:, 14:15], AF.Exp, [SS.b], [SS.b], scale=-0.5)
        self.ts("dve", Z.a, Z.a, SS.a[:, 12:13], SS.a[:, 14:15], ALU.subtract, ALU.mult, [Z.b, SS.b], [Z.b])
        self.tt("dve", Z.a, Z.a, gb.a[:, goff:goff + 1024], ALU.mult, [Z.b, gb.b], [Z.b])
        self.tt("pool", out, Z.a, gb.a[:, goff + 1024:goff + 2048], ALU.add, [Z.b, gb.b], [outb])

    def sgu(self, layer, s):
        j = layer // 2
        self.phase()
        w32 = self.alloc([128, 8, 128], F32)
        wsb = self.alloc([128, 8, 128], BF16)
        self.dma(w32.a, self.o_wsT[j].rearrange("p (g t) -> p g t", g=8), (), [w32.b])
        for g in range(8):
            self.tt("dve", wsb.a[:, g, :], w32.a[:, g, :], self.tril.a[:], ALU.mult, [w32.b, self.tril.b], [wsb.b])
        bsb = self.alloc([128, 8, 128], F32)
        self.dma(bsb.a.rearrange("p g t -> p (g t)"), self.o_bs[j].partition_broadcast(128), (), [bsb.b])
        vt = self.ring(2, [128, 4, 1024], BF16)
        uu = self.ring(2, [128, 8, 512], BF16)
        gc = self.ring(2, [128, 8, 512], BF16)
        tm = self.ring(2, [128, 512], F32)
        yo = self.ring(2, [128, 8, 512], BF16)
        for tc in range(8):
            t0 = tc * 512
            Vt, Uu, Gc, YO = vt[tc % 2], uu[tc % 2], gc[tc % 2], yo[tc % 2]
            self.dma(Vt.a, self.VLN[s, t0:t0 + 512, :].rearrange("(c p) n -> p c n", p=128), self.VLNb[s][tc * 4:tc * 4 + 4], [Vt.b])
            self.dma(Uu.a, self.U[s, :, :, t0:t0 + 512].rearrange("c p t -> p c t"), self.Ub[s], [Uu.b])
            self.dma(Gc.a, self.GC[s, :, :, t0:t0 + 512].rearrange("c p t -> p c t"), self.GCb[s], [Gc.b])
            for g in range(8):
                ps = self.psum()
                for c in range(4):
                    self.mm(ps.a[:, c * 128:(c + 1) * 128], Vt.a[:, c, g * 128:(g + 1) * 128], wsb.a[:, g, :], True, True, [Vt.b, wsb.b], [ps.b])
                Tm = tm[g % 2]
                self.tt("dve", Tm.a.rearrange("p (c t) -> p c t", c=4), ps.a.rearrange("p (c t) -> p c t", c=4),
                        bsb.a[:, g:g + 1, :].to_broadcast([128, 4, 128]), ALU.add, [ps.b, bsb.b], [Tm.b])
                self.tt("dve", Tm.a, Tm.a, Uu.a[:, g, :], ALU.mult, [Tm.b, Uu.b], [Tm.b])
                self.tt("pool", YO.a[:, g, :], Tm.a, Gc.a[:, g, :], ALU.mult, [Tm.b, Gc.b], [YO.b])
            self.dma(self.CATA[s, :, :, t0:t0 + 512].rearrange("c p t -> p c t"), YO.a, [YO.b], self.CATAb[s])

    def outproj(self, layer, s):
        j = layer // 2
        even = layer % 2 == 0
        self.phase()
        last = layer == self.layers - 1
        xsrc = self.x_in if layer == 0 else self.X
        xdst = self.y_out if last else self.X
        gb = self.alloc([128, 2048], F32)
        self.dma(gb.a[:, 0:1024], self.ln_gb[layer, 0:1, :].partition_broadcast(128), (), [gb.b])
        self.dma(gb.a[:, 1024:2048], self.ln_gb[layer, 1:2, :].partition_broadcast(128), (), [gb.b])
        w32 = self.alloc([128, 4 * 1024], F32)
        wo = self.alloc([128, 12, 1024], BF16)
        if even:
            srcs = [(self.e_woA[j], 0, 128), (self.e_woM[j], 8, 128)]
            for (src, c0, np_) in srcs:
                self.dma(w32.a, src, (), [w32.b])
                self.copy("pool", wo.a[:, c0:c0 + 4, :], w32.a.rearrange("p (c n) -> p c n", c=4), [w32.b], [wo.b])
        woB = None
        if even:
            woB = self.alloc([64, 8, 1024], BF16)
            for hh in range(2):
                self.dma(w32.a[0:64, :], self.e_woB[j, :, hh * 4096:(hh + 1) * 4096], (), [w32.b])
                self.copy("pool", woB.a[:, hh * 4:(hh + 1) * 4, :], w32.a[0:64, :].rearrange("p (c n) -> p c n", c=4), [w32.b], [woB.b])
        else:
            for c0 in range(0, 12, 4):
                self.dma(w32.a, self.o_wo[j, :, c0 * 1024:(c0 + 4) * 1024], (), [w32.b])
                self.copy("pool", wo.a[:, c0:c0 + 4, :], w32.a.rearrange("p (c n) -> p c n", c=4), [w32.b], [wo.b])
        ca = self.ring(2, [128, 12, 512], BF16)
        cb = self.ring(2, [64, 8, 512], BF16)
        xt = self.ring(2, [128, D], F32)
        zz = self.ring(2, [128, D], F32)
        xo = self.ring(2, [128, D], F32)
        ss = self.ring(2, [128, 16], F32)
        for tc in range(8):
            t0 = tc * 512
            CA, CB = ca[tc % 2], cb[tc % 2]
            if even:
                self.dma(CA.a[:, 0:4, :], self.CATA[s, 0:4, :, t0:t0 + 512].rearrange("c p t -> p c t"), self.CATAb[s][0:4], [CA.b])
                self.dma(CB.a, self.CATB[s, :, :, t0:t0 + 512].rearrange("h d t -> d h t"), self.CATBb[s][tc * 4:tc * 4 + 4], [CB.b])
            else:
                self.dma(CA.a[:, 0:8, :], self.CATA[s, :, :, t0:t0 + 512].rearrange("c p t -> p c t"), self.CATAb[s], [CA.b])
            self.dma(CA.a[:, 8:12, :], self.CATM[s, :, :, t0:t0 + 512].rearrange("c p t -> p c t"), self.CATMb[s], [CA.b])
            for i in range(4):
                ti = tc * 4 + i
                row = s * SEQ + ti * 128
                Xt, Z, XO, SS = (r[ti % 2] for r in (xt, zz, xo, ss))
                self.dma(Xt.a, xsrc[row:row + 128, :], [self.Xb[s][ti]], [Xt.b])
                tsl = slice(i * 128, (i + 1) * 128)
                for nh in range(2):
                    ps = self.psum()
                    ns = slice(nh * 512, (nh + 1) * 512)
                    ops = []
                    for c in ([0, 1, 2, 3, 8, 9, 10, 11] if even else range(12)):
                        ops.append((CA.a[:, c, tsl], wo.a[:, c, ns], [CA.b, wo.b]))
                    if even:
                        for h in range(8):
                            ops.append((CB.a[:, h, tsl], woB.a[:, h, ns], [CB.b, woB.b]))
                    for oi, (l, r, rd) in enumerate(ops):
                        self.mm(ps.a, l, r, oi == 0, oi == len(ops) - 1, rd, [ps.b])
                    self.stt(Z.a[:, ns], Xt.a[:, ns], float(DN_ALPHA), ps.a, ALU.mult, ALU.add, [Xt.b, ps.b], [Z.b])
                self.layernorm(Z, SS, gb, 0, XO.a, XO.b)
                self.dma(xdst[row:row + 128, :], XO.a, [XO.b], [self.Xb[s][ti]])

    def build(self, only=None):
        on = lambda n: only is None or n in only
        for s in range(self.nseq):
            if on("rope"):
                self.rope_tables(s)
        for layer in range(self.layers):
            for s in range(self.nseq):
                if layer % 2 == 0:
                    if on("inproj"):
                        self.inproj_even(layer, s)
                    if on("conv"):
                        self.conv_branch(layer, s)
                    if on("dsa"):
                        self.dsa(layer, s)
                else:
                    if on("inproj"):
                        self.inproj_odd(layer, s)
                    if on("sgu"):
                        self.sgu(layer, s)
                if on("mem"):
                    self.mem_attn(layer, s)
                if on("out"):
                    self.outproj(layer, s)
        self.S.barrier()
        self.S.emit()
        return self.nc


def _fm(W, cols):
    w = W[:, cols]
    return np.ascontiguousarray(w.reshape(8, 128, 128).transpose(1, 0, 2).reshape(128, 8 * 128))


def _kmajor(W):
    K, N = W.shape
    return np.ascontiguousarray(W.reshape(K // 128, 128, N).transpose(1, 0, 2).reshape(128, (K // 128) * N))


def host_consts():
    inv = (10000.0 ** (-np.arange(0, 64, 2, dtype=np.float32) / np.float32(64))).astype(np.float32)
    invf = np.zeros((128, 2), np.float32)
    for p in range(128):
        invf[p, 0] = inv[p % 32]
        invf[p, 1] = -1.0 if (p % 64) < 32 else 1.0
    t = np.arange(128)
    negtri = np.where(t[None, :] <= t[:, None], 0.0, -1e30).astype(np.float32)
    tril = (t[None, :] >= t[:, None]).astype(np.float32)
    pow2 = np.tile((2.0 ** -np.arange(NIT + 2, dtype=np.float32))[None, :], (128, 1)).astype(np.float32)
    return {"c_ident": np.eye(128, dtype=np.float32), "c_invf": invf, "c_negtri": negtri, "c_tril": tril, "c_pow2": pow2}


def host_weights(e_w_in, e_conv_w, e_conv_b, e_cln_g, e_cln_b, e_pw2_w, e_pw2_b, e_w_out,
                 o_w_in, o_vln_g, o_vln_b, o_ws, o_bs, o_w_out, mem_wk, mem_wv, ln_g, ln_b):
    f = lambda a: np.asarray(a, dtype=np.float32)
    nE, nO = e_w_in.shape[0], o_w_in.shape[0]
    ar = np.arange
    sw = (ar(64) + 32) % 64
    out = {}
    wfm = np.zeros((nE, NCH_E, 128, 1024), np.float32)
    wtm = np.zeros((nE, 128, 8 * 68), np.float32)
    for j in range(nE):
        W = f(e_w_in[j])
        ch = []
        ch += [0 + 128 * i + ar(128) for i in range(4)]
        ch += [512 + 128 * i + ar(128) for i in range(4)]
        ch += [1024 + 128 * i + ar(128) for i in range(4)]
        ch += [1536 + 128 * i + ar(128) for i in range(4)]
        ch += [1536 + 128 * i + np.concatenate([sw, 64 + sw]) for i in range(4)]
        ch += [np.concatenate([2048 + ar(64), 2432 + ar(64)])]
        ch += [np.concatenate([2048 + sw, 2432 + sw])]
        ch += [2176 + 128 * i + ar(128) for i in range(2)]
        ch += [2176 + 128 * i + np.concatenate([sw, 64 + sw]) for i in range(2)]
        ch += [3012 + 128 * i + ar(128) for i in range(4)]
        ch += [3524 + 128 * i + ar(128) for i in range(4)]
        ch += [2500 + 128 * i + ar(128) for i in range(4)]
        assert len(ch) == NCH_E
        for c, cols in enumerate(ch):
            wfm[j, c] = _fm(W, cols)
        wtm[j] = _kmajor(W[:, np.concatenate([2112 + ar(64), 2496 + ar(4)])])
    out["e_wfm"], out["e_wtm"] = wfm, wtm
    cw = f(e_conv_w)
    out["e_convw"] = np.ascontiguousarray(cw.reshape(nE, 31, 4, 128).transpose(0, 3, 2, 1).reshape(nE, 128, 4 * 31))
    pc = lambda v: f(v).reshape(nE, 4, 128).transpose(0, 2, 1)
    out["e_vec"] = np.ascontiguousarray(np.concatenate([pc(e_conv_b), pc(e_cln_g), pc(e_cln_b), pc(e_pw2_b)], axis=2))
    out["e_pw2"] = np.stack([_kmajor(f(e_pw2_w[j])) for j in range(nE)])
    wo = f(e_w_out)
    out["e_woA"] = np.stack([_kmajor(wo[j, 0:512]) for j in range(nE)])
    out["e_woB"] = np.stack([np.ascontiguousarray(wo[j, 512:1024].reshape(8, 64, 1024).transpose(1, 0, 2).reshape(64, 8 * 1024)) for j in range(nE)])
    out["e_woM"] = np.stack([_kmajor(wo[j, 1024:1536]) for j in range(nE)])
    ofm = np.zeros((nO, NCH_O, 128, 1024), np.float32)
    otm = np.zeros((nO, 128, 8 * 1024), np.float32)
    for j in range(nO):
        W = f(o_w_in[j])
        ch = [0 + 128 * i + ar(128) for i in range(8)] + [2048 + 128 * i + ar(128) for i in range(8)]
        ch += [3072 + 128 * i + ar(128) for i in range(4)] + [3584 + 128 * i + ar(128) for i in range(4)]
        for c, cols in enumerate(ch):
            ofm[j, c] = _fm(W, cols)
        otm[j] = _kmajor(W[:, 1024:2048])
    out["o_wfm"], out["o_wtm"] = ofm, otm
    out["o_vln"] = np.ascontiguousarray(np.stack([f(o_vln_g), f(o_vln_b)], axis=1))
    out["o_wsT"] = np.ascontiguousarray(f(o_ws).transpose(0, 3, 1, 2).reshape(nO, 128, 8 * 128))
    out["o_bs"] = np.ascontiguousarray(f(o_bs).reshape(nO, 1, 8 * 128))
    out["o_wo"] = np.stack([_kmajor(f(o_w_out[j])) for j in range(nO)])
    out["m_wk"] = np.stack([_kmajor(f(mem_wk[l])) for l in range(DEPTH)])
    out["m_wv"] = np.stack([_kmajor(f(mem_wv[l])) for l in range(DEPTH)])
    out["ln_gb"] = np.ascontiguousarray(np.stack([f(ln_g), f(ln_b)], axis=1))
    return out


_PROG_CACHE = {}


def kernel(x, mem, positions, e_w_in, e_conv_w, e_conv_b, e_cln_g, e_cln_b, e_pw2_w, e_pw2_b, e_w_out,
           o_w_in, o_vln_g, o_vln_b, o_ws, o_bs, o_w_out, mem_wk, mem_wv, ln_g, ln_b):
    x = np.asarray(x, dtype=np.float32)
    mem = np.asarray(mem, dtype=np.float32)
    positions = np.asarray(positions, dtype=np.int32)
    B = x.shape[0]
    nseq = B // NCORES
    shared = host_consts()
    shared.update(host_weights(e_w_in, e_conv_w, e_conv_b, e_cln_g, e_cln_b, e_pw2_w, e_pw2_b, e_w_out,
                               o_w_in, o_vln_g, o_vln_b, o_ws, o_bs, o_w_out, mem_wk, mem_wv, ln_g, ln_b))
    if "p" not in _PROG_CACHE:
        _PROG_CACHE["p"] = Prog(nseq, DEPTH).build()
    nc = _PROG_CACHE["p"]
    in_maps = []
    for c in range(NCORES):
        d = dict(shared)
        d["x"] = np.ascontiguousarray(x[c * nseq:(c + 1) * nseq].reshape(nseq * SEQ, D))
        d["mem"] = np.ascontiguousarray(mem[c * nseq:(c + 1) * nseq].reshape(nseq * MEM, D))
        d["pos"] = np.ascontiguousarray(positions[c * nseq:(c + 1) * nseq])
        in_maps.append(d)
    res = run_bass_kernel_spmd(nc, in_maps, core_ids=list(range(NCORES)))
    out = np.concatenate([r["y"].reshape(nseq, SEQ, D) for r in res.results], axis=0)
    return out.astype(np.float32)
```
